# Optimizing a Trainium2 kernel written in Bass

```python
import jax, jax.numpy as jnp
from jax import lax
import numpy as np


D_MODEL = 1024
BATCH = 2
SEQ = 8192
DEPTH = 2

GRID_W = 64
CTX_LEN = 256
N_MIXERS = 2
N_A_LAYERS = (DEPTH + N_MIXERS - 1) // N_MIXERS
N_B_LAYERS = DEPTH // N_MIXERS
POOL_WINDOWS = (2, 4, 8, 16)
POOL_GROUPS = 4
POOL_GROUP_DIM = D_MODEL // POOL_GROUPS
NA_HEADS = 16
NA_HEAD_DIM = D_MODEL // NA_HEADS
WIN_H = 8
WIN_W = 16
COL_BAND = 2 * WIN_W
D_FF = 2816
CONV_W = 3
ADA_CHUNKS = 6
EPS = 1e-6
NEG_INF = -1e30

kernel_name = 'hybrid_pool_natten_dit_block'


def rms_norm(x, g):
    x32 = x.astype(jnp.float32)
    y = x32 * lax.rsqrt(jnp.mean(x32 * x32, axis=-1, keepdims=True) + EPS)
    return (y * g.astype(jnp.float32)).astype(x.dtype)


def modulate(h, shift, scale):
    return h * (1 + scale) + shift


def ada_mods(cond, w, b):
    mods = (jax.nn.silu(cond) @ w + b)[..., None, :]
    return jnp.split(mods, ADA_CHUNKS, axis=-1)


def centred_window_mean(u, w):
    n = u.shape[1]
    u32 = u.astype(jnp.float32)
    cs = jnp.concatenate([jnp.zeros_like(u32[:, :1]), jnp.cumsum(u32, axis=1)], axis=1)
    t = jnp.arange(n)
    lo = jnp.maximum(t - w // 2, 0)
    hi = jnp.minimum(t + w // 2, n)
    cnt = (hi - lo).astype(jnp.float32)
    return ((cs[:, hi] - cs[:, lo]) / cnt[None, :, None]).astype(u.dtype)


def pool_mixer(u, w_groups, scale):
    outs = []
    for g, win in enumerate(POOL_WINDOWS):
        ug = u[..., g * POOL_GROUP_DIM:(g + 1) * POOL_GROUP_DIM]
        outs.append((centred_window_mean(ug, win) - ug) @ w_groups[g])
    return jnp.concatenate(outs, axis=-1) * scale


def depthwise_conv(h, w, b):
    n = h.shape[1]
    p = CONV_W // 2
    hp = jnp.pad(h, ((0, 0), (p, p), (0, 0)))
    out = b
    for j in range(CONV_W):
        out = out + hp[:, j:j + n] * w[j]
    return out


def conv_ffn(u, w_up, conv_w, conv_b, w_down):
    h = depthwise_conv(u @ w_up, conv_w, conv_b)
    gate, val = jnp.split(h, 2, axis=-1)
    return (jax.nn.gelu(gate) * val) @ w_down


def column_band_tables():
    n_cb = GRID_W // WIN_W
    qcol = np.arange(GRID_W).reshape(n_cb, WIN_W)
    band_start = np.clip(np.arange(n_cb) * WIN_W - WIN_W // 2, 0, GRID_W - COL_BAND)
    band = band_start[:, None] + np.arange(COL_BAND)[None]
    win_start = np.clip(qcol - WIN_W // 2, 0, GRID_W - WIN_W)
    kc = band[:, None, :]
    valid = (kc >= win_start[..., None]) & (kc < win_start[..., None] + WIN_W)
    dc = np.clip(kc - qcol[..., None] + WIN_W - 1, 0, 2 * WIN_W - 2)
    return band, valid, dc


def split_heads(t, n_parts):
    bsz, n, _ = t.shape
    t = t.reshape(bsz, n, n_parts, NA_HEADS, NA_HEAD_DIM)
    return [jnp.transpose(t[:, :, j], (0, 2, 1, 3)) for j in range(n_parts)]


def context_self_attention(uc, w_qkv, w_o):
    bsz, m, _ = uc.shape
    q_c, k_c, v_c = split_heads(uc @ w_qkv, 3)
    s = jnp.einsum('bhqd,bhkd->bhqk', q_c, k_c).astype(jnp.float32) * (NA_HEAD_DIM ** -0.5)
    p = jax.nn.softmax(s, axis=-1).astype(v_c.dtype)
    o = jnp.einsum('bhqk,bhkd->bhqd', p, v_c)
    return jnp.transpose(o, (0, 2, 1, 3)).reshape(bsz, m, D_MODEL) @ w_o


def neighbourhood_attention(u, uc, w_qkv, w_o, rpb):
    bsz, n, _ = u.shape
    rows = n // GRID_W
    kh = min(WIN_H, rows)
    n_cb = GRID_W // WIN_W
    nk = kh * COL_BAND
    scale = NA_HEAD_DIM ** -0.5
    band, valid, dc = column_band_tables()
    band_idx = jnp.asarray(band)
    col_valid = jnp.asarray(valid)
    dc_idx = jnp.asarray(dc)

    q, k, v = [t.reshape(bsz, NA_HEADS, rows, GRID_W, NA_HEAD_DIM) for t in split_heads(u @ w_qkv, 3)]
    k_c, v_c = split_heads(uc @ w_qkv[:, D_MODEL:], 2)

    def row_block(r):
        s0 = jnp.clip(r - kh // 2, 0, rows - kh)
        q_r = lax.dynamic_index_in_dim(q, r, axis=2, keepdims=False)
        q_r = q_r.reshape(bsz, NA_HEADS, n_cb, WIN_W, NA_HEAD_DIM)

        def gather_band(t):
            t_r = lax.dynamic_slice_in_dim(t, s0, kh, axis=2)
            t_b = t_r[:, :, :, band_idx]
            return jnp.moveaxis(t_b, 3, 2).reshape(bsz, NA_HEADS, n_cb, nk, NA_HEAD_DIM)

        k_b, v_b = gather_band(k), gather_band(v)
        dr_idx = s0 + jnp.arange(kh) - r + WIN_H - 1
        bias = rpb[:, dr_idx[:, None, None, None], dc_idx[None]]
        bias = jnp.where(col_valid[None, None], bias.astype(jnp.float32), NEG_INF)
        bias = jnp.moveaxis(bias, 1, 3).reshape(NA_HEADS, n_cb, WIN_W, nk)
        s_lat = jnp.einsum('bhnqd,bhnkd->bhnqk', q_r, k_b).astype(jnp.float32) * scale + bias
        s_ctx = jnp.einsum('bhnqd,bhcd->bhnqc', q_r, k_c).astype(jnp.float32) * scale
        p = jax.nn.softmax(jnp.concatenate([s_lat, s_ctx], axis=-1), axis=-1).astype(v.dtype)
        o = (jnp.einsum('bhnqk,bhnkd->bhnqd', p[..., :nk], v_b)
             + jnp.einsum('bhnqc,bhcd->bhnqd', p[..., nk:], v_c))
        return o.reshape(bsz, NA_HEADS, GRID_W, NA_HEAD_DIM)

    o = lax.map(row_block, jnp.arange(rows))
    return jnp.transpose(o, (1, 0, 3, 2, 4)).reshape(bsz, n, D_MODEL) @ w_o


def setup_inputs(seed: int = 0) -> dict:
    key = jax.random.key(seed)
    ks = jax.random.split(key, 19)
    D = D_MODEL
    G = POOL_GROUP_DIM

    def nrm(k, shape, s):
        return jax.random.normal(k, shape, jnp.float32) * s

    return {
        'x': nrm(ks[0], (BATCH, SEQ, D), 1.0),
        'c': nrm(ks[1], (BATCH, D), 1.0),
        'ctx': nrm(ks[2], (BATCH, CTX_LEN, D), 1.0),
        'c_ctx': nrm(ks[3], (D,), 1.0),
        'ada_w': nrm(ks[4], (DEPTH, D, ADA_CHUNKS * D), 0.5 * D ** -0.5),
        'ada_b': nrm(ks[5], (DEPTH, ADA_CHUNKS * D), 0.02),
        'mix_pre_g': 1.0 + nrm(ks[6], (DEPTH, D), 0.05),
        'mix_post_g': 1.0 + nrm(ks[7], (DEPTH, D), 0.05),
        'ffn_pre_g': 1.0 + nrm(ks[8], (DEPTH, D), 0.05),
        'ffn_post_g': 1.0 + nrm(ks[9], (DEPTH, D), 0.05),
        'pool_w': nrm(ks[10], (N_A_LAYERS, POOL_GROUPS, G, G), G ** -0.5),
        'pool_scale': 1.0 + nrm(ks[11], (N_A_LAYERS, D), 0.1),
        'na_w_qkv': nrm(ks[12], (N_B_LAYERS, D, 3 * D), D ** -0.5),
        'na_w_o': nrm(ks[13], (N_B_LAYERS, D, D), D ** -0.5),
        'na_rpb': nrm(ks[14], (N_B_LAYERS, NA_HEADS, 2 * WIN_H - 1, 2 * WIN_W - 1), 0.5),
        'ffn_w_up': nrm(ks[15], (DEPTH, D, 2 * D_FF), D ** -0.5),
        'ffn_conv_w': nrm(ks[16], (DEPTH, CONV_W, 2 * D_FF), CONV_W ** -0.5),
        'ffn_conv_b': nrm(ks[17], (DEPTH, 2 * D_FF), 0.02),
        'ffn_w_down': nrm(ks[18], (DEPTH, D_FF, D), D_FF ** -0.5),
    }


def reference(x, c, ctx, c_ctx, ada_w, ada_b, mix_pre_g, mix_post_g, ffn_pre_g, ffn_post_g,
              pool_w, pool_scale, na_w_qkv, na_w_o, na_rpb,
              ffn_w_up, ffn_conv_w, ffn_conv_b, ffn_w_down):
    for i in range(DEPTH):
        last = i == DEPTH - 1
        j = i // N_MIXERS
        sh1, sc1, g1, sh2, sc2, g2 = ada_mods(c, ada_w[i], ada_b[i])
        csh1, csc1, cg1, csh2, csc2, cg2 = ada_mods(c_ctx, ada_w[i], ada_b[i])

        u = modulate(rms_norm(x, mix_pre_g[i]), sh1, sc1)
        if i % N_MIXERS == 0:
            y = pool_mixer(u, pool_w[j], pool_scale[j])
            if not last:
                uc = modulate(rms_norm(ctx, mix_pre_g[i]), csh1, csc1)
                yc = pool_mixer(uc, pool_w[j], pool_scale[j])
        else:
            uc = modulate(rms_norm(ctx, mix_pre_g[i]), csh1, csc1)
            y = neighbourhood_attention(u, uc, na_w_qkv[j], na_w_o[j], na_rpb[j])
            if not last:
                yc = context_self_attention(uc, na_w_qkv[j], na_w_o[j])
        x = x + g1 * rms_norm(y, mix_post_g[i])
        if not last:
            ctx = ctx + cg1 * rms_norm(yc, mix_post_g[i])

        f = conv_ffn(modulate(rms_norm(x, ffn_pre_g[i]), sh2, sc2),
                     ffn_w_up[i], ffn_conv_w[i], ffn_conv_b[i], ffn_w_down[i])
        x = x + g2 * rms_norm(f, ffn_post_g[i])
        if not last:
            fc = conv_ffn(modulate(rms_norm(ctx, ffn_pre_g[i]), csh2, csc2),
                          ffn_w_up[i], ffn_conv_w[i], ffn_conv_b[i], ffn_w_down[i])
            ctx = ctx + cg2 * rms_norm(fc, ffn_post_g[i])
    return x
```

```python
import numpy as np
from contextlib import ExitStack
import concourse.bass as bass
import concourse.mybir as mybir
from concourse.bass_utils import run_bass_kernel_spmd

F32 = mybir.dt.float32
BF16 = mybir.dt.bfloat16
AF = mybir.ActivationFunctionType
ALU = mybir.AluOpType

D = 1024
NCH = 8
GRID_W = 64
PAD = 16
ROWS_EXT = 42
NT = PAD + ROWS_EXT * 64 + PAD
TOP = PAD + 5 * 64
BOT = TOP + 2048
CTX = 256
NTC = PAD + CTX + PAD
TOPC = PAD
BOTC = PAD + CTX
DFF = 2816
NJ = 22
EPS = 1e-6
MASK = -30000.0
NV = 8 * 5 + 48 + 3 * 44 + 44
V_PRE, V_POST, V_FPRE, V_FPOST, V_PSC, V_ADAB, V_CW, V_CB = 0, 8, 16, 24, 32, 40, 88, 220

AGROUPS = [
    (4, 1, [0, 1, 2, 3], 0),
    (5, 8, list(range(0, 8)), 1),
    (13, 8, list(range(4, 12)), 2),
    (21, 8, list(range(8, 16)), 2),
    (29, 8, list(range(12, 20)), 3),
    (37, 1, [16, 17, 18, 19, 20], 4),
]


def _row_window(gq):
    s0 = min(max(gq - 4, 0), 120)
    return s0, s0 + 8


def _group_colranges():
    out = {}
    for (q0, nr, chunks, typ) in AGROUPS:
        if typ in out:
            continue
        rngs = []
        for c in chunks:
            rows = set()
            for r0 in (0, 32, 96):
                for qr in range(nr):
                    gq = r0 - 5 + q0 + qr
                    if gq < 0 or gq > 127:
                        continue
                    lo, hi = _row_window(gq)
                    for kr2 in range(2):
                        gk = r0 - 5 + 2 * c + kr2
                        if lo <= gk < hi:
                            rows.add(qr)
            if not rows:
                rows = {0}
            rngs.append((min(rows), max(rows) + 1))
        out[typ] = rngs
    return out


COLR = _group_colranges()
TABW = {t: sum((hi - lo) * 64 for lo, hi in COLR[t]) for t in COLR}
TABMAX = max(TABW.values())
TABSMALL = max(TABW[0], TABW[4])


class Res:
    __slots__ = ("name", "w", "r", "cnt", "sem", "nobar")

    def __init__(self, name):
        self.name = name
        self.w = {}
        self.r = {}
        self.cnt = 0
        self.sem = None
        self.nobar = False


class Sched:
    ENG = ("pe", "act", "dve", "pool", "sp")

    def __init__(self):
        self.streams = {e: [] for e in self.ENG}
        self.cnt = {e: 0 for e in self.ENG}
        self.known = {e: {} for e in self.ENG}
        self.slots = []
        self.sw = {}

    def _need(self, eng, toks, waits, raw):
        for key, val in toks.items():
            if isinstance(key, str):
                if key == eng:
                    if eng == "pe" or not raw:
                        continue
            else:
                val = key.cnt
            if self.known[eng].get(key, 0) >= val:
                continue
            if waits.get(key, 0) < val:
                waits[key] = val

    def _deps(self, eng, reads, writes):
        waits = {}
        for r in reads:
            self._need(eng, r.w, waits, True)
        for w in writes:
            self._need(eng, w.w, waits, False)
            self._need(eng, w.r, waits, False)
        for k, v in waits.items():
            self.known[eng][k] = v
        return list(waits.items())

    def _commit(self, key, val, reads, writes):
        for r in reads:
            if r.r.get(key, 0) < val:
                r.r[key] = val
        for w in writes:
            w.w = {key: val}
            w.r = {}

    def op(self, eng, fn, reads=(), writes=(), sig=True):
        waits = self._deps(eng, reads, writes)
        if sig:
            self.cnt[eng] += 1
            val = self.cnt[eng]
        else:
            val = self.cnt[eng] + 1
        self.streams[eng].append((waits, fn, ("eng", eng) if sig else None))
        self._commit(eng, val, reads, writes)

    def dma(self, q, fn, reads, writes, slot):
        if q == "pool":
            key = id(slot)
            if key not in self.sw:
                r = Res(slot.name + "_sw")
                r.nobar = slot.nobar
                self.sw[key] = r
            slot = self.sw[key]
        waits = self._deps(q, reads, writes)
        if slot.cnt == 0:
            self.slots.append(slot)
        slot.cnt += 16
        self.streams[q].append((waits, fn, ("dma", slot)))
        self._commit(slot, slot.cnt, reads, writes)

    def barrier(self, final=False):
        for e in self.ENG:
            waits = {}
            for o in self.ENG[:4]:
                if o != e and self.known[e].get(o, 0) < self.cnt[o]:
                    waits[o] = self.cnt[o]
            for s in self.slots:
                if s.nobar and not final:
                    continue
                if self.known[e].get(s, 0) < s.cnt:
                    waits[s] = s.cnt
            for k, v in waits.items():
                self.known[e][k] = v
            if waits:
                self.streams[e].append((list(waits.items()), None, None))

    def emit(self, nc, es):
        sems = {}
        for e in self.ENG[:4]:
            sems[e] = es.enter_context(nc.semaphore("sem_" + e))
        for s in self.slots:
            s.sem = es.enter_context(nc.semaphore("sd_" + s.name))
        block = es.enter_context(nc.Block())

        def run(engname):
            def body(eng):
                for waits, fn, inc in self.streams[engname]:
                    for key, val in waits:
                        eng.wait_ge(sems[key] if isinstance(key, str) else key.sem, val)
                    if fn is None:
                        continue
                    inst = fn(eng)
                    if inc is not None:
                        if inc[0] == "eng":
                            inst.then_inc(sems[inc[1]], 1)
                        else:
                            inst.then_inc(inc[1].sem, 16)
            return body

        block.tensor(run("pe"))
        block.scalar(run("act"))
        block.vector(run("dve"))
        block.gpsimd(run("pool"))
        block.sync(run("sp"))


class Arena:
    def __init__(self, nc, base, top):
        self.nc = nc
        self.off = base
        self.top = top
        self.n = 0

    def alloc(self, name, shape, dtype):
        per = 1
        for s in shape[1:]:
            per *= s
        nbytes = per * (2 if dtype == BF16 else 4)
        off = (self.off + 63) // 64 * 64
        assert off + nbytes <= self.top, f"SBUF overflow at {name}: {off + nbytes} > {self.top}"
        self.off = off + nbytes
        self.hw = max(getattr(self, 'hw', 0), self.off)
        self.hwlog = getattr(self, 'hwlog', {})
        self.hwlog[name] = self.off
        self.n += 1
        return self.nc.alloc_sbuf_tensor_at(f"{name}_{self.n}", list(shape), dtype, offset=off)


def build_program(stop_after=99, dbg=False):
    nc = bass.Bass("TRN2", target_bir_lowering=False)
    S = Sched()

    def din(name, shape, dt=F32):
        return nc.dram_tensor(name, list(shape), dt, kind="ExternalInput")

    xin = din("xin", [128, NCH, NT])
    cin = din("cin", [128, NCH, NTC])
    cond = din("cond", [128, NCH, 2])
    vecs = din("vecs", [128, 2, NV])
    flg = din("flg", [128, 4])
    corr = din("corr", [128, 2, 4, 16])
    adaw = din("adaw", [2, 24, 128, 8, 256])
    wpool = din("wpool", [128, 4, 2, 256])
    wup = din("wup", [2, NJ, 128, 2, 8, 128])
    wdn = din("wdn", [2, 8, 128, NJ, 128])
    wq = din("wq", [128, 8, 8, 128])
    wk = din("wk", [128, 8, 8, 128])
    wv = din("wv", [128, 8, 1024])
    wo = din("wo", [128, 8, 8, 128])
    tabL = din("tabL", [3, 16, 128, TABMAX])
    tabS = din("tabS", [2, 16, 128, TABSMALL])
    outT = nc.dram_tensor("outT", [128, NCH, 2048], F32, kind="ExternalOutput")

    xsA = nc.dram_tensor("xsA", [128, NCH, NT], F32)
    xsB = nc.dram_tensor("xsB", [128, NCH, NT], F32)
    csA = nc.dram_tensor("csA", [128, NCH, NTC], F32)
    csB = nc.dram_tensor("csB", [128, NCH, NTC], F32)
    wupb = nc.dram_tensor("wupb", [2, NJ, 128, 2, 8, 128], BF16)
    wdnb = nc.dram_tensor("wdnb", [2, 8, 128, NJ, 128], BF16)
    kts = nc.dram_tensor("kts", [128, NCH, 21 * 128], BF16)
    vts = nc.dram_tensor("vts", [128, 21, 1024], BF16)
    if dbg:
        dbgx = nc.dram_tensor("dbgx", [128, NCH, NT], F32, kind="ExternalOutput")
        dbgc = nc.dram_tensor("dbgc", [128, NCH, NTC], F32, kind="ExternalOutput")

    R_xin, R_cin, R_xsA, R_xsB, R_csA, R_csB = (Res(n) for n in ("xin", "cin", "xsA", "xsB", "csA", "csB"))
    R_wupb, R_wdnb, R_kts, R_vts, R_out, R_const = (Res(n) for n in ("wupb", "wdnb", "kts", "vts", "out", "const"))

    ar = Arena(nc, 16512, 229344)
    RESC = {}
    es = ExitStack()

    def sb(name, shape, dt=F32, nres=1):
        t = ar.alloc(name, shape, dt)
        rs = [RESC.setdefault(f"{name}_{i}", Res(f"{name}_{i}")) for i in range(nres)]
        return t, (rs[0] if nres == 1 else rs)

    PS = []
    for i in range(8):
        PS.append((es.enter_context(nc.psum_tensor(f"ps{i}", [128, 512], F32)), Res(f"ps{i}")))

    VEC, R_VEC = sb("VEC", [128, 2, NV])
    SC, R_SC = sb("SC", [128, NCH, 2])
    SG0, R_SG0 = sb("SG0", [128, NCH, 2])
    MODS, R_MODS = sb("MODS", [128, 2, 48, 2])
    DER, R_DER = sb("DER", [128, 2, 2, 6, 8])
    FLG, R_FLG = sb("FLG", [128, 4])
    CORR, R_CORR = sb("CORR", [128, 2, 4, 16])
    ONES, R_ONES = sb("ONES", [128, 128], BF16)
    WP, R_WP = sb("WP", [128, 4, 2, 256], BF16)
    XT, R_XT = sb("XT", [128, NCH, 512])
    YS, R_YS = sb("YS", [128, NCH, 512], F32, nres=NCH)
    DB, R_DB = sb("DB", [128, NCH, 512], BF16, nres=NCH)
    RS = [sb(f"RS{i}", [128, 512]) for i in range(2)]
    SQ = [sb(f"SQ{i}", [128, 512], BF16) for i in range(3)]
    TMP = [sb(f"TMP{i}", [128, 512]) for i in range(3)]
    base_mark = ar.off

    rot = {}

    def nxt(lst, key):
        i = rot.get(key, 0)
        rot[key] = i + 1
        return lst[i % len(lst)]

    def der(layer, cnd, which):
        return lambda c: DER[:, layer, cnd, which, c:c + 1]

    S.dma("sp", lambda e: e.dma_start(out=VEC[:], in_=vecs[:]), [R_const], [R_VEC], R_VEC)
    S.dma("sp", lambda e: e.dma_start(out=SC[:], in_=cond[:]), [R_const], [R_SC], R_SC)
    S.dma("sp", lambda e: e.dma_start(out=FLG[:], in_=flg[:]), [R_const], [R_FLG], R_FLG)
    S.dma("sp", lambda e: e.dma_start(out=CORR[:], in_=corr[:]), [R_const], [R_CORR], R_CORR)
    S.dma("pool", lambda e: e.dma_start(out=WP[:], in_=wpool[:], max_dma_last_dim=4096), [R_const], [R_WP], R_WP)
    S.op("dve", lambda e: e.memset(ONES[:], 1.0), [], [R_ONES])
    LANES = [Res(f"lane{i}") for i in range(8)]
    for ln in LANES:
        ln.nobar = True
    R_wupc = [[Res(f"wupc{i}_{j}") for j in range(NJ)] for i in range(2)]
    R_wdnc = [[Res(f"wdnc{i}_{m}") for m in range(8)] for i in range(2)]
    kk = 0
    for i in range(2):
        for j in range(NJ):
            ln = LANES[kk % 8]
            S.dma("pool", lambda e, i=i, j=j: e.dma_start(out=wupb[i, j], in_=wup[i, j], max_dma_last_dim=4096), [R_const], [R_wupc[i][j], ln], ln)
            kk += 1
        for m in range(8):
            ln = LANES[kk % 8]
            S.dma("pool", lambda e, i=i, m=m: e.dma_start(out=wdnb[i, m], in_=wdn[i, m], max_dma_last_dim=4096), [R_const], [R_wdnc[i][m], ln], ln)
            kk += 1
    S.op("act", lambda e: e.activation(out=SG0[:], in_=SC[:], func=AF.Sigmoid), [R_SC], [R_SG0])
    S.op("dve", lambda e: e.tensor_tensor(out=SC[:], in0=SC[:], in1=SG0[:], op=ALU.mult), [R_SC, R_SG0], [R_SC])

    EPSB, R_EPSB = sb("EPSB", [128, 1])
    S.op("dve", lambda e: e.memset(EPSB[:], EPS), [], [R_EPSB])
    m1 = ar.off
    WA = [sb(f"WA{i}", [128, 8, 256]) for i in range(4)]
    PMB = {0: 0, 1: 3}
    wa_of = {}

    def mods_dma(layer, g):
        wa, R_wa = nxt(WA, "wa")
        wa_of[(layer, g)] = (wa, R_wa)
        S.dma("act" if layer == 1 else "sp", lambda e: e.dma_start(out=wa[:], in_=adaw[layer, g]), [R_const], [R_wa], R_wa)

    def mods_mm(layer, g):
        pm, R_pm = PS[PMB[layer]]
        wa, R_wa = wa_of[(layer, g)]
        for mm in range(2):
            m = g * 2 + mm
            for k in range(8):
                S.op("pe", lambda e, m=m, mm=mm, k=k: e.matmul(
                    pm[:, 2 * m:2 * m + 2], lhsT=wa[:, k, mm * 128:(mm + 1) * 128], rhs=SC[:, k, :],
                    start=(k == 0), stop=(k == 7)), [R_wa, R_SC], [R_pm], sig=True)

    def mods_finish(layer):
        pm, R_pm = PS[PMB[layer]]
        for cnd in range(2):
            S.op("dve", lambda e, pm=pm, layer=layer, cnd=cnd: e.tensor_tensor(
                out=MODS[:, layer, :, cnd], in0=pm[:, cnd:96:2], in1=VEC[:, layer, V_ADAB:V_ADAB + 48], op=ALU.add),
                [R_pm, R_VEC], [R_MODS])
            def mod(k0, layer=layer, cnd=cnd):
                return MODS[:, layer, k0:k0 + 8, cnd]
            for which, (sc0, gv) in ((0, (8, V_PRE)), (3, (32, V_FPRE))):
                S.op("dve", lambda e, which=which, sc0=sc0, gv=gv, layer=layer, cnd=cnd, mod=mod: e.scalar_tensor_tensor(
                    out=DER[:, layer, cnd, which, :], in0=mod(sc0), scalar=1.0, in1=VEC[:, layer, gv:gv + 8],
                    op0=ALU.add, op1=ALU.mult), [R_MODS, R_VEC], [R_DER])
            for which, sh0 in ((1, 0), (4, 24)):
                S.op("dve", lambda e, which=which, sh0=sh0, layer=layer, cnd=cnd, mod=mod: e.tensor_copy(
                    out=DER[:, layer, cnd, which, :], in_=mod(sh0)), [R_MODS], [R_DER])
            for which, (g0, gv) in ((2, (16, V_POST)), (5, (40, V_FPOST))):
                S.op("dve", lambda e, which=which, g0=g0, gv=gv, layer=layer, cnd=cnd, mod=mod: e.tensor_tensor(
                    out=DER[:, layer, cnd, which, :], in0=mod(g0), in1=VEC[:, layer, gv:gv + 8], op=ALU.mult),
                    [R_MODS, R_VEC], [R_DER])

    for g in range(3):
        mods_dma(0, g)
    for g in range(24):
        if g + 3 < 24:
            mods_dma(0, g + 3)
        mods_mm(0, g)
    mods_finish(0)
    mods_l1_dma = list(range(24))
    mods_l1_mm = []
    S.barrier()

    def rstd_from(ps_ss, R_ss, W):
        rs, R_rs = nxt(RS, "rs")
        S.op("act", lambda e: e.activation(out=rs[:, :W], in_=ps_ss[:, :W], func=AF.Sqrt, bias=EPSB[:, 0:1], scale=1.0 / D),
             [R_ss, R_EPSB], [R_rs])
        S.op("dve", lambda e: e.reciprocal(out=ps_ss[:, :W], in_=rs[:, :W]), [R_rs], [R_ss])
        return ps_ss, R_ss

    def sumsq(src_fn, reads, W, bank):
        ps, R_ps = PS[bank]
        for c in range(NCH):
            sq, R_sq = nxt(SQ, "sq")
            S.op("act", lambda e, c=c, sq=sq: e.activation(out=sq[:, :W], in_=src_fn(c), func=AF.Square), reads(c), [R_sq])
            S.op("pe", lambda e, c=c, sq=sq: e.matmul(ps[:, :W], lhsT=ONES[:], rhs=sq[:, :W], start=(c == 0), stop=(c == 7)),
                 [R_sq, R_ONES], [R_ps], sig=True)
        return ps, R_ps

    def load_x(src, R_src, a, W, xt=None):
        X_, R_X = xt or (XT, R_XT)
        S.dma("sp", lambda e: e.dma_start(out=X_[:, :, 0:W], in_=src[:, :, a:a + W]), [R_src], [R_X], R_X)

    def prenorm(W, A, SH, out_fn, out_res, xt=None):
        X_, R_X = xt or (XT, R_XT)
        ps, R_ps = sumsq(lambda c: X_[:, c, 0:W], lambda c: [R_X], W, 7)
        rs, R_rs = rstd_from(ps, R_ps, W)
        for c in range(NCH):
            tmp, R_tmp = nxt(TMP, "tmp")
            S.op("dve", lambda e, c=c, tmp=tmp: e.scalar_tensor_tensor(
                out=tmp[:, :W], in0=X_[:, c, 0:W], scalar=A(c), in1=rs[:, :W], op0=ALU.mult, op1=ALU.mult),
                [R_X, R_rs, R_DER], [R_tmp])
            S.op("act", lambda e, c=c, tmp=tmp: e.activation(out=out_fn(c), in_=tmp[:, :W], func=AF.Identity, bias=SH(c), scale=1.0),
                 [R_tmp, R_DER], [out_res(c)])

    def post_chunk_evac(m, ps, R_ps, n, scale, ys=None):
        Y_, R_Y = ys or (YS, R_YS)
        if scale is None:
            S.op("act", lambda e: e.activation(out=Y_[:, m, :n], in_=ps[:, :n], func=AF.Copy), [R_ps], [R_Y[m]])
        else:
            S.op("act", lambda e: e.activation(out=Y_[:, m, :n], in_=ps[:, :n], func=AF.Identity, scale=scale(m)),
                 [R_ps, R_VEC], [R_Y[m]])

    def post_finish(n, G, x_off, dst, R_dst, dcol, xt=None, ys=None):
        X_, R_X = xt or (XT, R_XT)
        Y_, R_Y = ys or (YS, R_YS)
        ps, R_ps = sumsq(lambda c: Y_[:, c, :n], lambda c: [R_Y[c]], n, 6)
        rs, R_rs = rstd_from(ps, R_ps, n)
        for c in range(NCH):
            tmp, R_tmp = nxt(TMP, "tmp")
            S.op("dve", lambda e, c=c, tmp=tmp: e.scalar_tensor_tensor(
                out=tmp[:, :n], in0=Y_[:, c, :n], scalar=G(c), in1=rs[:, :n], op0=ALU.mult, op1=ALU.mult),
                [R_Y[c], R_rs, R_DER], [R_tmp])
            S.op("pool", lambda e, c=c, tmp=tmp: e.tensor_tensor(
                out=X_[:, c, x_off:x_off + n], in0=X_[:, c, x_off:x_off + n], in1=tmp[:, :n], op=ALU.add),
                [R_X, R_tmp], [R_X])
        S.dma("pool", lambda e: e.dma_start(out=dst[:, :, dcol:dcol + n], in_=X_[:, :, x_off:x_off + n]), [R_X], [R_dst], R_X)

    def flag_cols(buf_fn, R_buf, a, W, lo, hi, fcol):
        l, h = max(lo, a), min(hi, a + W)
        if l >= h:
            return
        S.op("dve", lambda e: e.tensor_scalar(out=buf_fn(l - a, h - a), in0=buf_fn(l - a, h - a),
                                                scalar1=FLG[:, fcol:fcol + 1], scalar2=None, op0=ALU.mult),
             [R_buf, R_FLG], [R_buf])

    def pool_phase(src, R_src, dst, R_dst, ntok, top, bot, cnd, fcol, cidx, with_mods=False):
        m0 = ar.off
        Us = [sb(f"U{i}", [128, NCH, 512], F32, nres=NCH) for i in range(2)]
        XTs = [(XT, R_XT), sb("XTb", [128, NCH, 512])]
        YSs = [(YS, R_YS), sb("YSb", [128, NCH, 512], F32, nres=NCH)]
        DBs = [(DB, R_DB), sb("DBb", [128, NCH, 512], BF16, nres=NCH)]
        T = [sb(f"T{i}", [128, 512]) for i in range(6)]
        lo_out, hi_out = 8, ntok - 8
        ntile = -(-(hi_out - lo_out) // 496)
        tsz = -(-(hi_out - lo_out) // ntile)

        def mods_hook(k):
            if with_mods:
                for _ in range(k):
                    if mods_l1_mm:
                        mods_mm(1, mods_l1_mm.pop(0))

        def do_tile(a, b, par):
            U, R_U = Us[par]
            xt = XTs[par]
            ys = YSs[par]
            DBp, R_DBp = DBs[par]
            n = b - a
            W = n + 16
            a0 = a - 8
            load_x(src, R_src, a0, W, xt=xt)
            prenorm(W, der(0, cnd, 0), der(0, cnd, 1), lambda c: U[:, c, 0:W], lambda c: R_U[c], xt=xt)
            mods_hook(2)
            for c in range(NCH):
                def ub(l, h, c=c):
                    return U[:, c, l:h]
                flag_cols(ub, R_U[c], a0, W, top - 8, top, fcol)
                flag_cols(ub, R_U[c], a0, W, bot, bot + 8, fcol + 1)
            for c in range(NCH):
                g = c // 2
                w = 2 << g
                t, R_t = nxt(T, "T")
                S.op("pool", lambda e, t=t, c=c: e.tensor_tensor(out=t[:, 1:W], in0=U[:, c, 1:W], in1=U[:, c, 0:W - 1], op=ALU.add),
                     [R_U[c]], [R_t])
                cur, R_cur = t, R_t
                vlo, vhi = 1, W
                sh = 1
                for lvl in range(g):
                    t2, R_t2 = nxt(T, "T")
                    nlo, nhi = vlo + sh, vhi - sh
                    S.op("pool", lambda e, t2=t2, cur=cur, nlo=nlo, nhi=nhi, sh=sh: e.tensor_tensor(
                        out=t2[:, nlo:nhi], in0=cur[:, nlo + sh:nhi + sh], in1=cur[:, nlo - sh:nhi - sh], op=ALU.add),
                        [R_cur], [R_t2])
                    cur, R_cur = t2, R_t2
                    vlo, vhi = nlo, nhi
                    sh *= 2
                assert vlo <= 8 and vhi >= W - 8, (vlo, vhi, W)
                for (l0, h0, off) in ((top, top + 8, 0), (bot - 8, bot, 8)):
                    l, h = max(l0, a), min(h0, b)
                    if l < h:
                        S.op("dve", lambda e, cur=cur, l=l, h=h, l0=l0, off=off, g=g: e.tensor_tensor(
                            out=cur[:, l - a0:h - a0], in0=cur[:, l - a0:h - a0],
                            in1=CORR[:, cidx, g, off + l - l0:off + h - l0], op=ALU.mult), [R_cur, R_CORR], [R_cur])
                S.op("dve", lambda e, cur=cur, c=c, w=w: e.scalar_tensor_tensor(
                    out=DBp[:, c, 0:n], in0=cur[:, 8:8 + n], scalar=1.0 / w, in1=U[:, c, 8:8 + n],
                    op0=ALU.mult, op1=ALU.subtract), [R_cur, R_U[c]], [R_DBp[c]])
            for m in range(NCH):
                g, ml = m // 2, m % 2
                ps, R_ps = PS[m % 2]
                for kc in range(2):
                    S.op("pe", lambda e, ps=ps, g=g, ml=ml, kc=kc: e.matmul(
                        ps[:, :n], lhsT=WP[:, g, kc, ml * 128:(ml + 1) * 128], rhs=DBp[:, 2 * g + kc, 0:n],
                        start=(kc == 0), stop=(kc == 1)), [R_WP, R_DBp[2 * g + kc]], [R_ps], sig=True)
                post_chunk_evac(m, ps, R_ps, n, lambda m: VEC[:, 0, V_PSC + m:V_PSC + m + 1], ys=ys)
            post_finish(n, der(0, cnd, 2), 8, dst, R_dst, a, xt=xt, ys=ys)
            mods_hook(2)

        a = lo_out
        ti = 0
        while a < hi_out:
            b = min(a + tsz, hi_out)
            do_tile(a, b, ti % 2)
            if with_mods:
                while mods_l1_mm:
                    mods_mm(1, mods_l1_mm.pop(0))
                for _ in range(4):
                    if mods_l1_dma:
                        g = mods_l1_dma.pop(0)
                        mods_dma(1, g)
                        mods_l1_mm.append(g)
            a = b
            ti += 1
        if with_mods:
            while mods_l1_mm or mods_l1_dma:
                while mods_l1_mm:
                    mods_mm(1, mods_l1_mm.pop(0))
                for _ in range(4):
                    if mods_l1_dma:
                        g = mods_l1_dma.pop(0)
                        mods_dma(1, g)
                        mods_l1_mm.append(g)
            mods_finish(1)
        S.barrier()
        ar.off = m0

    def ffn_phase(layer, src, R_src, dst, R_dst, lo_out, hi_out, top, bot, cnd, fcol, dst_shift):
        m0 = ar.off
        NST = 2
        UBs = [sb(f"UB{i}", [128, NCH, NST * 512], BF16, nres=NST) for i in range(2)]
        G, R_G = sb("G", [128, NJ, NST * 448], BF16, nres=NST)
        WU = [sb(f"WU{i}", [128, 2, 8, 128], BF16) for i in range(2)]
        WD = [sb(f"WD{i}", [128, NJ, 128], BF16) for i in range(2)]
        FT = [sb(f"FT{i}", [128, 512]) for i in range(12)]
        XTn = sb("XTn", [128, NCH, 512])
        tot = hi_out - lo_out
        ntile = -(-tot // 448)
        if ntile > 1 and ntile % 2:
            ntile += 1
        tsz = -(-tot // ntile)
        tiles = []
        a = lo_out
        while a < hi_out:
            b = min(a + tsz, hi_out)
            tiles.append((a, b))
            a = b
        sts = [tiles[i:i + NST] for i in range(0, len(tiles), NST)]

        def norm_tile(si, ti):
            a, b = sts[si][ti]
            UB, R_UB = UBs[si % 2]
            n = b - a
            W = n + 2
            load_x(src, R_src, a - 1, W, xt=XTn)
            prenorm(W, der(layer, cnd, 3), der(layer, cnd, 4),
                    lambda c: UB[:, c, ti * 512:ti * 512 + W], lambda c: R_UB[ti], xt=XTn)

            def ubf(l, h):
                return UB[:, :, ti * 512 + l:ti * 512 + h]
            flag_cols(ubf, R_UB[ti], a - 1, W, top - 1, top, fcol)
            flag_cols(ubf, R_UB[ti], a - 1, W, bot, bot + 1, fcol + 1)

        pend = []

        def stage3(ag, R_ag, av, R_av, n, j, ti):
            p, R_p = nxt(FT, "ft")
            S.op("act", lambda e: e.activation(out=p[:, :n], in_=ag[:, :n], func=AF.Gelu_apprx_tanh), [R_ag], [R_p])
            S.op("pool", lambda e: e.tensor_tensor(out=G[:, j, ti * 448:ti * 448 + n], in0=p[:, :n], in1=av[:, :n], op=ALU.mult),
                 [R_p, R_av], [R_G[ti]])

        def stage12(si, j, wu, R_wu, ti):
            a, b = sts[si][ti]
            UB, R_UB = UBs[si % 2]
            n = b - a
            W = n + 2
            hp = []
            for part in range(2):
                ps, R_ps = PS[nxt([0, 1, 2, 3, 4, 5], "ffnps")]
                for k in range(8):
                    S.op("pe", lambda e, ps=ps, part=part, k=k: e.matmul(
                        ps[:, :W], lhsT=wu[:, part, k, :], rhs=UB[:, k, ti * 512:ti * 512 + W],
                        start=(k == 0), stop=(k == 7)), [R_wu, R_UB[ti]], [R_ps], sig=True)
                hp.append((ps, R_ps))
            conv = []
            for part in range(2):
                ps, R_ps = hp[part]
                ch = part * NJ + j
                acc, R_acc = nxt(FT, "ft")
                S.op("act", lambda e, ps=ps, acc=acc, ch=ch: e.activation(
                    out=acc[:, :n], in_=ps[:, 1:n + 1], func=AF.Identity,
                    bias=VEC[:, layer, V_CB + ch:V_CB + ch + 1], scale=VEC[:, layer, V_CW + 44 + ch:V_CW + 44 + ch + 1]),
                    [R_ps, R_VEC], [R_acc])
                for tap, off in ((0, 0), (2, 2)):
                    S.op("dve", lambda e, ps=ps, acc=acc, ch=ch, tap=tap, off=off: e.scalar_tensor_tensor(
                        out=acc[:, :n], in0=ps[:, off:off + n], scalar=VEC[:, layer, V_CW + tap * 44 + ch:V_CW + tap * 44 + ch + 1],
                        in1=acc[:, :n], op0=ALU.mult, op1=ALU.add), [R_ps, R_acc, R_VEC], [R_acc])
                conv.append((acc, R_acc))
            (ag, R_ag), (av, R_av) = conv
            pend.append((ag, R_ag, av, R_av, n, j, ti))
            if len(pend) > 1:
                stage3(*pend.pop(0))

        def down_tile(si, ti):
            a, b = sts[si][ti]
            n = b - a
            for m in range(NCH):
                wd, R_wd = nxt(WD, "wd")
                S.dma("sp", lambda e, wd=wd, m=m: e.dma_start(out=wd[:], in_=wdnb[layer, m]), [R_wdnc[layer][m]], [R_wd], R_wd)
                ps, R_ps = PS[m % 4]
                for j in range(NJ):
                    S.op("pe", lambda e, ps=ps, wd=wd, j=j: e.matmul(
                        ps[:, :n], lhsT=wd[:, j, :], rhs=G[:, j, ti * 448:ti * 448 + n],
                        start=(j == 0), stop=(j == NJ - 1)), [R_wd, R_G[ti]], [R_ps], sig=True)
                post_chunk_evac(m, ps, R_ps, n, None)
            load_x(src, R_src, a, n)
            if si + 1 < len(sts) and ti < len(sts[si + 1]):
                norm_tile(si + 1, ti)
            post_finish(n, der(layer, cnd, 5), 0, dst, R_dst, a - dst_shift)

        for ti in range(len(sts[0])):
            norm_tile(0, ti)
        for si, st in enumerate(sts):
            for j in range(NJ):
                wu, R_wu = nxt(WU, "wu")
                S.dma("sp", lambda e, wu=wu, j=j: e.dma_start(out=wu[:], in_=wupb[layer, j]), [R_wupc[layer][j]], [R_wu], R_wu)
                for ti in range(len(st)):
                    stage12(si, j, wu, R_wu, ti)
            while pend:
                stage3(*pend.pop(0))
            nxt_n = len(sts[si + 1]) if si + 1 < len(sts) else 0
            for ti in range(max(len(st), nxt_n)):
                if ti < len(st):
                    down_tile(si, ti)
                elif ti < nxt_n:
                    norm_tile(si + 1, ti)
        S.barrier()
        ar.off = m0

    if stop_after >= 1:
        pool_phase(xin, R_xin, xsA, R_xsA, NT, TOP, BOT, 0, 0, 0, with_mods=True)
        pool_phase(cin, R_cin, csA, R_csA, NTC, TOPC, BOTC, 1, 2, 1)
    ar.off = m1
    if stop_after >= 2:
        ffn_phase(0, xsA, R_xsA, xsB, R_xsB, PAD, NT - PAD, TOP, BOT, 0, 0, 0)
        ffn_phase(0, csA, R_csA, csB, R_csB, TOPC, BOTC, TOPC, BOTC, 1, 2, 0)

    def kv_phase():
        m0 = ar.off
        WK, R_WK = sb("WK", [128, 8, 8, 128], BF16)
        WV, R_WV = sb("WV", [128, 8, 1024], BF16)
        KOs = [sb(f"KO{i}", [128, NCH, 512], BF16) for i in range(2)]
        VOs = [sb(f"VO{i}", [128, 4, 1024], BF16) for i in range(2)]
        XTs = [(XT, R_XT), sb("XTk", [128, NCH, 512])]
        DBs = [(DB, R_DB), sb("DBk", [128, NCH, 512], BF16, nres=NCH)]
        S.dma("pool", lambda e: e.dma_start(out=WK[:], in_=wk[:], max_dma_last_dim=4096), [R_const], [R_WK], R_WK)
        S.dma("pool", lambda e: e.dma_start(out=WV[:], in_=wv[:], max_dma_last_dim=4096), [R_const], [R_WV], R_WV)

        def kv_tile(src, R_src, a, n, cnd, par):
            xt = XTs[par]
            DBp, R_DBp = DBs[par]
            KO, R_KO = KOs[par]
            VO, R_VO = VOs[par]
            load_x(src, R_src, a, n, xt=xt)
            prenorm(n, der(1, cnd, 0), der(1, cnd, 1), lambda c: DBp[:, c, 0:n], lambda c: R_DBp[c], xt=xt)
            for m in range(NCH):
                ps, R_ps = PS[m % 2]
                for k in range(8):
                    S.op("pe", lambda e, ps=ps, m=m, k=k: e.matmul(ps[:, :n], lhsT=WK[:, m, k, :], rhs=DBp[:, k, 0:n],
                                                                  start=(k == 0), stop=(k == 7)),
                         [R_WK, R_DBp[k]], [R_ps], sig=True)
                S.op("act", lambda e, ps=ps, m=m: e.activation(out=KO[:, m, :n], in_=ps[:, :n], func=AF.Copy), [R_ps], [R_KO])
            for sub in range(n // 128):
                for half in range(2):
                    ps, R_ps = PS[2 + nxt([0, 1, 2, 3], "kvps")]
                    for k in range(8):
                        S.op("pe", lambda e, ps=ps, k=k, sub=sub, half=half: e.matmul(
                            ps[:, :], lhsT=DBp[:, k, sub * 128:(sub + 1) * 128], rhs=WV[:, k, half * 512:(half + 1) * 512],
                            start=(k == 0), stop=(k == 7)), [R_WV, R_DBp[k]], [R_ps], sig=True)
                    if half == 0:
                        S.op("dve", lambda e, ps=ps, sub=sub, half=half: e.tensor_copy(out=VO[:, sub, half * 512:(half + 1) * 512], in_=ps[:, :]),
                             [R_ps], [R_VO])
                    else:
                        S.op("act", lambda e, ps=ps, sub=sub, half=half: e.activation(out=VO[:, sub, half * 512:(half + 1) * 512], in_=ps[:, :], func=AF.Copy),
                             [R_ps], [R_VO])
            if cnd == 0:
                tk = a - PAD
                S.dma("pool", lambda e: e.dma_start(out=kts[:, :, tk:tk + n], in_=KO[:, :, 0:n]), [R_KO], [R_kts], R_KO)
                S.dma("pool", lambda e: e.dma_start(out=vts[:, tk // 128:tk // 128 + n // 128, :], in_=VO[:, 0:n // 128, :]),
                      [R_VO], [R_vts], R_VO)
            else:
                S.op("act", lambda e: e.activation(out=KCT[:, :, 0:n], in_=KO[:, :, 0:n], func=AF.Copy), [R_KO], [R_KCT])
                S.op("dve", lambda e: e.tensor_copy(out=VC[:, 0:n // 128, :], in_=VO[:, 0:n // 128, :]), [R_VO], [R_VC])

        ti = 0
        for (src, R_src, t0, tend, cnd) in ((xsB, R_xsB, PAD, NT - PAD, 0), (csB, R_csB, TOPC, BOTC, 1)):
            a = t0
            while a < tend:
                n = min(512, tend - a)
                kv_tile(src, R_src, a, n, cnd, ti % 2)
                ti += 1
                a += n
        S.barrier()
        ar.off = m0

    if stop_after >= 3:
        KCT, R_KCT = sb("KCT", [128, NCH, 256], BF16)
        VC, R_VC = sb("VC", [128, 2, 1024], BF16)
        kv_phase()

    def attn_phase():
        m0 = ar.off
        WQ, R_WQ = sb("WQ", [128, 8, 8, 128], BF16)
        WO, R_WO = sb("WO", [128, 8, 8, 128], BF16)
        KT, R_KT = sb("KT", [128, NCH, 1024], BF16)
        VT, R_VT = sb("VT", [128, 8, 1024], BF16)
        QT, R_QT = sb("QT", [128, 2, NCH, 512], BF16)
        OT, R_OT = sb("OT", [128, NCH, 512], BF16, nres=NCH)
        TB = [sb(f"TB{i}", [128, TABMAX]) for i in range(2)]
        SBF = [sb(f"SBF{i}", [128, 512]) for i in range(3)]
        EB = [sb(f"EB{i}", [128, 512], BF16) for i in range(4)]
        RD, R_RD = sb("RD", [128, 512])
        S.dma("pool", lambda e: e.dma_start(out=WQ[:], in_=wq[:], max_dma_last_dim=4096), [R_const], [R_WQ], R_WQ)
        S.op("pool", lambda e: e.memset(QT[:], 0.0), [], [R_QT])
        S.dma("pool", lambda e: e.dma_start(out=WO[:], in_=wo[:], max_dma_last_dim=4096), [R_const], [R_WO], R_WO)
        XTq = [(XT, R_XT), sb("XTq", [128, NCH, 512])]
        gcount = [0]

        def do_group(q0, nr, chunks, typ):
            nq = nr * 64
            t0 = PAD + q0 * 64
            c0 = chunks[0]
            nck = len(chunks)
            xt = XTq[gcount[0] % 2]
            gcount[0] += 1
            load_x(xsB, R_xsB, t0, nq, xt=xt)
            S.dma("sp", lambda e, c0=c0, nck=nck: e.dma_start(out=KT[:, :, 0:nck * 128], in_=kts[:, :, c0 * 128:(c0 + nck) * 128]),
                  [R_kts], [R_KT], R_KT)
            S.dma("sp", lambda e, c0=c0, nck=nck: e.dma_start(out=VT[:, 0:nck, :], in_=vts[:, c0:c0 + nck, :]), [R_vts], [R_VT], R_VT)
            prenorm(nq, der(1, 0, 0), der(1, 0, 1), lambda c: DB[:, c, 0:nq], lambda c: R_DB[c], xt=xt)
            for m in range(NCH):
                ps, R_ps = PS[m % 2]
                for k in range(8):
                    S.op("pe", lambda e, ps=ps, m=m, k=k: e.matmul(ps[:, :nq], lhsT=WQ[:, m, k, :], rhs=DB[:, k, 0:nq],
                                                                  start=(k == 0), stop=(k == 7)), [R_WQ, R_DB[k]], [R_ps], sig=True)
                S.op("act", lambda e, ps=ps, m=m: e.activation(out=QT[0:64, 0, m, :nq], in_=ps[0:64, :nq], func=AF.Copy, scale=0.125), [R_ps], [R_QT])
                S.op("act", lambda e, ps=ps, m=m: e.activation(out=QT[64:128, 1, m, :nq], in_=ps[64:128, :nq], func=AF.Copy, scale=0.125), [R_ps], [R_QT])
            colr = COLR[typ]
            items = []
            tab_loaders = {}

            def make_head(h):
                m, pb = h // 2, (h % 2) * 64
                po, R_po = PS[4 + (h % 2)]
                pd, R_pd = PS[6 + (h % 2)]
                hs = {}
                nit = 2 + len(chunks)

                def load_tab():
                    if "tb" in hs:
                        return
                    tb, R_tb = nxt(TB, "tb")
                    hs["tb"] = (tb, R_tb)
                    S.dma("sp", lambda e: e.dma_start(out=tb[:, 0:TABW[typ]], in_=(tabL[typ - 1, h, :, 0:TABW[typ]] if typ in (1, 2, 3)
                                                                                  else tabS[typ // 4, h, :, 0:TABW[typ]])),
                          [R_const], [R_tb], R_tb)

                tab_loaders[h] = load_tab

                def end_fn():
                    S.op("act", lambda e: e.activation(out=RD[pb:pb + 64, :nq], in_=pd[pb:pb + 64, :nq], func=AF.Ln), [R_pd], [R_RD])
                    S.op("act", lambda e: e.activation(out=RD[pb:pb + 64, :nq], in_=RD[pb:pb + 64, :nq], func=AF.Exp, scale=-1.0), [R_RD], [R_RD])
                    S.op("dve", lambda e: e.tensor_tensor(out=OT[pb:pb + 64, m, :nq], in0=po[pb:pb + 64, :nq],
                                                          in1=RD[pb:pb + 64, :nq], op=ALU.mult), [R_po, R_RD], [R_OT[m]])

                def mk_item(idx, kind, j, off, clo, chi):
                    d = {}
                    ncj = chi - clo
                    first, last = (idx == 0), (idx == nit - 1)

                    def s_fn():
                        ps, R_ps = PS[nxt([0, 1, 2, 3], "aps")]
                        d["ps"] = (ps, R_ps)
                        if kind == "ctx":
                            S.op("pe", lambda e: e.matmul(ps[:, :ncj], lhsT=KCT[:, m, j * 128:(j + 1) * 128],
                                                          rhs=QT[:, h % 2, m, clo:chi], start=True, stop=True), [R_KCT, R_QT], [R_ps])
                        else:
                            S.op("pe", lambda e: e.matmul(ps[:, :ncj], lhsT=KT[:, m, j * 128:(j + 1) * 128],
                                                          rhs=QT[:, h % 2, m, clo:chi], start=True, stop=True), [R_KT, R_QT], [R_ps])

                    def pv_fn():
                        ps, R_ps = d["ps"]
                        eb, R_eb = nxt(EB, "eb")
                        if kind == "ctx":
                            S.op("act", lambda e: e.activation(out=eb[:, :ncj], in_=ps[:, :ncj], func=AF.Exp), [R_ps], [R_eb])
                            vsrc, R_vsrc = VC, R_VC
                        else:
                            tb, R_tb = hs["tb"]
                            sbf, R_sbf = nxt(SBF, "sbf")
                            S.op("dve", lambda e: e.tensor_tensor(out=sbf[:, :ncj], in0=ps[:, :ncj], in1=tb[:, off:off + ncj], op=ALU.add),
                                 [R_ps, R_tb], [R_sbf])
                            S.op("act", lambda e: e.activation(out=eb[:, :ncj], in_=sbf[:, :ncj], func=AF.Exp), [R_sbf], [R_eb])
                            vsrc, R_vsrc = VT, R_VT
                        S.op("pe", lambda e: e.matmul(po[:, clo:chi], lhsT=vsrc[:, j, m * 128:(m + 1) * 128], rhs=eb[:, :ncj],
                                                      start=first, stop=last), [R_vsrc, R_eb], [R_po], sig=True)
                        S.op("pe", lambda e: e.matmul(pd[:, clo:chi], lhsT=ONES[:, :], rhs=eb[:, :ncj],
                                                      start=first, stop=last), [R_ONES, R_eb], [R_pd], sig=True)
                    items.append((s_fn, pv_fn, (lambda: tab_loaders[h + 1]()) if (first and h + 1 < 16) else None, end_fn if last else None))

                idx = 0
                for cc in range(2):
                    mk_item(idx, "ctx", cc, 0, 0, nq)
                    idx += 1
                off = 0
                for j in range(len(chunks)):
                    lo, hi = colr[j]
                    mk_item(idx, "lat", j, off, lo * 64, hi * 64)
                    off += (hi - lo) * 64
                    idx += 1

            for h in range(16):
                make_head(h)
            tab_loaders[0]()
            LA = 3
            deferred = []
            for i in range(min(LA, len(items))):
                items[i][0]()
            for i in range(len(items)):
                if i + LA < len(items):
                    items[i + LA][0]()
                if items[i][2]:
                    items[i][2]()
                items[i][1]()
                if items[i][3]:
                    deferred.append((i + 2, items[i][3]))
                while deferred and deferred[0][0] <= i:
                    deferred.pop(0)[1]()
            while deferred:
                deferred.pop(0)[1]()
            for m in range(NCH):
                ps, R_ps = PS[m % 2]
                for k in range(8):
                    S.op("pe", lambda e, ps=ps, m=m, k=k: e.matmul(ps[:, :nq], lhsT=WO[:, m, k, :], rhs=OT[:, k, 0:nq],
                                                                  start=(k == 0), stop=(k == 7)), [R_WO, R_OT[k]], [R_ps], sig=True)
                post_chunk_evac(m, ps, R_ps, nq, None)
            post_finish(nq, der(1, 0, 2), 0, xsA, R_xsA, t0, xt=xt)

        for (q0, nr, chunks, typ) in AGROUPS:
            do_group(q0, nr, chunks, typ)
        S.barrier()
        ar.off = m0

    if stop_after >= 4:
        attn_phase()
    if stop_after >= 5:
        ffn_phase(1, xsA, R_xsA, outT, R_out, TOP, BOT, TOP, BOT, 0, 0, TOP)

    if dbg:
        srcx = xsA if stop_after in (1, 4) else xsB
        R_srcx = R_xsA if stop_after in (1, 4) else R_xsB
        srcc = csA if stop_after == 1 else csB
        R_srcc = R_csA if stop_after == 1 else R_csB
        R_d1, R_d2 = Res("dbg1"), Res("dbg2")
        S.dma("pool", lambda e: e.dma_start(out=dbgx[:], in_=srcx[:]), [R_srcx], [R_out], R_d1)
        S.dma("pool", lambda e: e.dma_start(out=dbgc[:], in_=srcc[:]), [R_srcc], [R_out], R_d2)
    S.barrier(final=True)
    S.emit(nc, es)
    es.close()
    return nc


def _fm(a2d):
    T = a2d.shape[0]
    return np.ascontiguousarray(a2d.reshape(T, NCH, 128).transpose(2, 1, 0))


def _vec_cols(v):
    return np.ascontiguousarray(v.reshape(-1, 128).T)


def _bias_tables(rpb, quarter):
    r0 = quarter * 32
    tabs = np.full((5, 16, 128, TABMAX), MASK, np.float32)
    qcol = np.arange(64)
    kcol = np.arange(64)
    ws = np.clip(qcol - 8, 0, 48)
    cvalid = (kcol[:, None] >= ws[None, :]) & (kcol[:, None] < ws[None, :] + 16)
    dc = np.clip(kcol[:, None] - qcol[None, :] + 15, 0, 30)
    done = set()
    for (q0, nr, chunks, typ) in AGROUPS:
        if typ in done:
            continue
        done.add(typ)
        off = 0
        for j, c in enumerate(chunks):
            lo, hi = COLR[typ][j]
            for qr in range(lo, hi):
                gq = r0 - 5 + q0 + qr
                for kr2 in range(2):
                    gk = r0 - 5 + 2 * c + kr2
                    ok = (0 <= gq < 128) and (0 <= gk < 128)
                    if ok:
                        wlo, whi = _row_window(gq)
                        ok = wlo <= gk < whi
                    if ok:
                        dr = gk - gq + 7
                        blk = np.where(cvalid[None], rpb[:, dr][:, dc], np.float32(MASK))
                        tabs[typ, :, kr2 * 64:(kr2 + 1) * 64, off + (qr - lo) * 64: off + (qr - lo + 1) * 64] = blk
            off += (hi - lo) * 64
    return tabs


_CACHE = {}


def kernel(x, c, ctx, c_ctx, ada_w, ada_b, mix_pre_g, mix_post_g, ffn_pre_g, ffn_post_g,
           pool_w, pool_scale, na_w_qkv, na_w_o, na_rpb, ffn_w_up, ffn_conv_w, ffn_conv_b, ffn_w_down,
           _stop_after=99, _dbg=False):
    f = lambda a: np.asarray(a, dtype=np.float32)
    x, c, ctx, c_ctx = f(x), f(c), f(ctx), f(c_ctx)
    ada_w, ada_b = f(ada_w), f(ada_b)
    key = (_stop_after, _dbg)
    if key not in _CACHE:
        _CACHE[key] = build_program(_stop_after, _dbg)
    nc = _CACHE[key]

    vecs = np.zeros((128, 2, NV), np.float32)
    for i in range(2):
        vecs[:, i, V_PRE:V_PRE + 8] = _vec_cols(f(mix_pre_g)[i])
        vecs[:, i, V_POST:V_POST + 8] = _vec_cols(f(mix_post_g)[i])
        vecs[:, i, V_FPRE:V_FPRE + 8] = _vec_cols(f(ffn_pre_g)[i])
        vecs[:, i, V_FPOST:V_FPOST + 8] = _vec_cols(f(ffn_post_g)[i])
        vecs[:, i, V_PSC:V_PSC + 8] = _vec_cols(f(pool_scale)[0])
        vecs[:, i, V_ADAB:V_ADAB + 48] = _vec_cols(ada_b[i])
        for tap in range(3):
            vecs[:, i, V_CW + tap * 44:V_CW + (tap + 1) * 44] = _vec_cols(f(ffn_conv_w)[i, tap])
        vecs[:, i, V_CB:V_CB + 44] = _vec_cols(f(ffn_conv_b)[i])
    adaw = np.ascontiguousarray(ada_w.reshape(2, 8, 128, 24, 256).transpose(0, 3, 2, 1, 4))
    wpool = np.ascontiguousarray(f(pool_w)[0].reshape(4, 2, 128, 256).transpose(2, 0, 1, 3))
    wup = np.ascontiguousarray(f(ffn_w_up).reshape(2, 8, 128, 2, NJ, 128).transpose(0, 4, 2, 3, 1, 5))
    wdn = np.ascontiguousarray(f(ffn_w_down).reshape(2, NJ, 128, 8, 128).transpose(0, 3, 2, 1, 4))
    wqkv = f(na_w_qkv)[0].reshape(8, 128, 3, 8, 128)
    wq = np.ascontiguousarray(wqkv[:, :, 0].transpose(1, 2, 0, 3))
    wk = np.ascontiguousarray(wqkv[:, :, 1].transpose(1, 2, 0, 3))
    wv = np.ascontiguousarray(wqkv[:, :, 2].reshape(8, 128, 1024).transpose(1, 0, 2))
    wo = np.ascontiguousarray(f(na_w_o)[0].reshape(8, 128, 8, 128).transpose(1, 2, 0, 3))
    rpb = f(na_rpb)[0]

    def corr_tab(top_on, bot_on):
        t = np.ones((4, 16), np.float32)
        for g in range(4):
            w = 2 << g
            for i in range(8):
                if top_on:
                    t[g, i] = w / (w // 2 + min(i, w // 2))
                if bot_on:
                    ip = 7 - i
                    t[g, 8 + i] = w / (w // 2 + min(ip + 1, w // 2))
        return t

    in_maps = []
    tabs_cache = {}
    for core in range(8):
        b, q = core // 4, core % 4
        r0 = q * 32
        xe = np.zeros((NT, D), np.float32)
        g_lo = (r0 - 5) * 64 - PAD
        lo, hi = max(g_lo, 0), min(g_lo + NT, 8192)
        xe[lo - g_lo:hi - g_lo] = x[b, lo:hi]
        ce = np.zeros((NTC, D), np.float32)
        ce[PAD:PAD + CTX] = ctx[b]
        cnd = np.stack([_vec_cols(c[b]), _vec_cols(c_ctx)], axis=-1)
        fl = np.zeros((128, 4), np.float32)
        fl[:, 0] = 0.0 if q == 0 else 1.0
        fl[:, 1] = 0.0 if q == 3 else 1.0
        cr = np.stack([corr_tab(q == 0, q == 3), corr_tab(True, True)], axis=0)
        cr = np.ascontiguousarray(np.broadcast_to(cr[None], (128, 2, 4, 16)))
        if q not in tabs_cache:
            tabs_cache[q] = _bias_tables(rpb, q)
        in_maps.append(dict(xin=_fm(xe), cin=_fm(ce), cond=np.ascontiguousarray(cnd), vecs=vecs, flg=fl, corr=cr,
                            adaw=adaw, wpool=wpool, wup=wup, wdn=wdn, wq=wq, wk=wk, wv=wv, wo=wo,
                            tabL=np.ascontiguousarray(tabs_cache[q][1:4]),
                            tabS=np.ascontiguousarray(tabs_cache[q][[0, 4]][..., :TABSMALL])))
    res = run_bass_kernel_spmd(nc, in_maps, core_ids=list(range(8)))
    if _dbg:
        return res.results
    out = np.zeros((2, 8192, D), np.float32)
    for core in range(8):
        b, q = core // 4, core % 4
        o = res.results[core]["outT"]
        out[b, q * 2048:(q + 1) * 2048] = o.transpose(2, 1, 0).reshape(2048, D)
    return out
```

```python
import numpy as np
from contextlib import ExitStack
import concourse.bass as bass
import concourse.mybir as mybir
from concourse.bass_utils import run_bass_kernel_spmd

F32 = mybir.dt.float32
BF16 = mybir.dt.bfloat16
AF = mybir.ActivationFunctionType
ALU = mybir.AluOpType

D = 1024
NCH = 8
GRID_W = 64
PAD = 16
ROWS_EXT = 42
NT = PAD + ROWS_EXT * 64 + PAD
TOP = PAD + 5 * 64
BOT = TOP + 2048
CTX = 256
NTC = PAD + CTX + PAD
TOPC = PAD
BOTC = PAD + CTX
DFF = 2816
NJ = 22
EPS = 1e-6
MASK = -30000.0
NV = 8 * 5 + 48 + 3 * 44 + 44
V_PRE, V_POST, V_FPRE, V_FPOST, V_PSC, V_ADAB, V_CW, V_CB = 0, 8, 16, 24, 32, 40, 88, 220

AGROUPS = [
    (4, 1, [0, 1, 2, 3], 0),
    (5, 8, list(range(0, 8)), 1),
    (13, 8, list(range(4, 12)), 2),
    (21, 8, list(range(8, 16)), 2),
    (29, 8, list(range(12, 20)), 3),
    (37, 1, [16, 17, 18, 19, 20], 4),
]


def _row_window(gq):
    s0 = min(max(gq - 4, 0), 120)
    return s0, s0 + 8


def _group_colranges():
    out = {}
    for (q0, nr, chunks, typ) in AGROUPS:
        if typ in out:
            continue
        rngs = []
        for c in chunks:
            rows = set()
            for r0 in (0, 32, 96):
                for qr in range(nr):
                    gq = r0 - 5 + q0 + qr
                    if gq < 0 or gq > 127:
                        continue
                    lo, hi = _row_window(gq)
                    for kr2 in range(2):
                        gk = r0 - 5 + 2 * c + kr2
                        if lo <= gk < hi:
                            rows.add(qr)
            if not rows:
                rows = {0}
            rngs.append((min(rows), max(rows) + 1))
        out[typ] = rngs
    return out


COLR = _group_colranges()
TABW = {t: sum((hi - lo) * 64 for lo, hi in COLR[t]) for t in COLR}
TABMAX = max(TABW.values())
TABSMALL = max(TABW[0], TABW[4])


class Res:
    __slots__ = ("name", "w", "r", "cnt", "sem", "nobar")

    def __init__(self, name):
        self.name = name
        self.w = {}
        self.r = {}
        self.cnt = 0
        self.sem = None
        self.nobar = False


class Sched:
    ENG = ("pe", "act", "dve", "pool", "sp")

    def __init__(self):
        self.streams = {e: [] for e in self.ENG}
        self.cnt = {e: 0 for e in self.ENG}
        self.known = {e: {} for e in self.ENG}
        self.slots = []
        self.sw = {}

    def _need(self, eng, toks, waits, raw):
        for key, val in toks.items():
            if isinstance(key, str):
                if key == eng:
                    if eng == "pe" or not raw:
                        continue
            else:
                val = key.cnt
            if self.known[eng].get(key, 0) >= val:
                continue
            if waits.get(key, 0) < val:
                waits[key] = val

    def _deps(self, eng, reads, writes):
        waits = {}
        for r in reads:
            self._need(eng, r.w, waits, True)
        for w in writes:
            self._need(eng, w.w, waits, False)
            self._need(eng, w.r, waits, False)
        for k, v in waits.items():
            self.known[eng][k] = v
        return list(waits.items())

    def _commit(self, key, val, reads, writes):
        for r in reads:
            if r.r.get(key, 0) < val:
                r.r[key] = val
        for w in writes:
            w.w = {key: val}
            w.r = {}

    def op(self, eng, fn, reads=(), writes=(), sig=True):
        waits = self._deps(eng, reads, writes)
        if sig:
            self.cnt[eng] += 1
            val = self.cnt[eng]
        else:
            val = self.cnt[eng] + 1
        self.streams[eng].append((waits, fn, ("eng", eng) if sig else None))
        self._commit(eng, val, reads, writes)

    def dma(self, q, fn, reads, writes, slot):
        if q == "pool":
            key = id(slot)
            if key not in self.sw:
                r = Res(slot.name + "_sw")
                r.nobar = slot.nobar
                self.sw[key] = r
            slot = self.sw[key]
        waits = self._deps(q, reads, writes)
        if slot.cnt == 0:
            self.slots.append(slot)
        slot.cnt += 16
        self.streams[q].append((waits, fn, ("dma", slot)))
        self._commit(slot, slot.cnt, reads, writes)

    def barrier(self, final=False):
        for e in self.ENG:
            waits = {}
            for o in self.ENG[:4]:
                if o != e and self.known[e].get(o, 0) < self.cnt[o]:
                    waits[o] = self.cnt[o]
            for s in self.slots:
                if s.nobar and not final:
                    continue
                if self.known[e].get(s, 0) < s.cnt:
                    waits[s] = s.cnt
            for k, v in waits.items():
                self.known[e][k] = v
            if waits:
                self.streams[e].append((list(waits.items()), None, None))

    def emit(self, nc, es):
        sems = {}
        for e in self.ENG[:4]:
            sems[e] = es.enter_context(nc.semaphore("sem_" + e))
        for s in self.slots:
            s.sem = es.enter_context(nc.semaphore("sd_" + s.name))
        block = es.enter_context(nc.Block())

        def run(engname):
            def body(eng):
                for waits, fn, inc in self.streams[engname]:
                    for key, val in waits:
                        eng.wait_ge(sems[key] if isinstance(key, str) else key.sem, val)
                    if fn is None:
                        continue
                    inst = fn(eng)
                    if inc is not None:
                        if inc[0] == "eng":
                            inst.then_inc(sems[inc[1]], 1)
                        else:
                            inst.then_inc(inc[1].sem, 16)
            return body

        block.tensor(run("pe"))
        block.scalar(run("act"))
        block.vector(run("dve"))
        block.gpsimd(run("pool"))
        block.sync(run("sp"))


class Arena:
    def __init__(self, nc, base, top):
        self.nc = nc
        self.off = base
        self.top = top
        self.n = 0

    def alloc(self, name, shape, dtype):
        per = 1
        for s in shape[1:]:
            per *= s
        nbytes = per * (2 if dtype == BF16 else 4)
        off = (self.off + 63) // 64 * 64
        assert off + nbytes <= self.top, f"SBUF overflow at {name}: {off + nbytes} > {self.top}"
        self.off = off + nbytes
        self.hw = max(getattr(self, 'hw', 0), self.off)
        self.hwlog = getattr(self, 'hwlog', {})
        self.hwlog[name] = self.off
        self.n += 1
        return self.nc.alloc_sbuf_tensor_at(f"{name}_{self.n}", list(shape), dtype, offset=off)


def build_program(stop_after=99, dbg=False):
    nc = bass.Bass("TRN2", target_bir_lowering=False)
    S = Sched()

    def din(name, shape, dt=F32):
        return nc.dram_tensor(name, list(shape), dt, kind="ExternalInput")

    xin = din("xin", [128, NCH, NT])
    cin = din("cin", [128, NCH, NTC])
    cond = din("cond", [128, NCH, 2])
    vecs = din("vecs", [128, 2, NV])
    flg = din("flg", [128, 4])
    corr = din("corr", [128, 2, 4, 16])
    adaw = din("adaw", [2, 24, 128, 8, 256])
    wpool = din("wpool", [128, 4, 2, 256])
    wup = din("wup", [2, NJ, 128, 2, 8, 128])
    wdn = din("wdn", [2, 8, 128, NJ, 128])
    wq = din("wq", [128, 8, 8, 128])
    wk = din("wk", [128, 8, 8, 128])
    wv = din("wv", [128, 8, 1024])
    wo = din("wo", [128, 8, 8, 128])
    tabL = din("tabL", [3, 16, 128, TABMAX])
    tabS = din("tabS", [2, 16, 128, TABSMALL])
    outT = nc.dram_tensor("outT", [128, NCH, 2048], F32, kind="ExternalOutput")

    xsA = nc.dram_tensor("xsA", [128, NCH, NT], F32)
    xsB = nc.dram_tensor("xsB", [128, NCH, NT], F32)
    csA = nc.dram_tensor("csA", [128, NCH, NTC], F32)
    csB = nc.dram_tensor("csB", [128, NCH, NTC], F32)
    wupb = nc.dram_tensor("wupb", [2, NJ, 128, 2, 8, 128], BF16)
    wdnb = nc.dram_tensor("wdnb", [2, 8, 128, NJ, 128], BF16)
    kts = nc.dram_tensor("kts", [128, NCH, 21 * 128], BF16)
    vts = nc.dram_tensor("vts", [128, 21, 1024], BF16)
    if dbg:
        dbgx = nc.dram_tensor("dbgx", [128, NCH, NT], F32, kind="ExternalOutput")
        dbgc = nc.dram_tensor("dbgc", [128, NCH, NTC], F32, kind="ExternalOutput")

    R_xin, R_cin, R_xsA, R_xsB, R_csA, R_csB = (Res(n) for n in ("xin", "cin", "xsA", "xsB", "csA", "csB"))
    R_wupb, R_wdnb, R_kts, R_vts, R_out, R_const = (Res(n) for n in ("wupb", "wdnb", "kts", "vts", "out", "const"))

    ar = Arena(nc, 16512, 229344)
    RESC = {}
    es = ExitStack()

    def sb(name, shape, dt=F32, nres=1):
        t = ar.alloc(name, shape, dt)
        rs = [RESC.setdefault(f"{name}_{i}", Res(f"{name}_{i}")) for i in range(nres)]
        return t, (rs[0] if nres == 1 else rs)

    PS = []
    for i in range(8):
        PS.append((es.enter_context(nc.psum_tensor(f"ps{i}", [128, 512], F32)), Res(f"ps{i}")))

    VEC, R_VEC = sb("VEC", [128, 2, NV])
    SC, R_SC = sb("SC", [128, NCH, 2])
    SG0, R_SG0 = sb("SG0", [128, NCH, 2])
    MODS, R_MODS = sb("MODS", [128, 2, 48, 2])
    DER, R_DER = sb("DER", [128, 2, 2, 6, 8])
    FLG, R_FLG = sb("FLG", [128, 4])
    CORR, R_CORR = sb("CORR", [128, 2, 4, 16])
    ONES, R_ONES = sb("ONES", [128, 128], BF16)
    WP, R_WP = sb("WP", [128, 4, 2, 256], BF16)
    XT, R_XT = sb("XT", [128, NCH, 512])
    YS, R_YS = sb("YS", [128, NCH, 512], F32, nres=NCH)
    DB, R_DB = sb("DB", [128, NCH, 512], BF16, nres=NCH)
    RS = [sb(f"RS{i}", [128, 512]) for i in range(2)]
    SQ = [sb(f"SQ{i}", [128, 512], BF16) for i in range(3)]
    TMP = [sb(f"TMP{i}", [128, 512]) for i in range(3)]
    base_mark = ar.off

    rot = {}

    def nxt(lst, key):
        i = rot.get(key, 0)
        rot[key] = i + 1
        return lst[i % len(lst)]

    def der(layer, cnd, which):
        return lambda c: DER[:, layer, cnd, which, c:c + 1]

    S.dma("sp", lambda e: e.dma_start(out=VEC[:], in_=vecs[:]), [R_const], [R_VEC], R_VEC)
    S.dma("sp", lambda e: e.dma_start(out=SC[:], in_=cond[:]), [R_const], [R_SC], R_SC)
    S.dma("sp", lambda e: e.dma_start(out=FLG[:], in_=flg[:]), [R_const], [R_FLG], R_FLG)
    S.dma("sp", lambda e: e.dma_start(out=CORR[:], in_=corr[:]), [R_const], [R_CORR], R_CORR)
    S.dma("pool", lambda e: e.dma_start(out=WP[:], in_=wpool[:], max_dma_last_dim=4096), [R_const], [R_WP], R_WP)
    S.op("dve", lambda e: e.memset(ONES[:], 1.0), [], [R_ONES])
    LANES = [Res(f"lane{i}") for i in range(8)]
    for ln in LANES:
        ln.nobar = True
    R_wupc = [[Res(f"wupc{i}_{j}") for j in range(NJ)] for i in range(2)]
    R_wdnc = [[Res(f"wdnc{i}_{m}") for m in range(8)] for i in range(2)]
    kk = 0
    for i in range(2):
        for j in range(NJ):
            ln = LANES[kk % 8]
            S.dma("pool", lambda e, i=i, j=j: e.dma_start(out=wupb[i, j], in_=wup[i, j], max_dma_last_dim=4096), [R_const], [R_wupc[i][j], ln], ln)
            kk += 1
        for m in range(8):
            ln = LANES[kk % 8]
            S.dma("pool", lambda e, i=i, m=m: e.dma_start(out=wdnb[i, m], in_=wdn[i, m], max_dma_last_dim=4096), [R_const], [R_wdnc[i][m], ln], ln)
            kk += 1
    S.op("act", lambda e: e.activation(out=SG0[:], in_=SC[:], func=AF.Sigmoid), [R_SC], [R_SG0])
    S.op("dve", lambda e: e.tensor_tensor(out=SC[:], in0=SC[:], in1=SG0[:], op=ALU.mult), [R_SC, R_SG0], [R_SC])

    EPSB, R_EPSB = sb("EPSB", [128, 1])
    S.op("dve", lambda e: e.memset(EPSB[:], EPS), [], [R_EPSB])
    m1 = ar.off
    WA = [sb(f"WA{i}", [128, 8, 256]) for i in range(4)]
    PMB = {0: 0, 1: 3}
    wa_of = {}

    def mods_dma(layer, g):
        wa, R_wa = nxt(WA, "wa")
        wa_of[(layer, g)] = (wa, R_wa)
        S.dma("act" if layer == 1 else "sp", lambda e: e.dma_start(out=wa[:], in_=adaw[layer, g]), [R_const], [R_wa], R_wa)

    def mods_mm(layer, g):
        pm, R_pm = PS[PMB[layer]]
        wa, R_wa = wa_of[(layer, g)]
        for mm in range(2):
            m = g * 2 + mm
            for k in range(8):
                S.op("pe", lambda e, m=m, mm=mm, k=k: e.matmul(
                    pm[:, 2 * m:2 * m + 2], lhsT=wa[:, k, mm * 128:(mm + 1) * 128], rhs=SC[:, k, :],
                    start=(k == 0), stop=(k == 7)), [R_wa, R_SC], [R_pm], sig=True)

    def mods_finish(layer):
        pm, R_pm = PS[PMB[layer]]
        for cnd in range(2):
            S.op("dve", lambda e, pm=pm, layer=layer, cnd=cnd: e.tensor_tensor(
                out=MODS[:, layer, :, cnd], in0=pm[:, cnd:96:2], in1=VEC[:, layer, V_ADAB:V_ADAB + 48], op=ALU.add),
                [R_pm, R_VEC], [R_MODS])
            def mod(k0, layer=layer, cnd=cnd):
                return MODS[:, layer, k0:k0 + 8, cnd]
            for which, (sc0, gv) in ((0, (8, V_PRE)), (3, (32, V_FPRE))):
                S.op("dve", lambda e, which=which, sc0=sc0, gv=gv, layer=layer, cnd=cnd, mod=mod: e.scalar_tensor_tensor(
                    out=DER[:, layer, cnd, which, :], in0=mod(sc0), scalar=1.0, in1=VEC[:, layer, gv:gv + 8],
                    op0=ALU.add, op1=ALU.mult), [R_MODS, R_VEC], [R_DER])
            for which, sh0 in ((1, 0), (4, 24)):
                S.op("dve", lambda e, which=which, sh0=sh0, layer=layer, cnd=cnd, mod=mod: e.tensor_copy(
                    out=DER[:, layer, cnd, which, :], in_=mod(sh0)), [R_MODS], [R_DER])
            for which, (g0, gv) in ((2, (16, V_POST)), (5, (40, V_FPOST))):
                S.op("dve", lambda e, which=which, g0=g0, gv=gv, layer=layer, cnd=cnd, mod=mod: e.tensor_tensor(
                    out=DER[:, layer, cnd, which, :], in0=mod(g0), in1=VEC[:, layer, gv:gv + 8], op=ALU.mult),
                    [R_MODS, R_VEC], [R_DER])

    for g in range(3):
        mods_dma(0, g)
    for g in range(24):
        if g + 3 < 24:
            mods_dma(0, g + 3)
        mods_mm(0, g)
    mods_finish(0)
    mods_l1_dma = list(range(24))
    mods_l1_mm = []

    def rstd_from(ps_ss, R_ss, W):
        rs, R_rs = nxt(RS, "rs")
        S.op("act", lambda e: e.activation(out=rs[:, :W], in_=ps_ss[:, :W], func=AF.Sqrt, bias=EPSB[:, 0:1], scale=1.0 / D),
             [R_ss, R_EPSB], [R_rs])
        S.op("dve", lambda e: e.reciprocal(out=ps_ss[:, :W], in_=rs[:, :W]), [R_rs], [R_ss])
        return ps_ss, R_ss

    def sumsq(src_fn, reads, W, bank):
        ps, R_ps = PS[bank]
        for c in range(NCH):
            sq, R_sq = nxt(SQ, "sq")
            S.op("act", lambda e, c=c, sq=sq: e.activation(out=sq[:, :W], in_=src_fn(c), func=AF.Square), reads(c), [R_sq])
            S.op("pe", lambda e, c=c, sq=sq: e.matmul(ps[:, :W], lhsT=ONES[:], rhs=sq[:, :W], start=(c == 0), stop=(c == 7)),
                 [R_sq, R_ONES], [R_ps], sig=True)
        return ps, R_ps

    def load_x(src, R_src, a, W, xt=None):
        X_, R_X = xt or (XT, R_XT)
        S.dma("sp", lambda e: e.dma_start(out=X_[:, :, 0:W], in_=src[:, :, a:a + W]), [R_src], [R_X], R_X)

    def prenorm(W, A, SH, out_fn, out_res, xt=None):
        X_, R_X = xt or (XT, R_XT)
        ps, R_ps = sumsq(lambda c: X_[:, c, 0:W], lambda c: [R_X], W, 7)
        rs, R_rs = rstd_from(ps, R_ps, W)
        for c in range(NCH):
            tmp, R_tmp = nxt(TMP, "tmp")
            S.op("dve", lambda e, c=c, tmp=tmp: e.scalar_tensor_tensor(
                out=tmp[:, :W], in0=X_[:, c, 0:W], scalar=A(c), in1=rs[:, :W], op0=ALU.mult, op1=ALU.mult),
                [R_X, R_rs, R_DER], [R_tmp])
            S.op("act", lambda e, c=c, tmp=tmp: e.activation(out=out_fn(c), in_=tmp[:, :W], func=AF.Identity, bias=SH(c), scale=1.0),
                 [R_tmp, R_DER], [out_res(c)])

    def post_chunk_evac(m, ps, R_ps, n, scale, ys=None):
        Y_, R_Y = ys or (YS, R_YS)
        if scale is None:
            S.op("act", lambda e: e.activation(out=Y_[:, m, :n], in_=ps[:, :n], func=AF.Copy), [R_ps], [R_Y[m]])
        else:
            S.op("act", lambda e: e.activation(out=Y_[:, m, :n], in_=ps[:, :n], func=AF.Identity, scale=scale(m)),
                 [R_ps, R_VEC], [R_Y[m]])

    def post_finish(n, G, x_off, dst, R_dst, dcol, xt=None, ys=None):
        X_, R_X = xt or (XT, R_XT)
        Y_, R_Y = ys or (YS, R_YS)
        ps, R_ps = sumsq(lambda c: Y_[:, c, :n], lambda c: [R_Y[c]], n, 6)
        rs, R_rs = rstd_from(ps, R_ps, n)
        for c in range(NCH):
            tmp, R_tmp = nxt(TMP, "tmp")
            S.op("dve", lambda e, c=c, tmp=tmp: e.scalar_tensor_tensor(
                out=tmp[:, :n], in0=Y_[:, c, :n], scalar=G(c), in1=rs[:, :n], op0=ALU.mult, op1=ALU.mult),
                [R_Y[c], R_rs, R_DER], [R_tmp])
            S.op("pool", lambda e, c=c, tmp=tmp: e.tensor_tensor(
                out=X_[:, c, x_off:x_off + n], in0=X_[:, c, x_off:x_off + n], in1=tmp[:, :n], op=ALU.add),
                [R_X, R_tmp], [R_X])
        S.dma("pool", lambda e: e.dma_start(out=dst[:, :, dcol:dcol + n], in_=X_[:, :, x_off:x_off + n]), [R_X], [R_dst], R_X)

    def flag_cols(buf_fn, R_buf, a, W, lo, hi, fcol):
        l, h = max(lo, a), min(hi, a + W)
        if l >= h:
            return
        S.op("dve", lambda e: e.tensor_scalar(out=buf_fn(l - a, h - a), in0=buf_fn(l - a, h - a),
                                                scalar1=FLG[:, fcol:fcol + 1], scalar2=None, op0=ALU.mult),
             [R_buf, R_FLG], [R_buf])

    def pool_phase(src, R_src, dst, R_dst, ntok, top, bot, cnd, fcol, cidx, with_mods=False, bar=True):
        m0 = ar.off
        Us = [sb(f"U{i}", [128, NCH, 512], F32, nres=NCH) for i in range(2)]
        XTs = [(XT, R_XT), sb("XTb", [128, NCH, 512])]
        YSs = [(YS, R_YS), sb("YSb", [128, NCH, 512], F32, nres=NCH)]
        DBs = [(DB, R_DB), sb("DBb", [128, NCH, 512], BF16, nres=NCH)]
        T = [sb(f"T{i}", [128, 512]) for i in range(6)]
        lo_out, hi_out = 8, ntok - 8
        ntile = -(-(hi_out - lo_out) // 496)
        tsz = -(-(hi_out - lo_out) // ntile)

        def mods_hook(k):
            if with_mods:
                for _ in range(k):
                    if mods_l1_mm:
                        mods_mm(1, mods_l1_mm.pop(0))

        def do_tile(a, b, par):
            U, R_U = Us[par]
            xt = XTs[par]
            ys = YSs[par]
            DBp, R_DBp = DBs[par]
            n = b - a
            W = n + 16
            a0 = a - 8
            load_x(src, R_src, a0, W, xt=xt)
            prenorm(W, der(0, cnd, 0), der(0, cnd, 1), lambda c: U[:, c, 0:W], lambda c: R_U[c], xt=xt)
            mods_hook(2)
            for c in range(NCH):
                def ub(l, h, c=c):
                    return U[:, c, l:h]
                flag_cols(ub, R_U[c], a0, W, top - 8, top, fcol)
                flag_cols(ub, R_U[c], a0, W, bot, bot + 8, fcol + 1)
            for c in range(NCH):
                g = c // 2
                w = 2 << g
                t, R_t = nxt(T, "T")
                S.op("pool", lambda e, t=t, c=c: e.tensor_tensor(out=t[:, 1:W], in0=U[:, c, 1:W], in1=U[:, c, 0:W - 1], op=ALU.add),
                     [R_U[c]], [R_t])
                cur, R_cur = t, R_t
                vlo, vhi = 1, W
                sh = 1
                for lvl in range(g):
                    t2, R_t2 = nxt(T, "T")
                    nlo, nhi = vlo + sh, vhi - sh
                    S.op("pool", lambda e, t2=t2, cur=cur, nlo=nlo, nhi=nhi, sh=sh: e.tensor_tensor(
                        out=t2[:, nlo:nhi], in0=cur[:, nlo + sh:nhi + sh], in1=cur[:, nlo - sh:nhi - sh], op=ALU.add),
                        [R_cur], [R_t2])
                    cur, R_cur = t2, R_t2
                    vlo, vhi = nlo, nhi
                    sh *= 2
                assert vlo <= 8 and vhi >= W - 8, (vlo, vhi, W)
                for (l0, h0, off) in ((top, top + 8, 0), (bot - 8, bot, 8)):
                    l, h = max(l0, a), min(h0, b)
                    if l < h:
                        S.op("dve", lambda e, cur=cur, l=l, h=h, l0=l0, off=off, g=g: e.tensor_tensor(
                            out=cur[:, l - a0:h - a0], in0=cur[:, l - a0:h - a0],
                            in1=CORR[:, cidx, g, off + l - l0:off + h - l0], op=ALU.mult), [R_cur, R_CORR], [R_cur])
                S.op("dve", lambda e, cur=cur, c=c, w=w: e.scalar_tensor_tensor(
                    out=DBp[:, c, 0:n], in0=cur[:, 8:8 + n], scalar=1.0 / w, in1=U[:, c, 8:8 + n],
                    op0=ALU.mult, op1=ALU.subtract), [R_cur, R_U[c]], [R_DBp[c]])
            for m in range(NCH):
                g, ml = m // 2, m % 2
                ps, R_ps = PS[m % 2]
                for kc in range(2):
                    S.op("pe", lambda e, ps=ps, g=g, ml=ml, kc=kc: e.matmul(
                        ps[:, :n], lhsT=WP[:, g, kc, ml * 128:(ml + 1) * 128], rhs=DBp[:, 2 * g + kc, 0:n],
                        start=(kc == 0), stop=(kc == 1)), [R_WP, R_DBp[2 * g + kc]], [R_ps], sig=True)
                post_chunk_evac(m, ps, R_ps, n, lambda m: VEC[:, 0, V_PSC + m:V_PSC + m + 1], ys=ys)
            post_finish(n, der(0, cnd, 2), 8, dst, R_dst, a, xt=xt, ys=ys)
            mods_hook(2)

        a = lo_out
        ti = 0
        while a < hi_out:
            b = min(a + tsz, hi_out)
            do_tile(a, b, ti % 2)
            if with_mods:
                while mods_l1_mm:
                    mods_mm(1, mods_l1_mm.pop(0))
                for _ in range(4):
                    if mods_l1_dma:
                        g = mods_l1_dma.pop(0)
                        mods_dma(1, g)
                        mods_l1_mm.append(g)
            a = b
            ti += 1
        if with_mods:
            while mods_l1_mm or mods_l1_dma:
                while mods_l1_mm:
                    mods_mm(1, mods_l1_mm.pop(0))
                for _ in range(4):
                    if mods_l1_dma:
                        g = mods_l1_dma.pop(0)
                        mods_dma(1, g)
                        mods_l1_mm.append(g)
            mods_finish(1)
        if bar:
            S.barrier()
        ar.off = m0

    def ffn_phase(layer, src, R_src, dst, R_dst, lo_out, hi_out, top, bot, cnd, fcol, dst_shift, bar=True):
        m0 = ar.off
        NST = 2
        UBs = [sb(f"UB{i}", [128, NCH, NST * 512], BF16, nres=NST) for i in range(2)]
        G, R_G = sb("G", [128, NJ, NST * 448], BF16, nres=NST)
        WU = [sb(f"WU{i}", [128, 2, 8, 128], BF16) for i in range(2)]
        WD = [sb(f"WD{i}", [128, NJ, 128], BF16) for i in range(2)]
        FT = [sb(f"FT{i}", [128, 512]) for i in range(12)]
        XTn = sb("XTn", [128, NCH, 512])
        tot = hi_out - lo_out
        ntile = -(-tot // 448)
        tsz = -(-tot // ntile)
        tiles = []
        a = lo_out
        while a < hi_out:
            b = min(a + tsz, hi_out)
            tiles.append((a, b))
            a = b
        sts = [tiles[i:i + NST] for i in range(0, len(tiles), NST)]

        def norm_tile(si, ti):
            a, b = sts[si][ti]
            UB, R_UB = UBs[si % 2]
            n = b - a
            W = n + 2
            load_x(src, R_src, a - 1, W, xt=XTn)
            prenorm(W, der(layer, cnd, 3), der(layer, cnd, 4),
                    lambda c: UB[:, c, ti * 512:ti * 512 + W], lambda c: R_UB[ti], xt=XTn)

            def ubf(l, h):
                return UB[:, :, ti * 512 + l:ti * 512 + h]
            flag_cols(ubf, R_UB[ti], a - 1, W, top - 1, top, fcol)
            flag_cols(ubf, R_UB[ti], a - 1, W, bot, bot + 1, fcol + 1)

        pend = []

        def stage3(ag, R_ag, av, R_av, n, j, ti):
            p, R_p = nxt(FT, "ft")
            S.op("act", lambda e: e.activation(out=p[:, :n], in_=ag[:, :n], func=AF.Gelu_apprx_tanh), [R_ag], [R_p])
            S.op("pool", lambda e: e.tensor_tensor(out=G[:, j, ti * 448:ti * 448 + n], in0=p[:, :n], in1=av[:, :n], op=ALU.mult),
                 [R_p, R_av], [R_G[ti]])

        def stage12(si, j, wu, R_wu, ti):
            a, b = sts[si][ti]
            UB, R_UB = UBs[si % 2]
            n = b - a
            W = n + 2
            hp = []
            for part in range(2):
                ps, R_ps = PS[nxt([0, 1, 2, 3, 4, 5], "ffnps")]
                for k in range(8):
                    S.op("pe", lambda e, ps=ps, part=part, k=k: e.matmul(
                        ps[:, :W], lhsT=wu[:, part, k, :], rhs=UB[:, k, ti * 512:ti * 512 + W],
                        start=(k == 0), stop=(k == 7)), [R_wu, R_UB[ti]], [R_ps], sig=True)
                hp.append((ps, R_ps))
            conv = []
            for part in range(2):
                ps, R_ps = hp[part]
                ch = part * NJ + j
                acc, R_acc = nxt(FT, "ft")
                S.op("act", lambda e, ps=ps, acc=acc, ch=ch: e.activation(
                    out=acc[:, :n], in_=ps[:, 1:n + 1], func=AF.Identity,
                    bias=VEC[:, layer, V_CB + ch:V_CB + ch + 1], scale=VEC[:, layer, V_CW + 44 + ch:V_CW + 44 + ch + 1]),
                    [R_ps, R_VEC], [R_acc])
                for tap, off in ((0, 0), (2, 2)):
                    S.op("dve", lambda e, ps=ps, acc=acc, ch=ch, tap=tap, off=off: e.scalar_tensor_tensor(
                        out=acc[:, :n], in0=ps[:, off:off + n], scalar=VEC[:, layer, V_CW + tap * 44 + ch:V_CW + tap * 44 + ch + 1],
                        in1=acc[:, :n], op0=ALU.mult, op1=ALU.add), [R_ps, R_acc, R_VEC], [R_acc])
                conv.append((acc, R_acc))
            (ag, R_ag), (av, R_av) = conv
            pend.append((ag, R_ag, av, R_av, n, j, ti))
            if len(pend) > 1:
                stage3(*pend.pop(0))

        def down_tile(si, ti):
            a, b = sts[si][ti]
            n = b - a
            for m in range(NCH):
                wd, R_wd = nxt(WD, "wd")
                S.dma("sp", lambda e, wd=wd, m=m: e.dma_start(out=wd[:], in_=wdnb[layer, m]), [R_wdnc[layer][m]], [R_wd], R_wd)
                ps, R_ps = PS[m % 4]
                for j in range(NJ):
                    S.op("pe", lambda e, ps=ps, wd=wd, j=j: e.matmul(
                        ps[:, :n], lhsT=wd[:, j, :], rhs=G[:, j, ti * 448:ti * 448 + n],
                        start=(j == 0), stop=(j == NJ - 1)), [R_wd, R_G[ti]], [R_ps], sig=True)
                post_chunk_evac(m, ps, R_ps, n, None)
            load_x(src, R_src, a, n)
            if si + 1 < len(sts) and ti < len(sts[si + 1]):
                norm_tile(si + 1, ti)
            post_finish(n, der(layer, cnd, 5), 0, dst, R_dst, a - dst_shift)

        for ti in range(len(sts[0])):
            norm_tile(0, ti)
        for si, st in enumerate(sts):
            for j in range(NJ):
                wu, R_wu = nxt(WU, "wu")
                S.dma("sp", lambda e, wu=wu, j=j: e.dma_start(out=wu[:], in_=wupb[layer, j]), [R_wupc[layer][j]], [R_wu], R_wu)
                for ti in range(len(st)):
                    stage12(si, j, wu, R_wu, ti)
            while pend:
                stage3(*pend.pop(0))
            nxt_n = len(sts[si + 1]) if si + 1 < len(sts) else 0
            for ti in range(max(len(st), nxt_n)):
                if ti < len(st):
                    down_tile(si, ti)
                elif ti < nxt_n:
                    norm_tile(si + 1, ti)
        if bar:
            S.barrier()
        ar.off = m0

    if stop_after >= 1:
        pool_phase(xin, R_xin, xsA, R_xsA, NT, TOP, BOT, 0, 0, 0, with_mods=True, bar=False)
        pool_phase(cin, R_cin, csA, R_csA, NTC, TOPC, BOTC, 1, 2, 1)
    ar.off = m1
    if stop_after >= 2:
        ffn_phase(0, xsA, R_xsA, xsB, R_xsB, PAD, NT - PAD, TOP, BOT, 0, 0, 0, bar=False)
        ffn_phase(0, csA, R_csA, csB, R_csB, TOPC, BOTC, TOPC, BOTC, 1, 2, 0)

    def kv_phase():
        m0 = ar.off
        WK, R_WK = sb("WK", [128, 8, 8, 128], BF16)
        WV, R_WV = sb("WV", [128, 8, 1024], BF16)
        KOs = [sb(f"KO{i}", [128, NCH, 512], BF16) for i in range(2)]
        VOs = [sb(f"VO{i}", [128, 4, 1024], BF16) for i in range(2)]
        XTs = [(XT, R_XT), sb("XTk", [128, NCH, 512])]
        DBs = [(DB, R_DB), sb("DBk", [128, NCH, 512], BF16, nres=NCH)]
        S.dma("pool", lambda e: e.dma_start(out=WK[:], in_=wk[:], max_dma_last_dim=4096), [R_const], [R_WK], R_WK)
        S.dma("pool", lambda e: e.dma_start(out=WV[:], in_=wv[:], max_dma_last_dim=4096), [R_const], [R_WV], R_WV)

        def kv_tile(src, R_src, a, n, cnd, par):
            xt = XTs[par]
            DBp, R_DBp = DBs[par]
            KO, R_KO = KOs[par]
            VO, R_VO = VOs[par]
            load_x(src, R_src, a, n, xt=xt)
            prenorm(n, der(1, cnd, 0), der(1, cnd, 1), lambda c: DBp[:, c, 0:n], lambda c: R_DBp[c], xt=xt)
            for m in range(NCH):
                ps, R_ps = PS[m % 2]
                for k in range(8):
                    S.op("pe", lambda e, ps=ps, m=m, k=k: e.matmul(ps[:, :n], lhsT=WK[:, m, k, :], rhs=DBp[:, k, 0:n],
                                                                  start=(k == 0), stop=(k == 7)),
                         [R_WK, R_DBp[k]], [R_ps], sig=True)
                S.op("act", lambda e, ps=ps, m=m: e.activation(out=KO[:, m, :n], in_=ps[:, :n], func=AF.Copy), [R_ps], [R_KO])
            for sub in range(n // 128):
                for half in range(2):
                    ps, R_ps = PS[2 + nxt([0, 1, 2, 3], "kvps")]
                    for k in range(8):
                        S.op("pe", lambda e, ps=ps, k=k, sub=sub, half=half: e.matmul(
                            ps[:, :], lhsT=DBp[:, k, sub * 128:(sub + 1) * 128], rhs=WV[:, k, half * 512:(half + 1) * 512],
                            start=(k == 0), stop=(k == 7)), [R_WV, R_DBp[k]], [R_ps], sig=True)
                    if half == 0:
                        S.op("dve", lambda e, ps=ps, sub=sub, half=half: e.tensor_copy(out=VO[:, sub, half * 512:(half + 1) * 512], in_=ps[:, :]),
                             [R_ps], [R_VO])
                    else:
                        S.op("act", lambda e, ps=ps, sub=sub, half=half: e.activation(out=VO[:, sub, half * 512:(half + 1) * 512], in_=ps[:, :], func=AF.Copy),
                             [R_ps], [R_VO])
            if cnd == 0:
                tk = a - PAD
                S.dma("pool", lambda e: e.dma_start(out=kts[:, :, tk:tk + n], in_=KO[:, :, 0:n]), [R_KO], [R_kts], R_KO)
                S.dma("pool", lambda e: e.dma_start(out=vts[:, tk // 128:tk // 128 + n // 128, :], in_=VO[:, 0:n // 128, :]),
                      [R_VO], [R_vts], R_VO)
            else:
                S.op("act", lambda e: e.activation(out=KCT[:, :, 0:n], in_=KO[:, :, 0:n], func=AF.Copy), [R_KO], [R_KCT])
                S.op("dve", lambda e: e.tensor_copy(out=VC[:, 0:n // 128, :], in_=VO[:, 0:n // 128, :]), [R_VO], [R_VC])

        ti = 0
        for (src, R_src, t0, tend, cnd) in ((xsB, R_xsB, PAD, NT - PAD, 0), (csB, R_csB, TOPC, BOTC, 1)):
            a = t0
            while a < tend:
                n = min(512, tend - a)
                kv_tile(src, R_src, a, n, cnd, ti % 2)
                ti += 1
                a += n
        S.barrier()
        ar.off = m0

    if stop_after >= 3:
        KCT, R_KCT = sb("KCT", [128, NCH, 256], BF16)
        VC, R_VC = sb("VC", [128, 2, 1024], BF16)
        kv_phase()

    def attn_phase():
        m0 = ar.off
        WQ, R_WQ = sb("WQ", [128, 8, 8, 128], BF16)
        WO, R_WO = sb("WO", [128, 8, 8, 128], BF16)
        KT, R_KT = sb("KT", [128, NCH, 1024], BF16)
        VT, R_VT = sb("VT", [128, 8, 1024], BF16)
        QT, R_QT = sb("QT", [128, 2, NCH, 512], BF16)
        OT, R_OT = sb("OT", [128, NCH, 512], BF16, nres=NCH)
        TB = [sb(f"TB{i}", [128, TABMAX]) for i in range(2)]
        SBF = [sb(f"SBF{i}", [128, 512]) for i in range(3)]
        EB = [sb(f"EB{i}", [128, 512], BF16) for i in range(4)]
        RD, R_RD = sb("RD", [128, 512])
        S.dma("pool", lambda e: e.dma_start(out=WQ[:], in_=wq[:], max_dma_last_dim=4096), [R_const], [R_WQ], R_WQ)
        S.op("pool", lambda e: e.memset(QT[:], 0.0), [], [R_QT])
        S.dma("pool", lambda e: e.dma_start(out=WO[:], in_=wo[:], max_dma_last_dim=4096), [R_const], [R_WO], R_WO)
        XTq = [(XT, R_XT), sb("XTq", [128, NCH, 512])]
        gcount = [0]

        def do_group(q0, nr, chunks, typ):
            nq = nr * 64
            t0 = PAD + q0 * 64
            c0 = chunks[0]
            nck = len(chunks)
            xt = XTq[gcount[0] % 2]
            gcount[0] += 1
            load_x(xsB, R_xsB, t0, nq, xt=xt)
            S.dma("sp", lambda e, c0=c0, nck=nck: e.dma_start(out=KT[:, :, 0:nck * 128], in_=kts[:, :, c0 * 128:(c0 + nck) * 128]),
                  [R_kts], [R_KT], R_KT)
            S.dma("sp", lambda e, c0=c0, nck=nck: e.dma_start(out=VT[:, 0:nck, :], in_=vts[:, c0:c0 + nck, :]), [R_vts], [R_VT], R_VT)
            prenorm(nq, der(1, 0, 0), der(1, 0, 1), lambda c: DB[:, c, 0:nq], lambda c: R_DB[c], xt=xt)
            for m in range(NCH):
                ps, R_ps = PS[m % 2]
                for k in range(8):
                    S.op("pe", lambda e, ps=ps, m=m, k=k: e.matmul(ps[:, :nq], lhsT=WQ[:, m, k, :], rhs=DB[:, k, 0:nq],
                                                                  start=(k == 0), stop=(k == 7)), [R_WQ, R_DB[k]], [R_ps], sig=True)
                S.op("act", lambda e, ps=ps, m=m: e.activation(out=QT[0:64, 0, m, :nq], in_=ps[0:64, :nq], func=AF.Copy, scale=0.125), [R_ps], [R_QT])
                S.op("act", lambda e, ps=ps, m=m: e.activation(out=QT[64:128, 1, m, :nq], in_=ps[64:128, :nq], func=AF.Copy, scale=0.125), [R_ps], [R_QT])
            colr = COLR[typ]
            items = []
            tab_loaders = {}

            def make_head(h):
                m, pb = h // 2, (h % 2) * 64
                po, R_po = PS[4 + (h % 2)]
                pd, R_pd = PS[6 + (h % 2)]
                hs = {}
                nit = 2 + len(chunks)

                def load_tab():
                    if "tb" in hs:
                        return
                    tb, R_tb = nxt(TB, "tb")
                    hs["tb"] = (tb, R_tb)
                    S.dma("sp", lambda e: e.dma_start(out=tb[:, 0:TABW[typ]], in_=(tabL[typ - 1, h, :, 0:TABW[typ]] if typ in (1, 2, 3)
                                                                                  else tabS[typ // 4, h, :, 0:TABW[typ]])),
                          [R_const], [R_tb], R_tb)

                tab_loaders[h] = load_tab

                def end_fn():
                    S.op("act", lambda e: e.activation(out=RD[pb:pb + 64, :nq], in_=pd[pb:pb + 64, :nq], func=AF.Ln), [R_pd], [R_RD])
                    S.op("act", lambda e: e.activation(out=RD[pb:pb + 64, :nq], in_=RD[pb:pb + 64, :nq], func=AF.Exp, scale=-1.0), [R_RD], [R_RD])
                    S.op("dve", lambda e: e.tensor_tensor(out=OT[pb:pb + 64, m, :nq], in0=po[pb:pb + 64, :nq],
                                                          in1=RD[pb:pb + 64, :nq], op=ALU.mult), [R_po, R_RD], [R_OT[m]])

                def mk_item(idx, kind, j, off, clo, chi):
                    d = {}
                    ncj = chi - clo
                    first, last = (idx == 0), (idx == nit - 1)

                    def s_fn():
                        ps, R_ps = PS[nxt([0, 1, 2, 3], "aps")]
                        d["ps"] = (ps, R_ps)
                        if kind == "ctx":
                            S.op("pe", lambda e: e.matmul(ps[:, :ncj], lhsT=KCT[:, m, j * 128:(j + 1) * 128],
                                                          rhs=QT[:, h % 2, m, clo:chi], start=True, stop=True), [R_KCT, R_QT], [R_ps])
                        else:
                            S.op("pe", lambda e: e.matmul(ps[:, :ncj], lhsT=KT[:, m, j * 128:(j + 1) * 128],
                                                          rhs=QT[:, h % 2, m, clo:chi], start=True, stop=True), [R_KT, R_QT], [R_ps])

                    def pv_fn():
                        ps, R_ps = d["ps"]
                        eb, R_eb = nxt(EB, "eb")
                        if kind == "ctx":
                            S.op("act", lambda e: e.activation(out=eb[:, :ncj], in_=ps[:, :ncj], func=AF.Exp), [R_ps], [R_eb])
                            vsrc, R_vsrc = VC, R_VC
                        else:
                            tb, R_tb = hs["tb"]
                            sbf, R_sbf = nxt(SBF, "sbf")
                            S.op("dve", lambda e: e.tensor_tensor(out=sbf[:, :ncj], in0=ps[:, :ncj], in1=tb[:, off:off + ncj], op=ALU.add),
                                 [R_ps, R_tb], [R_sbf])
                            S.op("act", lambda e: e.activation(out=eb[:, :ncj], in_=sbf[:, :ncj], func=AF.Exp), [R_sbf], [R_eb])
                            vsrc, R_vsrc = VT, R_VT
                        S.op("pe", lambda e: e.matmul(po[:, clo:chi], lhsT=vsrc[:, j, m * 128:(m + 1) * 128], rhs=eb[:, :ncj],
                                                      start=first, stop=last), [R_vsrc, R_eb], [R_po], sig=True)
                        S.op("pe", lambda e: e.matmul(pd[:, clo:chi], lhsT=ONES[:, :], rhs=eb[:, :ncj],
                                                      start=first, stop=last), [R_ONES, R_eb], [R_pd], sig=True)
                    items.append((s_fn, pv_fn, (lambda: tab_loaders[h + 1]()) if (first and h + 1 < 16) else None, end_fn if last else None))

                idx = 0
                for cc in range(2):
                    mk_item(idx, "ctx", cc, 0, 0, nq)
                    idx += 1
                off = 0
                for j in range(len(chunks)):
                    lo, hi = colr[j]
                    mk_item(idx, "lat", j, off, lo * 64, hi * 64)
                    off += (hi - lo) * 64
                    idx += 1

            for h in range(16):
                make_head(h)
            tab_loaders[0]()
            LA = 3
            deferred = []
            for i in range(min(LA, len(items))):
                items[i][0]()
            for i in range(len(items)):
                if i + LA < len(items):
                    items[i + LA][0]()
                if items[i][2]:
                    items[i][2]()
                items[i][1]()
                if items[i][3]:
                    deferred.append((i + 2, items[i][3]))
                while deferred and deferred[0][0] <= i:
                    deferred.pop(0)[1]()
            while deferred:
                deferred.pop(0)[1]()
            for m in range(NCH):
                ps, R_ps = PS[m % 2]
                for k in range(8):
                    S.op("pe", lambda e, ps=ps, m=m, k=k: e.matmul(ps[:, :nq], lhsT=WO[:, m, k, :], rhs=OT[:, k, 0:nq],
                                                                  start=(k == 0), stop=(k == 7)), [R_WO, R_OT[k]], [R_ps], sig=True)
                post_chunk_evac(m, ps, R_ps, nq, None)
            post_finish(nq, der(1, 0, 2), 0, xsA, R_xsA, t0, xt=xt)

        for (q0, nr, chunks, typ) in AGROUPS:
            do_group(q0, nr, chunks, typ)
        S.barrier()
        ar.off = m0

    if stop_after >= 4:
        attn_phase()
    if stop_after >= 5:
        ffn_phase(1, xsA, R_xsA, outT, R_out, TOP, BOT, TOP, BOT, 0, 0, TOP)

    if dbg:
        srcx = xsA if stop_after in (1, 4) else xsB
        R_srcx = R_xsA if stop_after in (1, 4) else R_xsB
        srcc = csA if stop_after == 1 else csB
        R_srcc = R_csA if stop_after == 1 else R_csB
        R_d1, R_d2 = Res("dbg1"), Res("dbg2")
        S.dma("pool", lambda e: e.dma_start(out=dbgx[:], in_=srcx[:]), [R_srcx], [R_out], R_d1)
        S.dma("pool", lambda e: e.dma_start(out=dbgc[:], in_=srcc[:]), [R_srcc], [R_out], R_d2)
    S.barrier(final=True)
    S.emit(nc, es)
    es.close()
    return nc


def _fm(a2d):
    T = a2d.shape[0]
    return np.ascontiguousarray(a2d.reshape(T, NCH, 128).transpose(2, 1, 0))


def _vec_cols(v):
    return np.ascontiguousarray(v.reshape(-1, 128).T)


def _bias_tables(rpb, quarter):
    r0 = quarter * 32
    tabs = np.full((5, 16, 128, TABMAX), MASK, np.float32)
    qcol = np.arange(64)
    kcol = np.arange(64)
    ws = np.clip(qcol - 8, 0, 48)
    cvalid = (kcol[:, None] >= ws[None, :]) & (kcol[:, None] < ws[None, :] + 16)
    dc = np.clip(kcol[:, None] - qcol[None, :] + 15, 0, 30)
    done = set()
    for (q0, nr, chunks, typ) in AGROUPS:
        if typ in done:
            continue
        done.add(typ)
        off = 0
        for j, c in enumerate(chunks):
            lo, hi = COLR[typ][j]
            for qr in range(lo, hi):
                gq = r0 - 5 + q0 + qr
                for kr2 in range(2):
                    gk = r0 - 5 + 2 * c + kr2
                    ok = (0 <= gq < 128) and (0 <= gk < 128)
                    if ok:
                        wlo, whi = _row_window(gq)
                        ok = wlo <= gk < whi
                    if ok:
                        dr = gk - gq + 7
                        blk = np.where(cvalid[None], rpb[:, dr][:, dc], np.float32(MASK))
                        tabs[typ, :, kr2 * 64:(kr2 + 1) * 64, off + (qr - lo) * 64: off + (qr - lo + 1) * 64] = blk
            off += (hi - lo) * 64
    return tabs


_CACHE = {}


def kernel(x, c, ctx, c_ctx, ada_w, ada_b, mix_pre_g, mix_post_g, ffn_pre_g, ffn_post_g,
           pool_w, pool_scale, na_w_qkv, na_w_o, na_rpb, ffn_w_up, ffn_conv_w, ffn_conv_b, ffn_w_down,
           _stop_after=99, _dbg=False):
    f = lambda a: np.asarray(a, dtype=np.float32)
    x, c, ctx, c_ctx = f(x), f(c), f(ctx), f(c_ctx)
    ada_w, ada_b = f(ada_w), f(ada_b)
    key = (_stop_after, _dbg)
    if key not in _CACHE:
        _CACHE[key] = build_program(_stop_after, _dbg)
    nc = _CACHE[key]

    vecs = np.zeros((128, 2, NV), np.float32)
    for i in range(2):
        vecs[:, i, V_PRE:V_PRE + 8] = _vec_cols(f(mix_pre_g)[i])
        vecs[:, i, V_POST:V_POST + 8] = _vec_cols(f(mix_post_g)[i])
        vecs[:, i, V_FPRE:V_FPRE + 8] = _vec_cols(f(ffn_pre_g)[i])
        vecs[:, i, V_FPOST:V_FPOST + 8] = _vec_cols(f(ffn_post_g)[i])
        vecs[:, i, V_PSC:V_PSC + 8] = _vec_cols(f(pool_scale)[0])
        vecs[:, i, V_ADAB:V_ADAB + 48] = _vec_cols(ada_b[i])
        for tap in range(3):
            vecs[:, i, V_CW + tap * 44:V_CW + (tap + 1) * 44] = _vec_cols(f(ffn_conv_w)[i, tap])
        vecs[:, i, V_CB:V_CB + 44] = _vec_cols(f(ffn_conv_b)[i])
    adaw = np.ascontiguousarray(ada_w.reshape(2, 8, 128, 24, 256).transpose(0, 3, 2, 1, 4))
    wpool = np.ascontiguousarray(f(pool_w)[0].reshape(4, 2, 128, 256).transpose(2, 0, 1, 3))
    wup = np.ascontiguousarray(f(ffn_w_up).reshape(2, 8, 128, 2, NJ, 128).transpose(0, 4, 2, 3, 1, 5))
    wdn = np.ascontiguousarray(f(ffn_w_down).reshape(2, NJ, 128, 8, 128).transpose(0, 3, 2, 1, 4))
    wqkv = f(na_w_qkv)[0].reshape(8, 128, 3, 8, 128)
    wq = np.ascontiguousarray(wqkv[:, :, 0].transpose(1, 2, 0, 3))
    wk = np.ascontiguousarray(wqkv[:, :, 1].transpose(1, 2, 0, 3))
    wv = np.ascontiguousarray(wqkv[:, :, 2].reshape(8, 128, 1024).transpose(1, 0, 2))
    wo = np.ascontiguousarray(f(na_w_o)[0].reshape(8, 128, 8, 128).transpose(1, 2, 0, 3))
    rpb = f(na_rpb)[0]

    def corr_tab(top_on, bot_on):
        t = np.ones((4, 16), np.float32)
        for g in range(4):
            w = 2 << g
            for i in range(8):
                if top_on:
                    t[g, i] = w / (w // 2 + min(i, w // 2))
                if bot_on:
                    ip = 7 - i
                    t[g, 8 + i] = w / (w // 2 + min(ip + 1, w // 2))
        return t

    in_maps = []
    tabs_cache = {}
    for core in range(8):
        b, q = core // 4, core % 4
        r0 = q * 32
        xe = np.zeros((NT, D), np.float32)
        g_lo = (r0 - 5) * 64 - PAD
        lo, hi = max(g_lo, 0), min(g_lo + NT, 8192)
        xe[lo - g_lo:hi - g_lo] = x[b, lo:hi]
        ce = np.zeros((NTC, D), np.float32)
        ce[PAD:PAD + CTX] = ctx[b]
        cnd = np.stack([_vec_cols(c[b]), _vec_cols(c_ctx)], axis=-1)
        fl = np.zeros((128, 4), np.float32)
        fl[:, 0] = 0.0 if q == 0 else 1.0
        fl[:, 1] = 0.0 if q == 3 else 1.0
        cr = np.stack([corr_tab(q == 0, q == 3), corr_tab(True, True)], axis=0)
        cr = np.ascontiguousarray(np.broadcast_to(cr[None], (128, 2, 4, 16)))
        if q not in tabs_cache:
            tabs_cache[q] = _bias_tables(rpb, q)
        in_maps.append(dict(xin=_fm(xe), cin=_fm(ce), cond=np.ascontiguousarray(cnd), vecs=vecs, flg=fl, corr=cr,
                            adaw=adaw, wpool=wpool, wup=wup, wdn=wdn, wq=wq, wk=wk, wv=wv, wo=wo,
                            tabL=np.ascontiguousarray(tabs_cache[q][1:4]),
                            tabS=np.ascontiguousarray(tabs_cache[q][[0, 4]][..., :TABSMALL])))
    res = run_bass_kernel_spmd(nc, in_maps, core_ids=list(range(8)))
    if _dbg:
        return res.results
    out = np.zeros((2, 8192, D), np.float32)
    for core in range(8):
        b, q = core // 4, core % 4
        o = res.results[core]["outT"]
        out[b, q * 2048:(q + 1) * 2048] = o.transpose(2, 1, 0).reshape(2048, D)
    return out
```

```python
import numpy as np
from contextlib import ExitStack
import concourse.bass as bass
import concourse.mybir as mybir
from concourse.bass_utils import run_bass_kernel_spmd

F32 = mybir.dt.float32
BF16 = mybir.dt.bfloat16
AF = mybir.ActivationFunctionType
ALU = mybir.AluOpType

D = 1024
NCH = 8
GRID_W = 64
PAD = 16
ROWS_EXT = 42
NT = PAD + ROWS_EXT * 64 + PAD
TOP = PAD + 5 * 64
BOT = TOP + 2048
CTX = 256
NTC = PAD + CTX + PAD
TOPC = PAD
BOTC = PAD + CTX
DFF = 2816
NJ = 22
EPS = 1e-6
MASK = -30000.0
NV = 8 * 5 + 48 + 3 * 44 + 44
V_PRE, V_POST, V_FPRE, V_FPOST, V_PSC, V_ADAB, V_CW, V_CB = 0, 8, 16, 24, 32, 40, 88, 220

AGROUPS = [
    (4, 1, [0, 1, 2, 3], 0),
    (5, 8, list(range(0, 8)), 1),
    (13, 8, list(range(4, 12)), 2),
    (21, 8, list(range(8, 16)), 2),
    (29, 8, list(range(12, 20)), 3),
    (37, 1, [16, 17, 18, 19, 20], 4),
]


def _row_window(gq):
    s0 = min(max(gq - 4, 0), 120)
    return s0, s0 + 8


def _group_colranges():
    out = {}
    for (q0, nr, chunks, typ) in AGROUPS:
        if typ in out:
            continue
        rngs = []
        for c in chunks:
            rows = set()
            for r0 in (0, 32, 96):
                for qr in range(nr):
                    gq = r0 - 5 + q0 + qr
                    if gq < 0 or gq > 127:
                        continue
                    lo, hi = _row_window(gq)
                    for kr2 in range(2):
                        gk = r0 - 5 + 2 * c + kr2
                        if lo <= gk < hi:
                            rows.add(qr)
            if not rows:
                rows = {0}
            rngs.append((min(rows), max(rows) + 1))
        out[typ] = rngs
    return out


COLR = _group_colranges()
TABW = {t: sum((hi - lo) * 64 for lo, hi in COLR[t]) for t in COLR}
TABMAX = max(TABW.values())
TABSMALL = max(TABW[0], TABW[4])


class Res:
    __slots__ = ("name", "w", "r", "cnt", "sem", "nobar")

    def __init__(self, name):
        self.name = name
        self.w = {}
        self.r = {}
        self.cnt = 0
        self.sem = None
        self.nobar = False


class Sched:
    ENG = ("pe", "act", "dve", "pool", "sp")

    def __init__(self):
        self.streams = {e: [] for e in self.ENG}
        self.cnt = {e: 0 for e in self.ENG}
        self.known = {e: {} for e in self.ENG}
        self.slots = []
        self.sw = {}

    def _need(self, eng, toks, waits, raw):
        for key, val in toks.items():
            if isinstance(key, str):
                if key == eng:
                    if eng == "pe" or not raw:
                        continue
            else:
                val = key.cnt
            if self.known[eng].get(key, 0) >= val:
                continue
            if waits.get(key, 0) < val:
                waits[key] = val

    def _deps(self, eng, reads, writes):
        waits = {}
        for r in reads:
            self._need(eng, r.w, waits, True)
        for w in writes:
            self._need(eng, w.w, waits, False)
            self._need(eng, w.r, waits, False)
        for k, v in waits.items():
            self.known[eng][k] = v
        return list(waits.items())

    def _commit(self, key, val, reads, writes):
        for r in reads:
            if r.r.get(key, 0) < val:
                r.r[key] = val
        for w in writes:
            w.w = {key: val}
            w.r = {}

    def op(self, eng, fn, reads=(), writes=(), sig=True):
        waits = self._deps(eng, reads, writes)
        if sig:
            self.cnt[eng] += 1
            val = self.cnt[eng]
        else:
            val = self.cnt[eng] + 1
        self.streams[eng].append((waits, fn, ("eng", eng) if sig else None))
        self._commit(eng, val, reads, writes)

    def dma(self, q, fn, reads, writes, slot):
        if q == "pool":
            key = id(slot)
            if key not in self.sw:
                r = Res(slot.name + "_sw")
                r.nobar = slot.nobar
                self.sw[key] = r
            slot = self.sw[key]
        waits = self._deps(q, reads, writes)
        if slot.cnt == 0:
            self.slots.append(slot)
        slot.cnt += 16
        self.streams[q].append((waits, fn, ("dma", slot)))
        self._commit(slot, slot.cnt, reads, writes)

    def barrier(self, final=False):
        for e in self.ENG:
            waits = {}
            for o in self.ENG[:4]:
                if o != e and self.known[e].get(o, 0) < self.cnt[o]:
                    waits[o] = self.cnt[o]
            for s in self.slots:
                if s.nobar and not final:
                    continue
                if self.known[e].get(s, 0) < s.cnt:
                    waits[s] = s.cnt
            for k, v in waits.items():
                self.known[e][k] = v
            if waits:
                self.streams[e].append((list(waits.items()), None, None))

    def emit(self, nc, es):
        sems = {}
        for e in self.ENG[:4]:
            sems[e] = es.enter_context(nc.semaphore("sem_" + e))
        for s in self.slots:
            s.sem = es.enter_context(nc.semaphore("sd_" + s.name))
        block = es.enter_context(nc.Block())

        def run(engname):
            def body(eng):
                for waits, fn, inc in self.streams[engname]:
                    for key, val in waits:
                        eng.wait_ge(sems[key] if isinstance(key, str) else key.sem, val)
                    if fn is None:
                        continue
                    inst = fn(eng)
                    if inc is not None:
                        if inc[0] == "eng":
                            inst.then_inc(sems[inc[1]], 1)
                        else:
                            inst.then_inc(inc[1].sem, 16)
            return body

        block.tensor(run("pe"))
        block.scalar(run("act"))
        block.vector(run("dve"))
        block.gpsimd(run("pool"))
        block.sync(run("sp"))


class Arena:
    def __init__(self, nc, base, top):
        self.nc = nc
        self.off = base
        self.top = top
        self.n = 0

    def alloc(self, name, shape, dtype):
        per = 1
        for s in shape[1:]:
            per *= s
        nbytes = per * (2 if dtype == BF16 else 4)
        off = (self.off + 63) // 64 * 64
        assert off + nbytes <= self.top, f"SBUF overflow at {name}: {off + nbytes} > {self.top}"
        self.off = off + nbytes
        self.hw = max(getattr(self, 'hw', 0), self.off)
        self.hwlog = getattr(self, 'hwlog', {})
        self.hwlog[name] = self.off
        self.n += 1
        return self.nc.alloc_sbuf_tensor_at(f"{name}_{self.n}", list(shape), dtype, offset=off)


def build_program(stop_after=99, dbg=False):
    nc = bass.Bass("TRN2", target_bir_lowering=False)
    S = Sched()

    def din(name, shape, dt=F32):
        return nc.dram_tensor(name, list(shape), dt, kind="ExternalInput")

    xin = din("xin", [128, NCH, NT])
    cin = din("cin", [128, NCH, NTC])
    cond = din("cond", [128, NCH, 2])
    vecs = din("vecs", [128, 2, NV])
    flg = din("flg", [128, 4])
    corr = din("corr", [128, 2, 4, 16])
    adaw = din("adaw", [2, 24, 128, 8, 256])
    wpool = din("wpool", [128, 4, 2, 256])
    wup = din("wup", [2, NJ, 128, 2, 8, 128])
    wdn = din("wdn", [2, 8, 128, NJ, 128])
    wq = din("wq", [128, 8, 8, 128])
    wk = din("wk", [128, 8, 8, 128])
    wv = din("wv", [128, 8, 1024])
    wo = din("wo", [128, 8, 8, 128])
    tabL = din("tabL", [3, 16, 128, TABMAX])
    tabS = din("tabS", [2, 16, 128, TABSMALL])
    outT = nc.dram_tensor("outT", [128, NCH, 2048], F32, kind="ExternalOutput")

    xsA = nc.dram_tensor("xsA", [128, NCH, NT], F32)
    xsB = nc.dram_tensor("xsB", [128, NCH, NT], F32)
    csA = nc.dram_tensor("csA", [128, NCH, NTC], F32)
    csB = nc.dram_tensor("csB", [128, NCH, NTC], F32)
    wupb = nc.dram_tensor("wupb", [2, NJ, 128, 2, 8, 128], BF16)
    wdnb = nc.dram_tensor("wdnb", [2, 8, 128, NJ, 128], BF16)
    wqb = nc.dram_tensor("wqb", [128, 8, 8, 128], BF16)
    wkb = nc.dram_tensor("wkb", [128, 8, 8, 128], BF16)
    wvb = nc.dram_tensor("wvb", [128, 8, 1024], BF16)
    wob = nc.dram_tensor("wob", [128, 8, 8, 128], BF16)
    kts = nc.dram_tensor("kts", [128, NCH, 21 * 128], BF16)
    vts = nc.dram_tensor("vts", [128, 21, 1024], BF16)
    if dbg:
        dbgx = nc.dram_tensor("dbgx", [128, NCH, NT], F32, kind="ExternalOutput")
        dbgc = nc.dram_tensor("dbgc", [128, NCH, NTC], F32, kind="ExternalOutput")

    R_xin, R_cin, R_xsA, R_xsB, R_csA, R_csB = (Res(n) for n in ("xin", "cin", "xsA", "xsB", "csA", "csB"))
    R_wupb, R_wdnb, R_kts, R_vts, R_out, R_const = (Res(n) for n in ("wupb", "wdnb", "kts", "vts", "out", "const"))

    ar = Arena(nc, 16512, 229344)
    RESC = {}
    es = ExitStack()

    def sb(name, shape, dt=F32, nres=1):
        t = ar.alloc(name, shape, dt)
        rs = [RESC.setdefault(f"{name}_{i}", Res(f"{name}_{i}")) for i in range(nres)]
        return t, (rs[0] if nres == 1 else rs)

    PS = []
    for i in range(8):
        PS.append((es.enter_context(nc.psum_tensor(f"ps{i}", [128, 512], F32)), Res(f"ps{i}")))

    VEC, R_VEC = sb("VEC", [128, 2, NV])
    SC, R_SC = sb("SC", [128, NCH, 2])
    SG0, R_SG0 = sb("SG0", [128, NCH, 2])
    MODS, R_MODS = sb("MODS", [128, 2, 48, 2])
    DER, R_DER = sb("DER", [128, 2, 2, 6, 8])
    FLG, R_FLG = sb("FLG", [128, 4])
    CORR, R_CORR = sb("CORR", [128, 2, 4, 16])
    ONES, R_ONES = sb("ONES", [128, 128], BF16)
    WP, R_WP = sb("WP", [128, 4, 2, 256], BF16)
    XT, R_XT = sb("XT", [128, NCH, 512])
    YS, R_YS = sb("YS", [128, NCH, 512], F32, nres=NCH)
    DB, R_DB = sb("DB", [128, NCH, 512], BF16, nres=NCH)
    RS = [sb(f"RS{i}", [128, 512]) for i in range(2)]
    SQ = [sb(f"SQ{i}", [128, 512], BF16) for i in range(3)]
    TMP = [sb(f"TMP{i}", [128, 512]) for i in range(3)]
    base_mark = ar.off

    rot = {}

    def nxt(lst, key):
        i = rot.get(key, 0)
        rot[key] = i + 1
        return lst[i % len(lst)]

    def der(layer, cnd, which):
        return lambda c: DER[:, layer, cnd, which, c:c + 1]

    S.dma("sp", lambda e: e.dma_start(out=VEC[:], in_=vecs[:]), [R_const], [R_VEC], R_VEC)
    S.dma("sp", lambda e: e.dma_start(out=SC[:], in_=cond[:]), [R_const], [R_SC], R_SC)
    S.dma("sp", lambda e: e.dma_start(out=FLG[:], in_=flg[:]), [R_const], [R_FLG], R_FLG)
    S.dma("sp", lambda e: e.dma_start(out=CORR[:], in_=corr[:]), [R_const], [R_CORR], R_CORR)
    S.dma("pool", lambda e: e.dma_start(out=WP[:], in_=wpool[:], max_dma_last_dim=4096), [R_const], [R_WP], R_WP)
    S.op("dve", lambda e: e.memset(ONES[:], 1.0), [], [R_ONES])
    LANES = [Res(f"lane{i}") for i in range(8)]
    for ln in LANES:
        ln.nobar = True
    R_wupc = [[Res(f"wupc{i}_{j}") for j in range(NJ)] for i in range(2)]
    R_wdnc = [[Res(f"wdnc{i}_{m}") for m in range(8)] for i in range(2)]
    kk = 0
    for i in range(2):
        for j in range(NJ):
            ln = LANES[kk % 8]
            S.dma("pool", lambda e, i=i, j=j: e.dma_start(out=wupb[i, j], in_=wup[i, j], max_dma_last_dim=4096), [R_const], [R_wupc[i][j], ln], ln)
            kk += 1
        for m in range(8):
            ln = LANES[kk % 8]
            S.dma("pool", lambda e, i=i, m=m: e.dma_start(out=wdnb[i, m], in_=wdn[i, m], max_dma_last_dim=4096), [R_const], [R_wdnc[i][m], ln], ln)
            kk += 1
    R_attw = {n: Res("attw_" + n) for n in ("q", "k", "v", "o")}
    for nm, srcw, dstw in (("k", wk, wkb), ("v", wv, wvb), ("q", wq, wqb), ("o", wo, wob)):
        ln = LANES[kk % 8]
        S.dma("pool", lambda e, srcw=srcw, dstw=dstw: e.dma_start(out=dstw[:], in_=srcw[:], max_dma_last_dim=4096), [R_const], [R_attw[nm], ln], ln)
        kk += 1
    S.op("act", lambda e: e.activation(out=SG0[:], in_=SC[:], func=AF.Sigmoid), [R_SC], [R_SG0])
    S.op("dve", lambda e: e.tensor_tensor(out=SC[:], in0=SC[:], in1=SG0[:], op=ALU.mult), [R_SC, R_SG0], [R_SC])

    EPSB, R_EPSB = sb("EPSB", [128, 1])
    S.op("dve", lambda e: e.memset(EPSB[:], EPS), [], [R_EPSB])
    m1 = ar.off
    WA = [sb(f"WA{i}", [128, 8, 256]) for i in range(4)]
    PMB = {0: 0, 1: 3}
    wa_of = {}

    def mods_dma(layer, g):
        wa, R_wa = nxt(WA, "wa")
        wa_of[(layer, g)] = (wa, R_wa)
        S.dma("act" if layer == 1 else "sp", lambda e: e.dma_start(out=wa[:], in_=adaw[layer, g]), [R_const], [R_wa], R_wa)

    def mods_mm(layer, g):
        pm, R_pm = PS[PMB[layer]]
        wa, R_wa = wa_of[(layer, g)]
        for mm in range(2):
            m = g * 2 + mm
            for k in range(8):
                S.op("pe", lambda e, m=m, mm=mm, k=k: e.matmul(
                    pm[:, 2 * m:2 * m + 2], lhsT=wa[:, k, mm * 128:(mm + 1) * 128], rhs=SC[:, k, :],
                    start=(k == 0), stop=(k == 7)), [R_wa, R_SC], [R_pm], sig=True)

    def mods_finish(layer):
        pm, R_pm = PS[PMB[layer]]
        for cnd in range(2):
            S.op("dve", lambda e, pm=pm, layer=layer, cnd=cnd: e.tensor_tensor(
                out=MODS[:, layer, :, cnd], in0=pm[:, cnd:96:2], in1=VEC[:, layer, V_ADAB:V_ADAB + 48], op=ALU.add),
                [R_pm, R_VEC], [R_MODS])
            def mod(k0, layer=layer, cnd=cnd):
                return MODS[:, layer, k0:k0 + 8, cnd]
            for which, (sc0, gv) in ((0, (8, V_PRE)), (3, (32, V_FPRE))):
                S.op("dve", lambda e, which=which, sc0=sc0, gv=gv, layer=layer, cnd=cnd, mod=mod: e.scalar_tensor_tensor(
                    out=DER[:, layer, cnd, which, :], in0=mod(sc0), scalar=1.0, in1=VEC[:, layer, gv:gv + 8],
                    op0=ALU.add, op1=ALU.mult), [R_MODS, R_VEC], [R_DER])
            for which, sh0 in ((1, 0), (4, 24)):
                S.op("dve", lambda e, which=which, sh0=sh0, layer=layer, cnd=cnd, mod=mod: e.tensor_copy(
                    out=DER[:, layer, cnd, which, :], in_=mod(sh0)), [R_MODS], [R_DER])
            for which, (g0, gv) in ((2, (16, V_POST)), (5, (40, V_FPOST))):
                S.op("dve", lambda e, which=which, g0=g0, gv=gv, layer=layer, cnd=cnd, mod=mod: e.tensor_tensor(
                    out=DER[:, layer, cnd, which, :], in0=mod(g0), in1=VEC[:, layer, gv:gv + 8], op=ALU.mult),
                    [R_MODS, R_VEC], [R_DER])

    for g in range(3):
        mods_dma(0, g)
    for g in range(24):
        if g + 3 < 24:
            mods_dma(0, g + 3)
        mods_mm(0, g)
    mods_finish(0)
    mods_l1_dma = list(range(24))
    mods_l1_mm = []
    S.barrier()

    def rstd_from(ps_ss, R_ss, W):
        rs, R_rs = nxt(RS, "rs")
        S.op("act", lambda e: e.activation(out=rs[:, :W], in_=ps_ss[:, :W], func=AF.Sqrt, bias=EPSB[:, 0:1], scale=1.0 / D),
             [R_ss, R_EPSB], [R_rs])
        S.op("dve", lambda e: e.reciprocal(out=ps_ss[:, :W], in_=rs[:, :W]), [R_rs], [R_ss])
        return ps_ss, R_ss

    def sumsq(src_fn, reads, W, bank):
        ps, R_ps = PS[bank]
        for c in range(NCH):
            sq, R_sq = nxt(SQ, "sq")
            S.op("act", lambda e, c=c, sq=sq: e.activation(out=sq[:, :W], in_=src_fn(c), func=AF.Square), reads(c), [R_sq])
            S.op("pe", lambda e, c=c, sq=sq: e.matmul(ps[:, :W], lhsT=ONES[:], rhs=sq[:, :W], start=(c == 0), stop=(c == 7)),
                 [R_sq, R_ONES], [R_ps], sig=True)
        return ps, R_ps

    def load_x(src, R_src, a, W, xt=None):
        X_, R_X = xt or (XT, R_XT)
        S.dma("sp", lambda e: e.dma_start(out=X_[:, :, 0:W], in_=src[:, :, a:a + W]), [R_src], [R_X], R_X)

    def prenorm(W, A, SH, out_fn, out_res, xt=None):
        X_, R_X = xt or (XT, R_XT)
        ps, R_ps = sumsq(lambda c: X_[:, c, 0:W], lambda c: [R_X], W, 7)
        rs, R_rs = rstd_from(ps, R_ps, W)
        for c in range(NCH):
            tmp, R_tmp = nxt(TMP, "tmp")
            S.op("dve", lambda e, c=c, tmp=tmp: e.scalar_tensor_tensor(
                out=tmp[:, :W], in0=X_[:, c, 0:W], scalar=A(c), in1=rs[:, :W], op0=ALU.mult, op1=ALU.mult),
                [R_X, R_rs, R_DER], [R_tmp])
            S.op("act", lambda e, c=c, tmp=tmp: e.activation(out=out_fn(c), in_=tmp[:, :W], func=AF.Identity, bias=SH(c), scale=1.0),
                 [R_tmp, R_DER], [out_res(c)])

    def post_chunk_evac(m, ps, R_ps, n, scale, ys=None):
        Y_, R_Y = ys or (YS, R_YS)
        if scale is None:
            S.op("act", lambda e: e.activation(out=Y_[:, m, :n], in_=ps[:, :n], func=AF.Copy), [R_ps], [R_Y[m]])
        else:
            S.op("act", lambda e: e.activation(out=Y_[:, m, :n], in_=ps[:, :n], func=AF.Identity, scale=scale(m)),
                 [R_ps, R_VEC], [R_Y[m]])

    def post_finish(n, G, x_off, dst, R_dst, dcol, xt=None, ys=None):
        X_, R_X = xt or (XT, R_XT)
        Y_, R_Y = ys or (YS, R_YS)
        ps, R_ps = sumsq(lambda c: Y_[:, c, :n], lambda c: [R_Y[c]], n, 6)
        rs, R_rs = rstd_from(ps, R_ps, n)
        for c in range(NCH):
            tmp, R_tmp = nxt(TMP, "tmp")
            S.op("dve", lambda e, c=c, tmp=tmp: e.scalar_tensor_tensor(
                out=tmp[:, :n], in0=Y_[:, c, :n], scalar=G(c), in1=rs[:, :n], op0=ALU.mult, op1=ALU.mult),
                [R_Y[c], R_rs, R_DER], [R_tmp])
            S.op("pool", lambda e, c=c, tmp=tmp: e.tensor_tensor(
                out=X_[:, c, x_off:x_off + n], in0=X_[:, c, x_off:x_off + n], in1=tmp[:, :n], op=ALU.add),
                [R_X, R_tmp], [R_X])
        S.dma("pool", lambda e: e.dma_start(out=dst[:, :, dcol:dcol + n], in_=X_[:, :, x_off:x_off + n]), [R_X], [R_dst], R_X)

    def flag_cols(buf_fn, R_buf, a, W, lo, hi, fcol):
        l, h = max(lo, a), min(hi, a + W)
        if l >= h:
            return
        S.op("dve", lambda e: e.tensor_scalar(out=buf_fn(l - a, h - a), in0=buf_fn(l - a, h - a),
                                                scalar1=FLG[:, fcol:fcol + 1], scalar2=None, op0=ALU.mult),
             [R_buf, R_FLG], [R_buf])

    def pool_phase(src, R_src, dst, R_dst, ntok, top, bot, cnd, fcol, cidx, with_mods=False):
        m0 = ar.off
        Us = [sb(f"U{i}", [128, NCH, 512], F32, nres=NCH) for i in range(2)]
        XTs = [(XT, R_XT), sb("XTb", [128, NCH, 512])]
        YSs = [(YS, R_YS), sb("YSb", [128, NCH, 512], F32, nres=NCH)]
        DBs = [(DB, R_DB), sb("DBb", [128, NCH, 512], BF16, nres=NCH)]
        T = [sb(f"T{i}", [128, 512]) for i in range(6)]
        lo_out, hi_out = 8, ntok - 8
        ntile = -(-(hi_out - lo_out) // 496)
        tsz = -(-(hi_out - lo_out) // ntile)

        def mods_hook(k):
            if with_mods:
                for _ in range(k):
                    if mods_l1_mm:
                        mods_mm(1, mods_l1_mm.pop(0))

        def do_tile(a, b, par):
            U, R_U = Us[par]
            xt = XTs[par]
            ys = YSs[par]
            DBp, R_DBp = DBs[par]
            n = b - a
            W = n + 16
            a0 = a - 8
            load_x(src, R_src, a0, W, xt=xt)
            prenorm(W, der(0, cnd, 0), der(0, cnd, 1), lambda c: U[:, c, 0:W], lambda c: R_U[c], xt=xt)
            mods_hook(2)
            for c in range(NCH):
                def ub(l, h, c=c):
                    return U[:, c, l:h]
                flag_cols(ub, R_U[c], a0, W, top - 8, top, fcol)
                flag_cols(ub, R_U[c], a0, W, bot, bot + 8, fcol + 1)
            for c in range(NCH):
                g = c // 2
                w = 2 << g
                t, R_t = nxt(T, "T")
                S.op("pool", lambda e, t=t, c=c: e.tensor_tensor(out=t[:, 1:W], in0=U[:, c, 1:W], in1=U[:, c, 0:W - 1], op=ALU.add),
                     [R_U[c]], [R_t])
                cur, R_cur = t, R_t
                vlo, vhi = 1, W
                sh = 1
                for lvl in range(g):
                    t2, R_t2 = nxt(T, "T")
                    nlo, nhi = vlo + sh, vhi - sh
                    S.op("pool", lambda e, t2=t2, cur=cur, nlo=nlo, nhi=nhi, sh=sh: e.tensor_tensor(
                        out=t2[:, nlo:nhi], in0=cur[:, nlo + sh:nhi + sh], in1=cur[:, nlo - sh:nhi - sh], op=ALU.add),
                        [R_cur], [R_t2])
                    cur, R_cur = t2, R_t2
                    vlo, vhi = nlo, nhi
                    sh *= 2
                assert vlo <= 8 and vhi >= W - 8, (vlo, vhi, W)
                for (l0, h0, off) in ((top, top + 8, 0), (bot - 8, bot, 8)):
                    l, h = max(l0, a), min(h0, b)
                    if l < h:
                        S.op("dve", lambda e, cur=cur, l=l, h=h, l0=l0, off=off, g=g: e.tensor_tensor(
                            out=cur[:, l - a0:h - a0], in0=cur[:, l - a0:h - a0],
                            in1=CORR[:, cidx, g, off + l - l0:off + h - l0], op=ALU.mult), [R_cur, R_CORR], [R_cur])
                S.op("dve", lambda e, cur=cur, c=c, w=w: e.scalar_tensor_tensor(
                    out=DBp[:, c, 0:n], in0=cur[:, 8:8 + n], scalar=1.0 / w, in1=U[:, c, 8:8 + n],
                    op0=ALU.mult, op1=ALU.subtract), [R_cur, R_U[c]], [R_DBp[c]])
            for m in range(NCH):
                g, ml = m // 2, m % 2
                ps, R_ps = PS[m % 2]
                for kc in range(2):
                    S.op("pe", lambda e, ps=ps, g=g, ml=ml, kc=kc: e.matmul(
                        ps[:, :n], lhsT=WP[:, g, kc, ml * 128:(ml + 1) * 128], rhs=DBp[:, 2 * g + kc, 0:n],
                        start=(kc == 0), stop=(kc == 1)), [R_WP, R_DBp[2 * g + kc]], [R_ps], sig=True)
                post_chunk_evac(m, ps, R_ps, n, lambda m: VEC[:, 0, V_PSC + m:V_PSC + m + 1], ys=ys)
            post_finish(n, der(0, cnd, 2), 8, dst, R_dst, a, xt=xt, ys=ys)
            mods_hook(2)

        a = lo_out
        ti = 0
        while a < hi_out:
            b = min(a + tsz, hi_out)
            do_tile(a, b, ti % 2)
            if with_mods:
                while mods_l1_mm:
                    mods_mm(1, mods_l1_mm.pop(0))
                for _ in range(4):
                    if mods_l1_dma:
                        g = mods_l1_dma.pop(0)
                        mods_dma(1, g)
                        mods_l1_mm.append(g)
            a = b
            ti += 1
        if with_mods:
            while mods_l1_mm or mods_l1_dma:
                while mods_l1_mm:
                    mods_mm(1, mods_l1_mm.pop(0))
                for _ in range(4):
                    if mods_l1_dma:
                        g = mods_l1_dma.pop(0)
                        mods_dma(1, g)
                        mods_l1_mm.append(g)
            mods_finish(1)
        S.barrier()
        ar.off = m0

    def ffn_phase(layer, src, R_src, dst, R_dst, lo_out, hi_out, top, bot, cnd, fcol, dst_shift):
        m0 = ar.off
        NST = 2
        UBs = [sb(f"UB{i}", [128, NCH, NST * 512], BF16, nres=NST) for i in range(2)]
        G, R_G = sb("G", [128, NJ, NST * 448], BF16, nres=NST)
        WU = [sb(f"WU{i}", [128, 2, 8, 128], BF16) for i in range(2)]
        WD = [sb(f"WD{i}", [128, NJ, 128], BF16) for i in range(2)]
        FT = [sb(f"FT{i}", [128, 512]) for i in range(12)]
        XTn = sb("XTn", [128, NCH, 512])
        tot = hi_out - lo_out
        ntile = -(-tot // 448)
        tsz = -(-tot // ntile)
        tiles = []
        a = lo_out
        while a < hi_out:
            b = min(a + tsz, hi_out)
            tiles.append((a, b))
            a = b
        sts = [tiles[i:i + NST] for i in range(0, len(tiles), NST)]

        def norm_tile(si, ti):
            a, b = sts[si][ti]
            UB, R_UB = UBs[si % 2]
            n = b - a
            W = n + 2
            load_x(src, R_src, a - 1, W, xt=XTn)
            prenorm(W, der(layer, cnd, 3), der(layer, cnd, 4),
                    lambda c: UB[:, c, ti * 512:ti * 512 + W], lambda c: R_UB[ti], xt=XTn)

            def ubf(l, h):
                return UB[:, :, ti * 512 + l:ti * 512 + h]
            flag_cols(ubf, R_UB[ti], a - 1, W, top - 1, top, fcol)
            flag_cols(ubf, R_UB[ti], a - 1, W, bot, bot + 1, fcol + 1)

        pend = []

        def stage3(ag, R_ag, av, R_av, n, j, ti):
            p, R_p = nxt(FT, "ft")
            S.op("act", lambda e: e.activation(out=p[:, :n], in_=ag[:, :n], func=AF.Gelu_apprx_tanh), [R_ag], [R_p])
            S.op("pool", lambda e: e.tensor_tensor(out=G[:, j, ti * 448:ti * 448 + n], in0=p[:, :n], in1=av[:, :n], op=ALU.mult),
                 [R_p, R_av], [R_G[ti]])

        def stage12(si, j, wu, R_wu, ti):
            a, b = sts[si][ti]
            UB, R_UB = UBs[si % 2]
            n = b - a
            W = n + 2
            hp = []
            for part in range(2):
                ps, R_ps = PS[nxt([0, 1, 2, 3, 4, 5], "ffnps")]
                for k in range(8):
                    S.op("pe", lambda e, ps=ps, part=part, k=k: e.matmul(
                        ps[:, :W], lhsT=wu[:, part, k, :], rhs=UB[:, k, ti * 512:ti * 512 + W],
                        start=(k == 0), stop=(k == 7)), [R_wu, R_UB[ti]], [R_ps], sig=True)
                hp.append((ps, R_ps))
            conv = []
            for part in range(2):
                ps, R_ps = hp[part]
                ch = part * NJ + j
                acc, R_acc = nxt(FT, "ft")
                S.op("act", lambda e, ps=ps, acc=acc, ch=ch: e.activation(
                    out=acc[:, :n], in_=ps[:, 1:n + 1], func=AF.Identity,
                    bias=VEC[:, layer, V_CB + ch:V_CB + ch + 1], scale=VEC[:, layer, V_CW + 44 + ch:V_CW + 44 + ch + 1]),
                    [R_ps, R_VEC], [R_acc])
                for tap, off in ((0, 0), (2, 2)):
                    S.op("dve", lambda e, ps=ps, acc=acc, ch=ch, tap=tap, off=off: e.scalar_tensor_tensor(
                        out=acc[:, :n], in0=ps[:, off:off + n], scalar=VEC[:, layer, V_CW + tap * 44 + ch:V_CW + tap * 44 + ch + 1],
                        in1=acc[:, :n], op0=ALU.mult, op1=ALU.add), [R_ps, R_acc, R_VEC], [R_acc])
                conv.append((acc, R_acc))
            (ag, R_ag), (av, R_av) = conv
            pend.append((ag, R_ag, av, R_av, n, j, ti))
            if len(pend) > 1:
                stage3(*pend.pop(0))

        def down_tile(si, ti):
            a, b = sts[si][ti]
            n = b - a
            for m in range(NCH):
                wd, R_wd = nxt(WD, "wd")
                S.dma("sp", lambda e, wd=wd, m=m: e.dma_start(out=wd[:], in_=wdnb[layer, m]), [R_wdnc[layer][m]], [R_wd], R_wd)
                ps, R_ps = PS[m % 4]
                for j in range(NJ):
                    S.op("pe", lambda e, ps=ps, wd=wd, j=j: e.matmul(
                        ps[:, :n], lhsT=wd[:, j, :], rhs=G[:, j, ti * 448:ti * 448 + n],
                        start=(j == 0), stop=(j == NJ - 1)), [R_wd, R_G[ti]], [R_ps], sig=True)
                post_chunk_evac(m, ps, R_ps, n, None)
            load_x(src, R_src, a, n)
            if si + 1 < len(sts) and ti < len(sts[si + 1]):
                norm_tile(si + 1, ti)
            post_finish(n, der(layer, cnd, 5), 0, dst, R_dst, a - dst_shift)

        for ti in range(len(sts[0])):
            norm_tile(0, ti)
        for si, st in enumerate(sts):
            for j in range(NJ):
                wu, R_wu = nxt(WU, "wu")
                S.dma("sp", lambda e, wu=wu, j=j: e.dma_start(out=wu[:], in_=wupb[layer, j]), [R_wupc[layer][j]], [R_wu], R_wu)
                for ti in range(len(st)):
                    stage12(si, j, wu, R_wu, ti)
            while pend:
                stage3(*pend.pop(0))
            nxt_n = len(sts[si + 1]) if si + 1 < len(sts) else 0
            for ti in range(max(len(st), nxt_n)):
                if ti < len(st):
                    down_tile(si, ti)
                elif ti < nxt_n:
                    norm_tile(si + 1, ti)
        S.barrier()
        ar.off = m0

    if stop_after >= 1:
        pool_phase(xin, R_xin, xsA, R_xsA, NT, TOP, BOT, 0, 0, 0, with_mods=True)
        pool_phase(cin, R_cin, csA, R_csA, NTC, TOPC, BOTC, 1, 2, 1)
    ar.off = m1
    if stop_after >= 2:
        ffn_phase(0, xsA, R_xsA, xsB, R_xsB, PAD, NT - PAD, TOP, BOT, 0, 0, 0)
        ffn_phase(0, csA, R_csA, csB, R_csB, TOPC, BOTC, TOPC, BOTC, 1, 2, 0)

    def kv_phase():
        m0 = ar.off
        WK, R_WK = sb("WK", [128, 8, 8, 128], BF16)
        WV, R_WV = sb("WV", [128, 8, 1024], BF16)
        KOs = [sb(f"KO{i}", [128, NCH, 512], BF16) for i in range(2)]
        VOs = [sb(f"VO{i}", [128, 4, 1024], BF16) for i in range(2)]
        XTs = [(XT, R_XT), sb("XTk", [128, NCH, 512])]
        DBs = [(DB, R_DB), sb("DBk", [128, NCH, 512], BF16, nres=NCH)]
        S.dma("sp", lambda e: e.dma_start(out=WK[:], in_=wkb[:]), [R_attw["k"]], [R_WK], R_WK)
        S.dma("sp", lambda e: e.dma_start(out=WV[:], in_=wvb[:]), [R_attw["v"]], [R_WV], R_WV)

        def kv_tile(src, R_src, a, n, cnd, par):
            xt = XTs[par]
            DBp, R_DBp = DBs[par]
            KO, R_KO = KOs[par]
            VO, R_VO = VOs[par]
            load_x(src, R_src, a, n, xt=xt)
            prenorm(n, der(1, cnd, 0), der(1, cnd, 1), lambda c: DBp[:, c, 0:n], lambda c: R_DBp[c], xt=xt)
            for m in range(NCH):
                ps, R_ps = PS[m % 2]
                for k in range(8):
                    S.op("pe", lambda e, ps=ps, m=m, k=k: e.matmul(ps[:, :n], lhsT=WK[:, m, k, :], rhs=DBp[:, k, 0:n],
                                                                  start=(k == 0), stop=(k == 7)),
                         [R_WK, R_DBp[k]], [R_ps], sig=True)
                S.op("act", lambda e, ps=ps, m=m: e.activation(out=KO[:, m, :n], in_=ps[:, :n], func=AF.Copy), [R_ps], [R_KO])
            for sub in range(n // 128):
                for half in range(2):
                    ps, R_ps = PS[2 + nxt([0, 1, 2, 3], "kvps")]
                    for k in range(8):
                        S.op("pe", lambda e, ps=ps, k=k, sub=sub, half=half: e.matmul(
                            ps[:, :], lhsT=DBp[:, k, sub * 128:(sub + 1) * 128], rhs=WV[:, k, half * 512:(half + 1) * 512],
                            start=(k == 0), stop=(k == 7)), [R_WV, R_DBp[k]], [R_ps], sig=True)
                    if half == 0:
                        S.op("dve", lambda e, ps=ps, sub=sub, half=half: e.tensor_copy(out=VO[:, sub, half * 512:(half + 1) * 512], in_=ps[:, :]),
                             [R_ps], [R_VO])
                    else:
                        S.op("act", lambda e, ps=ps, sub=sub, half=half: e.activation(out=VO[:, sub, half * 512:(half + 1) * 512], in_=ps[:, :], func=AF.Copy),
                             [R_ps], [R_VO])
            if cnd == 0:
                tk = a - PAD
                S.dma("pool", lambda e: e.dma_start(out=kts[:, :, tk:tk + n], in_=KO[:, :, 0:n]), [R_KO], [R_kts], R_KO)
                S.dma("pool", lambda e: e.dma_start(out=vts[:, tk // 128:tk // 128 + n // 128, :], in_=VO[:, 0:n // 128, :]),
                      [R_VO], [R_vts], R_VO)
            else:
                S.op("act", lambda e: e.activation(out=KCT[:, :, 0:n], in_=KO[:, :, 0:n], func=AF.Copy), [R_KO], [R_KCT])
                S.op("dve", lambda e: e.tensor_copy(out=VC[:, 0:n // 128, :], in_=VO[:, 0:n // 128, :]), [R_VO], [R_VC])

        ti = 0
        for (src, R_src, t0, tend, cnd) in ((xsB, R_xsB, PAD, NT - PAD, 0), (csB, R_csB, TOPC, BOTC, 1)):
            a = t0
            while a < tend:
                n = min(512, tend - a)
                kv_tile(src, R_src, a, n, cnd, ti % 2)
                ti += 1
                a += n
        S.barrier()
        ar.off = m0

    if stop_after >= 3:
        KCT, R_KCT = sb("KCT", [128, NCH, 256], BF16)
        VC, R_VC = sb("VC", [128, 2, 1024], BF16)
        kv_phase()

    def attn_phase():
        m0 = ar.off
        WQ, R_WQ = sb("WQ", [128, 8, 8, 128], BF16)
        WO, R_WO = sb("WO", [128, 8, 8, 128], BF16)
        KT, R_KT = sb("KT", [128, NCH, 1024], BF16)
        VT, R_VT = sb("VT", [128, 8, 1024], BF16)
        QT, R_QT = sb("QT", [128, 2, NCH, 512], BF16)
        OT, R_OT = sb("OT", [128, NCH, 512], BF16, nres=NCH)
        TB = [sb(f"TB{i}", [128, TABMAX]) for i in range(2)]
        SBF = [sb(f"SBF{i}", [128, 512]) for i in range(3)]
        EB = [sb(f"EB{i}", [128, 512], BF16) for i in range(4)]
        RD, R_RD = sb("RD", [128, 512])
        S.dma("sp", lambda e: e.dma_start(out=WQ[:], in_=wqb[:]), [R_attw["q"]], [R_WQ], R_WQ)
        S.op("pool", lambda e: e.memset(QT[:], 0.0), [], [R_QT])
        S.dma("sp", lambda e: e.dma_start(out=WO[:], in_=wob[:]), [R_attw["o"]], [R_WO], R_WO)
        XTq = [(XT, R_XT), sb("XTq", [128, NCH, 512])]
        gcount = [0]

        def do_group(q0, nr, chunks, typ):
            nq = nr * 64
            t0 = PAD + q0 * 64
            c0 = chunks[0]
            nck = len(chunks)
            xt = XTq[gcount[0] % 2]
            gcount[0] += 1
            load_x(xsB, R_xsB, t0, nq, xt=xt)
            S.dma("sp", lambda e, c0=c0, nck=nck: e.dma_start(out=KT[:, :, 0:nck * 128], in_=kts[:, :, c0 * 128:(c0 + nck) * 128]),
                  [R_kts], [R_KT], R_KT)
            S.dma("sp", lambda e, c0=c0, nck=nck: e.dma_start(out=VT[:, 0:nck, :], in_=vts[:, c0:c0 + nck, :]), [R_vts], [R_VT], R_VT)
            prenorm(nq, der(1, 0, 0), der(1, 0, 1), lambda c: DB[:, c, 0:nq], lambda c: R_DB[c], xt=xt)
            for m in range(NCH):
                ps, R_ps = PS[m % 2]
                for k in range(8):
                    S.op("pe", lambda e, ps=ps, m=m, k=k: e.matmul(ps[:, :nq], lhsT=WQ[:, m, k, :], rhs=DB[:, k, 0:nq],
                                                                  start=(k == 0), stop=(k == 7)), [R_WQ, R_DB[k]], [R_ps], sig=True)
                S.op("act", lambda e, ps=ps, m=m: e.activation(out=QT[0:64, 0, m, :nq], in_=ps[0:64, :nq], func=AF.Copy, scale=0.125), [R_ps], [R_QT])
                S.op("act", lambda e, ps=ps, m=m: e.activation(out=QT[64:128, 1, m, :nq], in_=ps[64:128, :nq], func=AF.Copy, scale=0.125), [R_ps], [R_QT])
            colr = COLR[typ]
            items = []
            tab_loaders = {}

            def make_head(h):
                m, pb = h // 2, (h % 2) * 64
                po, R_po = PS[4 + (h % 2)]
                pd, R_pd = PS[6 + (h % 2)]
                hs = {}
                nit = 2 + len(chunks)

                def load_tab():
                    if "tb" in hs:
                        return
                    tb, R_tb = nxt(TB, "tb")
                    hs["tb"] = (tb, R_tb)
                    S.dma("sp", lambda e: e.dma_start(out=tb[:, 0:TABW[typ]], in_=(tabL[typ - 1, h, :, 0:TABW[typ]] if typ in (1, 2, 3)
                                                                                  else tabS[typ // 4, h, :, 0:TABW[typ]])),
                          [R_const], [R_tb], R_tb)

                tab_loaders[h] = load_tab

                def end_fn():
                    S.op("act", lambda e: e.activation(out=RD[pb:pb + 64, :nq], in_=pd[pb:pb + 64, :nq], func=AF.Ln), [R_pd], [R_RD])
                    S.op("act", lambda e: e.activation(out=RD[pb:pb + 64, :nq], in_=RD[pb:pb + 64, :nq], func=AF.Exp, scale=-1.0), [R_RD], [R_RD])
                    S.op("dve", lambda e: e.tensor_tensor(out=OT[pb:pb + 64, m, :nq], in0=po[pb:pb + 64, :nq],
                                                          in1=RD[pb:pb + 64, :nq], op=ALU.mult), [R_po, R_RD], [R_OT[m]])

                def mk_item(idx, kind, j, off, clo, chi):
                    d = {}
                    ncj = chi - clo
                    first, last = (idx == 0), (idx == nit - 1)

                    def s_fn():
                        ps, R_ps = PS[nxt([0, 1, 2, 3], "aps")]
                        d["ps"] = (ps, R_ps)
                        if kind == "ctx":
                            S.op("pe", lambda e: e.matmul(ps[:, :ncj], lhsT=KCT[:, m, j * 128:(j + 1) * 128],
                                                          rhs=QT[:, h % 2, m, clo:chi], start=True, stop=True), [R_KCT, R_QT], [R_ps])
                        else:
                            S.op("pe", lambda e: e.matmul(ps[:, :ncj], lhsT=KT[:, m, j * 128:(j + 1) * 128],
                                                          rhs=QT[:, h % 2, m, clo:chi], start=True, stop=True), [R_KT, R_QT], [R_ps])

                    def pv_fn():
                        ps, R_ps = d["ps"]
                        eb, R_eb = nxt(EB, "eb")
                        if kind == "ctx":
                            S.op("act", lambda e: e.activation(out=eb[:, :ncj], in_=ps[:, :ncj], func=AF.Exp), [R_ps], [R_eb])
                            vsrc, R_vsrc = VC, R_VC
                        else:
                            tb, R_tb = hs["tb"]
                            sbf, R_sbf = nxt(SBF, "sbf")
                            S.op("dve", lambda e: e.tensor_tensor(out=sbf[:, :ncj], in0=ps[:, :ncj], in1=tb[:, off:off + ncj], op=ALU.add),
                                 [R_ps, R_tb], [R_sbf])
                            S.op("act", lambda e: e.activation(out=eb[:, :ncj], in_=sbf[:, :ncj], func=AF.Exp), [R_sbf], [R_eb])
                            vsrc, R_vsrc = VT, R_VT
                        S.op("pe", lambda e: e.matmul(po[:, clo:chi], lhsT=vsrc[:, j, m * 128:(m + 1) * 128], rhs=eb[:, :ncj],
                                                      start=first, stop=last), [R_vsrc, R_eb], [R_po], sig=True)
                        S.op("pe", lambda e: e.matmul(pd[:, clo:chi], lhsT=ONES[:, :], rhs=eb[:, :ncj],
                                                      start=first, stop=last), [R_ONES, R_eb], [R_pd], sig=True)
                    items.append((s_fn, pv_fn, (lambda: tab_loaders[h + 1]()) if (first and h + 1 < 16) else None, end_fn if last else None))

                idx = 0
                for cc in range(2):
                    mk_item(idx, "ctx", cc, 0, 0, nq)
                    idx += 1
                off = 0
                for j in range(len(chunks)):
                    lo, hi = colr[j]
                    mk_item(idx, "lat", j, off, lo * 64, hi * 64)
                    off += (hi - lo) * 64
                    idx += 1

            for h in range(16):
                make_head(h)
            tab_loaders[0]()
            LA = 3
            deferred = []
            for i in range(min(LA, len(items))):
                items[i][0]()
            for i in range(len(items)):
                if i + LA < len(items):
                    items[i + LA][0]()
                if items[i][2]:
                    items[i][2]()
                items[i][1]()
                if items[i][3]:
                    deferred.append((i + 2, items[i][3]))
                while deferred and deferred[0][0] <= i:
                    deferred.pop(0)[1]()
            while deferred:
                deferred.pop(0)[1]()
            for m in range(NCH):
                ps, R_ps = PS[m % 2]
                for k in range(8):
                    S.op("pe", lambda e, ps=ps, m=m, k=k: e.matmul(ps[:, :nq], lhsT=WO[:, m, k, :], rhs=OT[:, k, 0:nq],
                                                                  start=(k == 0), stop=(k == 7)), [R_WO, R_OT[k]], [R_ps], sig=True)
                post_chunk_evac(m, ps, R_ps, nq, None)
            post_finish(nq, der(1, 0, 2), 0, xsA, R_xsA, t0, xt=xt)

        for (q0, nr, chunks, typ) in AGROUPS:
            do_group(q0, nr, chunks, typ)
        S.barrier()
        ar.off = m0

    if stop_after >= 4:
        attn_phase()
    if stop_after >= 5:
        ffn_phase(1, xsA, R_xsA, outT, R_out, TOP, BOT, TOP, BOT, 0, 0, TOP)

    if dbg:
        srcx = xsA if stop_after in (1, 4) else xsB
        R_srcx = R_xsA if stop_after in (1, 4) else R_xsB
        srcc = csA if stop_after == 1 else csB
        R_srcc = R_csA if stop_after == 1 else R_csB
        R_d1, R_d2 = Res("dbg1"), Res("dbg2")
        S.dma("pool", lambda e: e.dma_start(out=dbgx[:], in_=srcx[:]), [R_srcx], [R_out], R_d1)
        S.dma("pool", lambda e: e.dma_start(out=dbgc[:], in_=srcc[:]), [R_srcc], [R_out], R_d2)
    S.barrier(final=True)
    S.emit(nc, es)
    es.close()
    return nc


def _fm(a2d):
    T = a2d.shape[0]
    return np.ascontiguousarray(a2d.reshape(T, NCH, 128).transpose(2, 1, 0))


def _vec_cols(v):
    return np.ascontiguousarray(v.reshape(-1, 128).T)


def _bias_tables(rpb, quarter):
    r0 = quarter * 32
    tabs = np.full((5, 16, 128, TABMAX), MASK, np.float32)
    qcol = np.arange(64)
    kcol = np.arange(64)
    ws = np.clip(qcol - 8, 0, 48)
    cvalid = (kcol[:, None] >= ws[None, :]) & (kcol[:, None] < ws[None, :] + 16)
    dc = np.clip(kcol[:, None] - qcol[None, :] + 15, 0, 30)
    done = set()
    for (q0, nr, chunks, typ) in AGROUPS:
        if typ in done:
            continue
        done.add(typ)
        off = 0
        for j, c in enumerate(chunks):
            lo, hi = COLR[typ][j]
            for qr in range(lo, hi):
                gq = r0 - 5 + q0 + qr
                for kr2 in range(2):
                    gk = r0 - 5 + 2 * c + kr2
                    ok = (0 <= gq < 128) and (0 <= gk < 128)
                    if ok:
                        wlo, whi = _row_window(gq)
                        ok = wlo <= gk < whi
                    if ok:
                        dr = gk - gq + 7
                        blk = np.where(cvalid[None], rpb[:, dr][:, dc], np.float32(MASK))
                        tabs[typ, :, kr2 * 64:(kr2 + 1) * 64, off + (qr - lo) * 64: off + (qr - lo + 1) * 64] = blk
            off += (hi - lo) * 64
    return tabs


_CACHE = {}


def kernel(x, c, ctx, c_ctx, ada_w, ada_b, mix_pre_g, mix_post_g, ffn_pre_g, ffn_post_g,
           pool_w, pool_scale, na_w_qkv, na_w_o, na_rpb, ffn_w_up, ffn_conv_w, ffn_conv_b, ffn_w_down,
           _stop_after=99, _dbg=False):
    f = lambda a: np.asarray(a, dtype=np.float32)
    x, c, ctx, c_ctx = f(x), f(c), f(ctx), f(c_ctx)
    ada_w, ada_b = f(ada_w), f(ada_b)
    key = (_stop_after, _dbg)
    if key not in _CACHE:
        _CACHE[key] = build_program(_stop_after, _dbg)
    nc = _CACHE[key]

    vecs = np.zeros((128, 2, NV), np.float32)
    for i in range(2):
        vecs[:, i, V_PRE:V_PRE + 8] = _vec_cols(f(mix_pre_g)[i])
        vecs[:, i, V_POST:V_POST + 8] = _vec_cols(f(mix_post_g)[i])
        vecs[:, i, V_FPRE:V_FPRE + 8] = _vec_cols(f(ffn_pre_g)[i])
        vecs[:, i, V_FPOST:V_FPOST + 8] = _vec_cols(f(ffn_post_g)[i])
        vecs[:, i, V_PSC:V_PSC + 8] = _vec_cols(f(pool_scale)[0])
        vecs[:, i, V_ADAB:V_ADAB + 48] = _vec_cols(ada_b[i])
        for tap in range(3):
            vecs[:, i, V_CW + tap * 44:V_CW + (tap + 1) * 44] = _vec_cols(f(ffn_conv_w)[i, tap])
        vecs[:, i, V_CB:V_CB + 44] = _vec_cols(f(ffn_conv_b)[i])
    adaw = np.ascontiguousarray(ada_w.reshape(2, 8, 128, 24, 256).transpose(0, 3, 2, 1, 4))
    wpool = np.ascontiguousarray(f(pool_w)[0].reshape(4, 2, 128, 256).transpose(2, 0, 1, 3))
    wup = np.ascontiguousarray(f(ffn_w_up).reshape(2, 8, 128, 2, NJ, 128).transpose(0, 4, 2, 3, 1, 5))
    wdn = np.ascontiguousarray(f(ffn_w_down).reshape(2, NJ, 128, 8, 128).transpose(0, 3, 2, 1, 4))
    wqkv = f(na_w_qkv)[0].reshape(8, 128, 3, 8, 128)
    wq = np.ascontiguousarray(wqkv[:, :, 0].transpose(1, 2, 0, 3))
    wk = np.ascontiguousarray(wqkv[:, :, 1].transpose(1, 2, 0, 3))
    wv = np.ascontiguousarray(wqkv[:, :, 2].reshape(8, 128, 1024).transpose(1, 0, 2))
    wo = np.ascontiguousarray(f(na_w_o)[0].reshape(8, 128, 8, 128).transpose(1, 2, 0, 3))
    rpb = f(na_rpb)[0]

    def corr_tab(top_on, bot_on):
        t = np.ones((4, 16), np.float32)
        for g in range(4):
            w = 2 << g
            for i in range(8):
                if top_on:
                    t[g, i] = w / (w // 2 + min(i, w // 2))
                if bot_on:
                    ip = 7 - i
                    t[g, 8 + i] = w / (w // 2 + min(ip + 1, w // 2))
        return t

    in_maps = []
    tabs_cache = {}
    for core in range(8):
        b, q = core // 4, core % 4
        r0 = q * 32
        xe = np.zeros((NT, D), np.float32)
        g_lo = (r0 - 5) * 64 - PAD
        lo, hi = max(g_lo, 0), min(g_lo + NT, 8192)
        xe[lo - g_lo:hi - g_lo] = x[b, lo:hi]
        ce = np.zeros((NTC, D), np.float32)
        ce[PAD:PAD + CTX] = ctx[b]
        cnd = np.stack([_vec_cols(c[b]), _vec_cols(c_ctx)], axis=-1)
        fl = np.zeros((128, 4), np.float32)
        fl[:, 0] = 0.0 if q == 0 else 1.0
        fl[:, 1] = 0.0 if q == 3 else 1.0
        cr = np.stack([corr_tab(q == 0, q == 3), corr_tab(True, True)], axis=0)
        cr = np.ascontiguousarray(np.broadcast_to(cr[None], (128, 2, 4, 16)))
        if q not in tabs_cache:
            tabs_cache[q] = _bias_tables(rpb, q)
        in_maps.append(dict(xin=_fm(xe), cin=_fm(ce), cond=np.ascontiguousarray(cnd), vecs=vecs, flg=fl, corr=cr,
                            adaw=adaw, wpool=wpool, wup=wup, wdn=wdn, wq=wq, wk=wk, wv=wv, wo=wo,
                            tabL=np.ascontiguousarray(tabs_cache[q][1:4]),
                            tabS=np.ascontiguousarray(tabs_cache[q][[0, 4]][..., :TABSMALL])))
    res = run_bass_kernel_spmd(nc, in_maps, core_ids=list(range(8)))
    if _dbg:
        return res.results
    out = np.zeros((2, 8192, D), np.float32)
    for core in range(8):
        b, q = core // 4, core % 4
        o = res.results[core]["outT"]
        out[b, q * 2048:(q + 1) * 2048] = o.transpose(2, 1, 0).reshape(2048, D)
    return out
```

```python
import numpy as np
from contextlib import ExitStack
import concourse.bass as bass
import concourse.mybir as mybir
from concourse.bass_utils import run_bass_kernel_spmd

F32 = mybir.dt.float32
BF16 = mybir.dt.bfloat16
AF = mybir.ActivationFunctionType
ALU = mybir.AluOpType

D = 1024
NCH = 8
GRID_W = 64
PAD = 16
ROWS_EXT = 42
NT = PAD + ROWS_EXT * 64 + PAD
TOP = PAD + 5 * 64
BOT = TOP + 2048
CTX = 256
NTC = PAD + CTX + PAD
TOPC = PAD
BOTC = PAD + CTX
DFF = 2816
NJ = 22
EPS = 1e-6
MASK = -30000.0
NV = 8 * 5 + 48 + 3 * 44 + 44
V_PRE, V_POST, V_FPRE, V_FPOST, V_PSC, V_ADAB, V_CW, V_CB = 0, 8, 16, 24, 32, 40, 88, 220

AGROUPS = [
    (4, 1, [0, 1, 2, 3], 0),
    (5, 8, list(range(0, 8)), 1),
    (13, 8, list(range(4, 12)), 2),
    (21, 8, list(range(8, 16)), 2),
    (29, 8, list(range(12, 20)), 3),
    (37, 1, [16, 17, 18, 19, 20], 4),
]


def _row_window(gq):
    s0 = min(max(gq - 4, 0), 120)
    return s0, s0 + 8


def _group_colranges():
    out = {}
    for (q0, nr, chunks, typ) in AGROUPS:
        if typ in out:
            continue
        rngs = []
        for c in chunks:
            rows = set()
            for r0 in (0, 32, 96):
                for qr in range(nr):
                    gq = r0 - 5 + q0 + qr
                    if gq < 0 or gq > 127:
                        continue
                    lo, hi = _row_window(gq)
                    for kr2 in range(2):
                        gk = r0 - 5 + 2 * c + kr2
                        if lo <= gk < hi:
                            rows.add(qr)
            if not rows:
                rows = {0}
            rngs.append((min(rows), max(rows) + 1))
        out[typ] = rngs
    return out


COLR = _group_colranges()
TABW = {t: sum((hi - lo) * 64 for lo, hi in COLR[t]) for t in COLR}
TABMAX = max(TABW.values())
TABSMALL = max(TABW[0], TABW[4])


class Res:
    __slots__ = ("name", "w", "r", "cnt", "sem", "nobar")

    def __init__(self, name):
        self.name = name
        self.w = {}
        self.r = {}
        self.cnt = 0
        self.sem = None
        self.nobar = False


class Sched:
    ENG = ("pe", "act", "dve", "pool", "sp")

    def __init__(self):
        self.streams = {e: [] for e in self.ENG}
        self.cnt = {e: 0 for e in self.ENG}
        self.known = {e: {} for e in self.ENG}
        self.slots = []
        self.sw = {}

    def _need(self, eng, toks, waits, raw):
        for key, val in toks.items():
            if isinstance(key, str):
                if key == eng:
                    if eng == "pe" or not raw:
                        continue
            else:
                val = key.cnt
            if self.known[eng].get(key, 0) >= val:
                continue
            if waits.get(key, 0) < val:
                waits[key] = val

    def _deps(self, eng, reads, writes):
        waits = {}
        for r in reads:
            self._need(eng, r.w, waits, True)
        for w in writes:
            self._need(eng, w.w, waits, False)
            self._need(eng, w.r, waits, False)
        for k, v in waits.items():
            self.known[eng][k] = v
        return list(waits.items())

    def _commit(self, key, val, reads, writes):
        for r in reads:
            if r.r.get(key, 0) < val:
                r.r[key] = val
        for w in writes:
            w.w = {key: val}
            w.r = {}

    def op(self, eng, fn, reads=(), writes=(), sig=True):
        waits = self._deps(eng, reads, writes)
        if sig:
            self.cnt[eng] += 1
            val = self.cnt[eng]
        else:
            val = self.cnt[eng] + 1
        self.streams[eng].append((waits, fn, ("eng", eng) if sig else None))
        self._commit(eng, val, reads, writes)

    def dma(self, q, fn, reads, writes, slot):
        if q == "pool":
            key = id(slot)
            if key not in self.sw:
                r = Res(slot.name + "_sw")
                r.nobar = slot.nobar
                self.sw[key] = r
            slot = self.sw[key]
        waits = self._deps(q, reads, writes)
        if slot.cnt == 0:
            self.slots.append(slot)
        slot.cnt += 16
        self.streams[q].append((waits, fn, ("dma", slot)))
        self._commit(slot, slot.cnt, reads, writes)

    def barrier(self, final=False):
        for e in self.ENG:
            waits = {}
            for o in self.ENG[:4]:
                if o != e and self.known[e].get(o, 0) < self.cnt[o]:
                    waits[o] = self.cnt[o]
            for s in self.slots:
                if s.nobar and not final:
                    continue
                if self.known[e].get(s, 0) < s.cnt:
                    waits[s] = s.cnt
            for k, v in waits.items():
                self.known[e][k] = v
            if waits:
                self.streams[e].append((list(waits.items()), None, None))

    def emit(self, nc, es):
        sems = {}
        for e in self.ENG[:4]:
            sems[e] = es.enter_context(nc.semaphore("sem_" + e))
        for s in self.slots:
            s.sem = es.enter_context(nc.semaphore("sd_" + s.name))
        block = es.enter_context(nc.Block())

        def run(engname):
            def body(eng):
                for waits, fn, inc in self.streams[engname]:
                    for key, val in waits:
                        eng.wait_ge(sems[key] if isinstance(key, str) else key.sem, val)
                    if fn is None:
                        continue
                    inst = fn(eng)
                    if inc is not None:
                        if inc[0] == "eng":
                            inst.then_inc(sems[inc[1]], 1)
                        else:
                            inst.then_inc(inc[1].sem, 16)
            return body

        block.tensor(run("pe"))
        block.scalar(run("act"))
        block.vector(run("dve"))
        block.gpsimd(run("pool"))
        block.sync(run("sp"))


class Arena:
    def __init__(self, nc, base, top):
        self.nc = nc
        self.off = base
        self.top = top
        self.n = 0

    def alloc(self, name, shape, dtype):
        per = 1
        for s in shape[1:]:
            per *= s
        nbytes = per * (2 if dtype == BF16 else 4)
        off = (self.off + 63) // 64 * 64
        assert off + nbytes <= self.top, f"SBUF overflow at {name}: {off + nbytes} > {self.top}"
        self.off = off + nbytes
        self.hw = max(getattr(self, 'hw', 0), self.off)
        self.hwlog = getattr(self, 'hwlog', {})
        self.hwlog[name] = self.off
        self.n += 1
        return self.nc.alloc_sbuf_tensor_at(f"{name}_{self.n}", list(shape), dtype, offset=off)


def build_program(stop_after=99, dbg=False):
    nc = bass.Bass("TRN2", target_bir_lowering=False)
    S = Sched()

    def din(name, shape, dt=F32):
        return nc.dram_tensor(name, list(shape), dt, kind="ExternalInput")

    xin = din("xin", [128, NCH, NT])
    cin = din("cin", [128, NCH, NTC])
    cond = din("cond", [128, NCH, 2])
    vecs = din("vecs", [128, 2, NV])
    flg = din("flg", [128, 4])
    corr = din("corr", [128, 2, 4, 16])
    adaw = din("adaw", [2, 24, 128, 8, 256])
    wpool = din("wpool", [128, 4, 2, 256])
    wup = din("wup", [2, NJ, 128, 2, 8, 128])
    wdn = din("wdn", [2, 8, 128, NJ, 128])
    wq = din("wq", [128, 8, 8, 128])
    wk = din("wk", [128, 8, 8, 128])
    wv = din("wv", [128, 8, 1024])
    wo = din("wo", [128, 8, 8, 128])
    tabL = din("tabL", [3, 16, 128, TABMAX])
    tabS = din("tabS", [2, 16, 128, TABSMALL])
    outT = nc.dram_tensor("outT", [128, NCH, 2048], F32, kind="ExternalOutput")

    xsA = nc.dram_tensor("xsA", [128, NCH, NT], F32)
    xsB = nc.dram_tensor("xsB", [128, NCH, NT], F32)
    csA = nc.dram_tensor("csA", [128, NCH, NTC], F32)
    csB = nc.dram_tensor("csB", [128, NCH, NTC], F32)
    wupb = nc.dram_tensor("wupb", [2, NJ, 128, 2, 8, 128], BF16)
    wdnb = nc.dram_tensor("wdnb", [2, 8, 128, NJ, 128], BF16)
    wqb = nc.dram_tensor("wqb", [128, 8, 8, 128], BF16)
    wkb = nc.dram_tensor("wkb", [128, 8, 8, 128], BF16)
    wvb = nc.dram_tensor("wvb", [128, 8, 1024], BF16)
    wob = nc.dram_tensor("wob", [128, 8, 8, 128], BF16)
    kts = nc.dram_tensor("kts", [128, NCH, 21 * 128], BF16)
    vts = nc.dram_tensor("vts", [128, 21, 1024], BF16)
    if dbg:
        dbgx = nc.dram_tensor("dbgx", [128, NCH, NT], F32, kind="ExternalOutput")
        dbgc = nc.dram_tensor("dbgc", [128, NCH, NTC], F32, kind="ExternalOutput")

    R_xin, R_cin, R_xsA, R_xsB, R_csA, R_csB = (Res(n) for n in ("xin", "cin", "xsA", "xsB", "csA", "csB"))
    R_wupb, R_wdnb, R_kts, R_vts, R_out, R_const = (Res(n) for n in ("wupb", "wdnb", "kts", "vts", "out", "const"))

    ar = Arena(nc, 16512, 229344)
    RESC = {}
    es = ExitStack()

    def sb(name, shape, dt=F32, nres=1):
        t = ar.alloc(name, shape, dt)
        rs = [RESC.setdefault(f"{name}_{i}", Res(f"{name}_{i}")) for i in range(nres)]
        return t, (rs[0] if nres == 1 else rs)

    PS = []
    for i in range(8):
        PS.append((es.enter_context(nc.psum_tensor(f"ps{i}", [128, 512], F32)), Res(f"ps{i}")))

    VEC, R_VEC = sb("VEC", [128, 2, NV])
    SC, R_SC = sb("SC", [128, NCH, 2])
    SG0, R_SG0 = sb("SG0", [128, NCH, 2])
    MODS, R_MODS = sb("MODS", [128, 2, 48, 2])
    DER, R_DER = sb("DER", [128, 2, 2, 6, 8])
    FLG, R_FLG = sb("FLG", [128, 4])
    CORR, R_CORR = sb("CORR", [128, 2, 4, 16])
    ONES, R_ONES = sb("ONES", [128, 128], BF16)
    WP, R_WP = sb("WP", [128, 4, 2, 256], BF16)
    XT, R_XT = sb("XT", [128, NCH, 512])
    YS, R_YS = sb("YS", [128, NCH, 512], F32, nres=NCH)
    DB, R_DB = sb("DB", [128, NCH, 512], BF16, nres=NCH)
    RS = [sb(f"RS{i}", [128, 512]) for i in range(2)]
    SQ = [sb(f"SQ{i}", [128, 512], BF16) for i in range(3)]
    TMP = [sb(f"TMP{i}", [128, 512]) for i in range(3)]
    base_mark = ar.off

    rot = {}

    def nxt(lst, key):
        i = rot.get(key, 0)
        rot[key] = i + 1
        return lst[i % len(lst)]

    def der(layer, cnd, which):
        return lambda c: DER[:, layer, cnd, which, c:c + 1]

    S.dma("sp", lambda e: e.dma_start(out=VEC[:], in_=vecs[:]), [R_const], [R_VEC], R_VEC)
    S.dma("sp", lambda e: e.dma_start(out=SC[:], in_=cond[:]), [R_const], [R_SC], R_SC)
    S.dma("sp", lambda e: e.dma_start(out=FLG[:], in_=flg[:]), [R_const], [R_FLG], R_FLG)
    S.dma("sp", lambda e: e.dma_start(out=CORR[:], in_=corr[:]), [R_const], [R_CORR], R_CORR)
    S.dma("pool", lambda e: e.dma_start(out=WP[:], in_=wpool[:], max_dma_last_dim=4096), [R_const], [R_WP], R_WP)
    S.op("dve", lambda e: e.memset(ONES[:], 1.0), [], [R_ONES])
    LANES = [Res(f"lane{i}") for i in range(8)]
    for ln in LANES:
        ln.nobar = True
    R_wupc = [[Res(f"wupc{i}_{j}") for j in range(NJ)] for i in range(2)]
    R_wdnc = [[Res(f"wdnc{i}_{m}") for m in range(8)] for i in range(2)]
    kk = 0
    for i in range(2):
        for j in range(NJ):
            ln = LANES[kk % 8]
            S.dma("pool", lambda e, i=i, j=j: e.dma_start(out=wupb[i, j], in_=wup[i, j], max_dma_last_dim=4096), [R_const], [R_wupc[i][j], ln], ln)
            kk += 1
        for m in range(8):
            ln = LANES[kk % 8]
            S.dma("pool", lambda e, i=i, m=m: e.dma_start(out=wdnb[i, m], in_=wdn[i, m], max_dma_last_dim=4096), [R_const], [R_wdnc[i][m], ln], ln)
            kk += 1
    R_attw = {n: Res("attw_" + n) for n in ("q", "k", "v", "o")}
    for nm, srcw, dstw in (("k", wk, wkb), ("v", wv, wvb), ("q", wq, wqb), ("o", wo, wob)):
        ln = LANES[kk % 8]
        S.dma("pool", lambda e, srcw=srcw, dstw=dstw: e.dma_start(out=dstw[:], in_=srcw[:], max_dma_last_dim=4096), [R_const], [R_attw[nm], ln], ln)
        kk += 1
    S.op("act", lambda e: e.activation(out=SG0[:], in_=SC[:], func=AF.Sigmoid), [R_SC], [R_SG0])
    S.op("dve", lambda e: e.tensor_tensor(out=SC[:], in0=SC[:], in1=SG0[:], op=ALU.mult), [R_SC, R_SG0], [R_SC])

    EPSB, R_EPSB = sb("EPSB", [128, 1])
    S.op("dve", lambda e: e.memset(EPSB[:], EPS), [], [R_EPSB])
    m1 = ar.off
    WA = [sb(f"WA{i}", [128, 8, 256]) for i in range(4)]
    PMB = {0: 0, 1: 3}
    wa_of = {}

    def mods_dma(layer, g):
        wa, R_wa = nxt(WA, "wa")
        wa_of[(layer, g)] = (wa, R_wa)
        S.dma("act" if layer == 1 else "sp", lambda e: e.dma_start(out=wa[:], in_=adaw[layer, g]), [R_const], [R_wa], R_wa)

    def mods_mm(layer, g):
        pm, R_pm = PS[PMB[layer]]
        wa, R_wa = wa_of[(layer, g)]
        for mm in range(2):
            m = g * 2 + mm
            for k in range(8):
                S.op("pe", lambda e, m=m, mm=mm, k=k: e.matmul(
                    pm[:, 2 * m:2 * m + 2], lhsT=wa[:, k, mm * 128:(mm + 1) * 128], rhs=SC[:, k, :],
                    start=(k == 0), stop=(k == 7)), [R_wa, R_SC], [R_pm], sig=True)

    def mods_finish(layer):
        pm, R_pm = PS[PMB[layer]]
        for cnd in range(2):
            S.op("dve", lambda e, pm=pm, layer=layer, cnd=cnd: e.tensor_tensor(
                out=MODS[:, layer, :, cnd], in0=pm[:, cnd:96:2], in1=VEC[:, layer, V_ADAB:V_ADAB + 48], op=ALU.add),
                [R_pm, R_VEC], [R_MODS])
            def mod(k0, layer=layer, cnd=cnd):
                return MODS[:, layer, k0:k0 + 8, cnd]
            for which, (sc0, gv) in ((0, (8, V_PRE)), (3, (32, V_FPRE))):
                S.op("dve", lambda e, which=which, sc0=sc0, gv=gv, layer=layer, cnd=cnd, mod=mod: e.scalar_tensor_tensor(
                    out=DER[:, layer, cnd, which, :], in0=mod(sc0), scalar=1.0, in1=VEC[:, layer, gv:gv + 8],
                    op0=ALU.add, op1=ALU.mult), [R_MODS, R_VEC], [R_DER])
            for which, sh0 in ((1, 0), (4, 24)):
                S.op("dve", lambda e, which=which, sh0=sh0, layer=layer, cnd=cnd, mod=mod: e.tensor_copy(
                    out=DER[:, layer, cnd, which, :], in_=mod(sh0)), [R_MODS], [R_DER])
            for which, (g0, gv) in ((2, (16, V_POST)), (5, (40, V_FPOST))):
                S.op("dve", lambda e, which=which, g0=g0, gv=gv, layer=layer, cnd=cnd, mod=mod: e.tensor_tensor(
                    out=DER[:, layer, cnd, which, :], in0=mod(g0), in1=VEC[:, layer, gv:gv + 8], op=ALU.mult),
                    [R_MODS, R_VEC], [R_DER])

    for g in range(3):
        mods_dma(0, g)
    for g in range(24):
        if g + 3 < 24:
            mods_dma(0, g + 3)
        mods_mm(0, g)
    mods_finish(0)
    mods_l1_dma = list(range(24))
    mods_l1_mm = []

    def rstd_from(ps_ss, R_ss, W):
        rs, R_rs = nxt(RS, "rs")
        S.op("act", lambda e: e.activation(out=rs[:, :W], in_=ps_ss[:, :W], func=AF.Sqrt, bias=EPSB[:, 0:1], scale=1.0 / D),
             [R_ss, R_EPSB], [R_rs])
        S.op("dve", lambda e: e.reciprocal(out=ps_ss[:, :W], in_=rs[:, :W]), [R_rs], [R_ss])
        return ps_ss, R_ss

    def sumsq(src_fn, reads, W, bank):
        ps, R_ps = PS[bank]
        for c in range(NCH):
            sq, R_sq = nxt(SQ, "sq")
            S.op("act", lambda e, c=c, sq=sq: e.activation(out=sq[:, :W], in_=src_fn(c), func=AF.Square), reads(c), [R_sq])
            S.op("pe", lambda e, c=c, sq=sq: e.matmul(ps[:, :W], lhsT=ONES[:], rhs=sq[:, :W], start=(c == 0), stop=(c == 7)),
                 [R_sq, R_ONES], [R_ps], sig=True)
        return ps, R_ps

    def load_x(src, R_src, a, W, xt=None):
        X_, R_X = xt or (XT, R_XT)
        S.dma("sp", lambda e: e.dma_start(out=X_[:, :, 0:W], in_=src[:, :, a:a + W]), [R_src], [R_X], R_X)

    def prenorm(W, A, SH, out_fn, out_res, xt=None):
        X_, R_X = xt or (XT, R_XT)
        ps, R_ps = sumsq(lambda c: X_[:, c, 0:W], lambda c: [R_X], W, 7)
        rs, R_rs = rstd_from(ps, R_ps, W)
        for c in range(NCH):
            tmp, R_tmp = nxt(TMP, "tmp")
            S.op("dve", lambda e, c=c, tmp=tmp: e.scalar_tensor_tensor(
                out=tmp[:, :W], in0=X_[:, c, 0:W], scalar=A(c), in1=rs[:, :W], op0=ALU.mult, op1=ALU.mult),
                [R_X, R_rs, R_DER], [R_tmp])
            S.op("act", lambda e, c=c, tmp=tmp: e.activation(out=out_fn(c), in_=tmp[:, :W], func=AF.Identity, bias=SH(c), scale=1.0),
                 [R_tmp, R_DER], [out_res(c)])

    def post_chunk_evac(m, ps, R_ps, n, scale, ys=None):
        Y_, R_Y = ys or (YS, R_YS)
        if scale is None:
            S.op("act", lambda e: e.activation(out=Y_[:, m, :n], in_=ps[:, :n], func=AF.Copy), [R_ps], [R_Y[m]])
        else:
            S.op("act", lambda e: e.activation(out=Y_[:, m, :n], in_=ps[:, :n], func=AF.Identity, scale=scale(m)),
                 [R_ps, R_VEC], [R_Y[m]])

    def post_finish(n, G, x_off, dst, R_dst, dcol, xt=None, ys=None):
        X_, R_X = xt or (XT, R_XT)
        Y_, R_Y = ys or (YS, R_YS)
        ps, R_ps = sumsq(lambda c: Y_[:, c, :n], lambda c: [R_Y[c]], n, 6)
        rs, R_rs = rstd_from(ps, R_ps, n)
        for c in range(NCH):
            tmp, R_tmp = nxt(TMP, "tmp")
            S.op("dve", lambda e, c=c, tmp=tmp: e.scalar_tensor_tensor(
                out=tmp[:, :n], in0=Y_[:, c, :n], scalar=G(c), in1=rs[:, :n], op0=ALU.mult, op1=ALU.mult),
                [R_Y[c], R_rs, R_DER], [R_tmp])
            S.op("pool", lambda e, c=c, tmp=tmp: e.tensor_tensor(
                out=X_[:, c, x_off:x_off + n], in0=X_[:, c, x_off:x_off + n], in1=tmp[:, :n], op=ALU.add),
                [R_X, R_tmp], [R_X])
        S.dma("pool", lambda e: e.dma_start(out=dst[:, :, dcol:dcol + n], in_=X_[:, :, x_off:x_off + n]), [R_X], [R_dst], R_X)

    def flag_cols(buf_fn, R_buf, a, W, lo, hi, fcol):
        l, h = max(lo, a), min(hi, a + W)
        if l >= h:
            return
        S.op("dve", lambda e: e.tensor_scalar(out=buf_fn(l - a, h - a), in0=buf_fn(l - a, h - a),
                                                scalar1=FLG[:, fcol:fcol + 1], scalar2=None, op0=ALU.mult),
             [R_buf, R_FLG], [R_buf])

    def pool_phase(src, R_src, dst, R_dst, ntok, top, bot, cnd, fcol, cidx, with_mods=False, bar=True):
        m0 = ar.off
        Us = [sb(f"U{i}", [128, NCH, 512], F32, nres=NCH) for i in range(2)]
        XTs = [(XT, R_XT), sb("XTb", [128, NCH, 512])]
        YSs = [(YS, R_YS), sb("YSb", [128, NCH, 512], F32, nres=NCH)]
        DBs = [(DB, R_DB), sb("DBb", [128, NCH, 512], BF16, nres=NCH)]
        T = [sb(f"T{i}", [128, 512]) for i in range(6)]
        lo_out, hi_out = 8, ntok - 8
        ntile = -(-(hi_out - lo_out) // 496)
        tsz = -(-(hi_out - lo_out) // ntile)

        def mods_hook(k):
            if with_mods:
                for _ in range(k):
                    if mods_l1_mm:
                        mods_mm(1, mods_l1_mm.pop(0))

        def do_tile(a, b, par):
            U, R_U = Us[par]
            xt = XTs[par]
            ys = YSs[par]
            DBp, R_DBp = DBs[par]
            n = b - a
            W = n + 16
            a0 = a - 8
            load_x(src, R_src, a0, W, xt=xt)
            prenorm(W, der(0, cnd, 0), der(0, cnd, 1), lambda c: U[:, c, 0:W], lambda c: R_U[c], xt=xt)
            mods_hook(2)
            for c in range(NCH):
                def ub(l, h, c=c):
                    return U[:, c, l:h]
                flag_cols(ub, R_U[c], a0, W, top - 8, top, fcol)
                flag_cols(ub, R_U[c], a0, W, bot, bot + 8, fcol + 1)
            for c in range(NCH):
                g = c // 2
                w = 2 << g
                t, R_t = nxt(T, "T")
                S.op("pool", lambda e, t=t, c=c: e.tensor_tensor(out=t[:, 1:W], in0=U[:, c, 1:W], in1=U[:, c, 0:W - 1], op=ALU.add),
                     [R_U[c]], [R_t])
                cur, R_cur = t, R_t
                vlo, vhi = 1, W
                sh = 1
                for lvl in range(g):
                    t2, R_t2 = nxt(T, "T")
                    nlo, nhi = vlo + sh, vhi - sh
                    S.op("pool", lambda e, t2=t2, cur=cur, nlo=nlo, nhi=nhi, sh=sh: e.tensor_tensor(
                        out=t2[:, nlo:nhi], in0=cur[:, nlo + sh:nhi + sh], in1=cur[:, nlo - sh:nhi - sh], op=ALU.add),
                        [R_cur], [R_t2])
                    cur, R_cur = t2, R_t2
                    vlo, vhi = nlo, nhi
                    sh *= 2
                assert vlo <= 8 and vhi >= W - 8, (vlo, vhi, W)
                for (l0, h0, off) in ((top, top + 8, 0), (bot - 8, bot, 8)):
                    l, h = max(l0, a), min(h0, b)
                    if l < h:
                        S.op("dve", lambda e, cur=cur, l=l, h=h, l0=l0, off=off, g=g: e.tensor_tensor(
                            out=cur[:, l - a0:h - a0], in0=cur[:, l - a0:h - a0],
                            in1=CORR[:, cidx, g, off + l - l0:off + h - l0], op=ALU.mult), [R_cur, R_CORR], [R_cur])
                S.op("dve", lambda e, cur=cur, c=c, w=w: e.scalar_tensor_tensor(
                    out=DBp[:, c, 0:n], in0=cur[:, 8:8 + n], scalar=1.0 / w, in1=U[:, c, 8:8 + n],
                    op0=ALU.mult, op1=ALU.subtract), [R_cur, R_U[c]], [R_DBp[c]])
            for m in range(NCH):
                g, ml = m // 2, m % 2
                ps, R_ps = PS[m % 2]
                for kc in range(2):
                    S.op("pe", lambda e, ps=ps, g=g, ml=ml, kc=kc: e.matmul(
                        ps[:, :n], lhsT=WP[:, g, kc, ml * 128:(ml + 1) * 128], rhs=DBp[:, 2 * g + kc, 0:n],
                        start=(kc == 0), stop=(kc == 1)), [R_WP, R_DBp[2 * g + kc]], [R_ps], sig=True)
                post_chunk_evac(m, ps, R_ps, n, lambda m: VEC[:, 0, V_PSC + m:V_PSC + m + 1], ys=ys)
            post_finish(n, der(0, cnd, 2), 8, dst, R_dst, a, xt=xt, ys=ys)
            mods_hook(2)

        a = lo_out
        ti = 0
        while a < hi_out:
            b = min(a + tsz, hi_out)
            do_tile(a, b, ti % 2)
            if with_mods:
                while mods_l1_mm:
                    mods_mm(1, mods_l1_mm.pop(0))
                for _ in range(4):
                    if mods_l1_dma:
                        g = mods_l1_dma.pop(0)
                        mods_dma(1, g)
                        mods_l1_mm.append(g)
            a = b
            ti += 1
        if with_mods:
            while mods_l1_mm or mods_l1_dma:
                while mods_l1_mm:
                    mods_mm(1, mods_l1_mm.pop(0))
                for _ in range(4):
                    if mods_l1_dma:
                        g = mods_l1_dma.pop(0)
                        mods_dma(1, g)
                        mods_l1_mm.append(g)
            mods_finish(1)
        if bar:
            S.barrier()
        ar.off = m0

    def ffn_phase(layer, src, R_src, dst, R_dst, lo_out, hi_out, top, bot, cnd, fcol, dst_shift, bar=True):
        m0 = ar.off
        NST = 2
        UBs = [sb(f"UB{i}", [128, NCH, NST * 512], BF16, nres=NST) for i in range(2)]
        G, R_G = sb("G", [128, NJ, NST * 448], BF16, nres=NST)
        WU = [sb(f"WU{i}", [128, 2, 8, 128], BF16) for i in range(2)]
        WD = [sb(f"WD{i}", [128, NJ, 128], BF16) for i in range(2)]
        FT = [sb(f"FT{i}", [128, 512]) for i in range(12)]
        XTn = sb("XTn", [128, NCH, 512])
        tot = hi_out - lo_out
        ntile = -(-tot // 448)
        tsz = -(-tot // ntile)
        tiles = []
        a = lo_out
        while a < hi_out:
            b = min(a + tsz, hi_out)
            tiles.append((a, b))
            a = b
        sts = [tiles[i:i + NST] for i in range(0, len(tiles), NST)]

        def norm_tile(si, ti):
            a, b = sts[si][ti]
            UB, R_UB = UBs[si % 2]
            n = b - a
            W = n + 2
            load_x(src, R_src, a - 1, W, xt=XTn)
            prenorm(W, der(layer, cnd, 3), der(layer, cnd, 4),
                    lambda c: UB[:, c, ti * 512:ti * 512 + W], lambda c: R_UB[ti], xt=XTn)

            def ubf(l, h):
                return UB[:, :, ti * 512 + l:ti * 512 + h]
            flag_cols(ubf, R_UB[ti], a - 1, W, top - 1, top, fcol)
            flag_cols(ubf, R_UB[ti], a - 1, W, bot, bot + 1, fcol + 1)

        pend = []

        def stage3(ag, R_ag, av, R_av, n, j, ti):
            p, R_p = nxt(FT, "ft")
            S.op("act", lambda e: e.activation(out=p[:, :n], in_=ag[:, :n], func=AF.Gelu_apprx_tanh), [R_ag], [R_p])
            S.op("pool", lambda e: e.tensor_tensor(out=G[:, j, ti * 448:ti * 448 + n], in0=p[:, :n], in1=av[:, :n], op=ALU.mult),
                 [R_p, R_av], [R_G[ti]])

        def stage12(si, j, wu, R_wu, ti):
            a, b = sts[si][ti]
            UB, R_UB = UBs[si % 2]
            n = b - a
            W = n + 2
            hp = []
            for part in range(2):
                ps, R_ps = PS[nxt([0, 1, 2, 3, 4, 5], "ffnps")]
                for k in range(8):
                    S.op("pe", lambda e, ps=ps, part=part, k=k: e.matmul(
                        ps[:, :W], lhsT=wu[:, part, k, :], rhs=UB[:, k, ti * 512:ti * 512 + W],
                        start=(k == 0), stop=(k == 7)), [R_wu, R_UB[ti]], [R_ps], sig=True)
                hp.append((ps, R_ps))
            conv = []
            for part in range(2):
                ps, R_ps = hp[part]
                ch = part * NJ + j
                acc, R_acc = nxt(FT, "ft")
                S.op("act", lambda e, ps=ps, acc=acc, ch=ch: e.activation(
                    out=acc[:, :n], in_=ps[:, 1:n + 1], func=AF.Identity,
                    bias=VEC[:, layer, V_CB + ch:V_CB + ch + 1], scale=VEC[:, layer, V_CW + 44 + ch:V_CW + 44 + ch + 1]),
                    [R_ps, R_VEC], [R_acc])
                for tap, off in ((0, 0), (2, 2)):
                    S.op("dve", lambda e, ps=ps, acc=acc, ch=ch, tap=tap, off=off: e.scalar_tensor_tensor(
                        out=acc[:, :n], in0=ps[:, off:off + n], scalar=VEC[:, layer, V_CW + tap * 44 + ch:V_CW + tap * 44 + ch + 1],
                        in1=acc[:, :n], op0=ALU.mult, op1=ALU.add), [R_ps, R_acc, R_VEC], [R_acc])
                conv.append((acc, R_acc))
            (ag, R_ag), (av, R_av) = conv
            pend.append((ag, R_ag, av, R_av, n, j, ti))
            if len(pend) > 1:
                stage3(*pend.pop(0))

        def down_tile(si, ti):
            a, b = sts[si][ti]
            n = b - a
            for m in range(NCH):
                wd, R_wd = nxt(WD, "wd")
                S.dma("sp", lambda e, wd=wd, m=m: e.dma_start(out=wd[:], in_=wdnb[layer, m]), [R_wdnc[layer][m]], [R_wd], R_wd)
                ps, R_ps = PS[m % 4]
                for j in range(NJ):
                    S.op("pe", lambda e, ps=ps, wd=wd, j=j: e.matmul(
                        ps[:, :n], lhsT=wd[:, j, :], rhs=G[:, j, ti * 448:ti * 448 + n],
                        start=(j == 0), stop=(j == NJ - 1)), [R_wd, R_G[ti]], [R_ps], sig=True)
                post_chunk_evac(m, ps, R_ps, n, None)
            load_x(src, R_src, a, n)
            if si + 1 < len(sts) and ti < len(sts[si + 1]):
                norm_tile(si + 1, ti)
            post_finish(n, der(layer, cnd, 5), 0, dst, R_dst, a - dst_shift)

        for ti in range(len(sts[0])):
            norm_tile(0, ti)
        for si, st in enumerate(sts):
            for j in range(NJ):
                wu, R_wu = nxt(WU, "wu")
                S.dma("sp", lambda e, wu=wu, j=j: e.dma_start(out=wu[:], in_=wupb[layer, j]), [R_wupc[layer][j]], [R_wu], R_wu)
                for ti in range(len(st)):
                    stage12(si, j, wu, R_wu, ti)
            while pend:
                stage3(*pend.pop(0))
            nxt_n = len(sts[si + 1]) if si + 1 < len(sts) else 0
            for ti in range(max(len(st), nxt_n)):
                if ti < len(st):
                    down_tile(si, ti)
                elif ti < nxt_n:
                    norm_tile(si + 1, ti)
        if bar:
            S.barrier()
        ar.off = m0

    if stop_after >= 1:
        pool_phase(xin, R_xin, xsA, R_xsA, NT, TOP, BOT, 0, 0, 0, with_mods=True, bar=False)
        pool_phase(cin, R_cin, csA, R_csA, NTC, TOPC, BOTC, 1, 2, 1)
    ar.off = m1
    if stop_after >= 2:
        ffn_phase(0, xsA, R_xsA, xsB, R_xsB, PAD, NT - PAD, TOP, BOT, 0, 0, 0, bar=False)
        ffn_phase(0, csA, R_csA, csB, R_csB, TOPC, BOTC, TOPC, BOTC, 1, 2, 0)

    def kv_phase():
        m0 = ar.off
        WK, R_WK = sb("WK", [128, 8, 8, 128], BF16)
        WV, R_WV = sb("WV", [128, 8, 1024], BF16)
        KOs = [sb(f"KO{i}", [128, NCH, 512], BF16) for i in range(2)]
        VOs = [sb(f"VO{i}", [128, 4, 1024], BF16) for i in range(2)]
        XTs = [(XT, R_XT), sb("XTk", [128, NCH, 512])]
        DBs = [(DB, R_DB), sb("DBk", [128, NCH, 512], BF16, nres=NCH)]
        S.dma("sp", lambda e: e.dma_start(out=WK[:], in_=wkb[:]), [R_attw["k"]], [R_WK], R_WK)
        S.dma("sp", lambda e: e.dma_start(out=WV[:], in_=wvb[:]), [R_attw["v"]], [R_WV], R_WV)

        def kv_tile(src, R_src, a, n, cnd, par):
            xt = XTs[par]
            DBp, R_DBp = DBs[par]
            KO, R_KO = KOs[par]
            VO, R_VO = VOs[par]
            load_x(src, R_src, a, n, xt=xt)
            prenorm(n, der(1, cnd, 0), der(1, cnd, 1), lambda c: DBp[:, c, 0:n], lambda c: R_DBp[c], xt=xt)
            for m in range(NCH):
                ps, R_ps = PS[m % 2]
                for k in range(8):
                    S.op("pe", lambda e, ps=ps, m=m, k=k: e.matmul(ps[:, :n], lhsT=WK[:, m, k, :], rhs=DBp[:, k, 0:n],
                                                                  start=(k == 0), stop=(k == 7)),
                         [R_WK, R_DBp[k]], [R_ps], sig=True)
                S.op("act", lambda e, ps=ps, m=m: e.activation(out=KO[:, m, :n], in_=ps[:, :n], func=AF.Copy), [R_ps], [R_KO])
            for sub in range(n // 128):
                for half in range(2):
                    ps, R_ps = PS[2 + nxt([0, 1, 2, 3], "kvps")]
                    for k in range(8):
                        S.op("pe", lambda e, ps=ps, k=k, sub=sub, half=half: e.matmul(
                            ps[:, :], lhsT=DBp[:, k, sub * 128:(sub + 1) * 128], rhs=WV[:, k, half * 512:(half + 1) * 512],
                            start=(k == 0), stop=(k == 7)), [R_WV, R_DBp[k]], [R_ps], sig=True)
                    if half == 0:
                        S.op("dve", lambda e, ps=ps, sub=sub, half=half: e.tensor_copy(out=VO[:, sub, half * 512:(half + 1) * 512], in_=ps[:, :]),
                             [R_ps], [R_VO])
                    else:
                        S.op("act", lambda e, ps=ps, sub=sub, half=half: e.activation(out=VO[:, sub, half * 512:(half + 1) * 512], in_=ps[:, :], func=AF.Copy),
                             [R_ps], [R_VO])
            if cnd == 0:
                tk = a - PAD
                S.dma("pool", lambda e: e.dma_start(out=kts[:, :, tk:tk + n], in_=KO[:, :, 0:n]), [R_KO], [R_kts], R_KO)
                S.dma("pool", lambda e: e.dma_start(out=vts[:, tk // 128:tk // 128 + n // 128, :], in_=VO[:, 0:n // 128, :]),
                      [R_VO], [R_vts], R_VO)
            else:
                S.op("act", lambda e: e.activation(out=KCT[:, :, 0:n], in_=KO[:, :, 0:n], func=AF.Copy), [R_KO], [R_KCT])
                S.op("dve", lambda e: e.tensor_copy(out=VC[:, 0:n // 128, :], in_=VO[:, 0:n // 128, :]), [R_VO], [R_VC])

        ti = 0
        for (src, R_src, t0, tend, cnd) in ((xsB, R_xsB, PAD, NT - PAD, 0), (csB, R_csB, TOPC, BOTC, 1)):
            a = t0
            while a < tend:
                n = min(512, tend - a)
                kv_tile(src, R_src, a, n, cnd, ti % 2)
                ti += 1
                a += n
        S.barrier()
        ar.off = m0

    if stop_after >= 3:
        KCT, R_KCT = sb("KCT", [128, NCH, 256], BF16)
        VC, R_VC = sb("VC", [128, 2, 1024], BF16)
        kv_phase()

    def attn_phase():
        m0 = ar.off
        WQ, R_WQ = sb("WQ", [128, 8, 8, 128], BF16)
        WO, R_WO = sb("WO", [128, 8, 8, 128], BF16)
        KT, R_KT = sb("KT", [128, NCH, 1024], BF16)
        VT, R_VT = sb("VT", [128, 8, 1024], BF16)
        QT, R_QT = sb("QT", [128, 2, NCH, 512], BF16)
        OT, R_OT = sb("OT", [128, NCH, 512], BF16, nres=NCH)
        TB = [sb(f"TB{i}", [128, TABMAX]) for i in range(2)]
        SBF = [sb(f"SBF{i}", [128, 512]) for i in range(3)]
        EB = [sb(f"EB{i}", [128, 512], BF16) for i in range(4)]
        RD, R_RD = sb("RD", [128, 512])
        S.dma("sp", lambda e: e.dma_start(out=WQ[:], in_=wqb[:]), [R_attw["q"]], [R_WQ], R_WQ)
        S.op("pool", lambda e: e.memset(QT[:], 0.0), [], [R_QT])
        S.dma("sp", lambda e: e.dma_start(out=WO[:], in_=wob[:]), [R_attw["o"]], [R_WO], R_WO)
        XTq = [(XT, R_XT), sb("XTq", [128, NCH, 512])]
        gcount = [0]

        def do_group(q0, nr, chunks, typ):
            nq = nr * 64
            t0 = PAD + q0 * 64
            c0 = chunks[0]
            nck = len(chunks)
            xt = XTq[gcount[0] % 2]
            gcount[0] += 1
            load_x(xsB, R_xsB, t0, nq, xt=xt)
            S.dma("sp", lambda e, c0=c0, nck=nck: e.dma_start(out=KT[:, :, 0:nck * 128], in_=kts[:, :, c0 * 128:(c0 + nck) * 128]),
                  [R_kts], [R_KT], R_KT)
            S.dma("sp", lambda e, c0=c0, nck=nck: e.dma_start(out=VT[:, 0:nck, :], in_=vts[:, c0:c0 + nck, :]), [R_vts], [R_VT], R_VT)
            prenorm(nq, der(1, 0, 0), der(1, 0, 1), lambda c: DB[:, c, 0:nq], lambda c: R_DB[c], xt=xt)
            for m in range(NCH):
                ps, R_ps = PS[m % 2]
                for k in range(8):
                    S.op("pe", lambda e, ps=ps, m=m, k=k: e.matmul(ps[:, :nq], lhsT=WQ[:, m, k, :], rhs=DB[:, k, 0:nq],
                                                                  start=(k == 0), stop=(k == 7)), [R_WQ, R_DB[k]], [R_ps], sig=True)
                S.op("act", lambda e, ps=ps, m=m: e.activation(out=QT[0:64, 0, m, :nq], in_=ps[0:64, :nq], func=AF.Copy, scale=0.125), [R_ps], [R_QT])
                S.op("act", lambda e, ps=ps, m=m: e.activation(out=QT[64:128, 1, m, :nq], in_=ps[64:128, :nq], func=AF.Copy, scale=0.125), [R_ps], [R_QT])
            colr = COLR[typ]
            items = []
            tab_loaders = {}

            def make_head(h):
                m, pb = h // 2, (h % 2) * 64
                po, R_po = PS[4 + (h % 2)]
                pd, R_pd = PS[6 + (h % 2)]
                hs = {}
                nit = 2 + len(chunks)

                def load_tab():
                    if "tb" in hs:
                        return
                    tb, R_tb = nxt(TB, "tb")
                    hs["tb"] = (tb, R_tb)
                    S.dma("sp", lambda e: e.dma_start(out=tb[:, 0:TABW[typ]], in_=(tabL[typ - 1, h, :, 0:TABW[typ]] if typ in (1, 2, 3)
                                                                                  else tabS[typ // 4, h, :, 0:TABW[typ]])),
                          [R_const], [R_tb], R_tb)

                tab_loaders[h] = load_tab

                def end_fn():
                    S.op("act", lambda e: e.activation(out=RD[pb:pb + 64, :nq], in_=pd[pb:pb + 64, :nq], func=AF.Ln), [R_pd], [R_RD])
                    S.op("act", lambda e: e.activation(out=RD[pb:pb + 64, :nq], in_=RD[pb:pb + 64, :nq], func=AF.Exp, scale=-1.0), [R_RD], [R_RD])
                    S.op("dve", lambda e: e.tensor_tensor(out=OT[pb:pb + 64, m, :nq], in0=po[pb:pb + 64, :nq],
                                                          in1=RD[pb:pb + 64, :nq], op=ALU.mult), [R_po, R_RD], [R_OT[m]])

                def mk_item(idx, kind, j, off, clo, chi):
                    d = {}
                    ncj = chi - clo
                    first, last = (idx == 0), (idx == nit - 1)

                    def s_fn():
                        ps, R_ps = PS[nxt([0, 1, 2, 3], "aps")]
                        d["ps"] = (ps, R_ps)
                        if kind == "ctx":
                            S.op("pe", lambda e: e.matmul(ps[:, :ncj], lhsT=KCT[:, m, j * 128:(j + 1) * 128],
                                                          rhs=QT[:, h % 2, m, clo:chi], start=True, stop=True), [R_KCT, R_QT], [R_ps])
                        else:
                            S.op("pe", lambda e: e.matmul(ps[:, :ncj], lhsT=KT[:, m, j * 128:(j + 1) * 128],
                                                          rhs=QT[:, h % 2, m, clo:chi], start=True, stop=True), [R_KT, R_QT], [R_ps])

                    def pv_fn():
                        ps, R_ps = d["ps"]
                        eb, R_eb = nxt(EB, "eb")
                        if kind == "ctx":
                            S.op("act", lambda e: e.activation(out=eb[:, :ncj], in_=ps[:, :ncj], func=AF.Exp), [R_ps], [R_eb])
                            vsrc, R_vsrc = VC, R_VC
                        else:
                            tb, R_tb = hs["tb"]
                            sbf, R_sbf = nxt(SBF, "sbf")
                            S.op("dve", lambda e: e.tensor_tensor(out=sbf[:, :ncj], in0=ps[:, :ncj], in1=tb[:, off:off + ncj], op=ALU.add),
                                 [R_ps, R_tb], [R_sbf])
                            S.op("act", lambda e: e.activation(out=eb[:, :ncj], in_=sbf[:, :ncj], func=AF.Exp), [R_sbf], [R_eb])
                            vsrc, R_vsrc = VT, R_VT
                        S.op("pe", lambda e: e.matmul(po[:, clo:chi], lhsT=vsrc[:, j, m * 128:(m + 1) * 128], rhs=eb[:, :ncj],
                                                      start=first, stop=last), [R_vsrc, R_eb], [R_po], sig=True)
                        S.op("pe", lambda e: e.matmul(pd[:, clo:chi], lhsT=ONES[:, :], rhs=eb[:, :ncj],
                                                      start=first, stop=last), [R_ONES, R_eb], [R_pd], sig=True)
                    items.append((s_fn, pv_fn, (lambda: tab_loaders[h + 1]()) if (first and h + 1 < 16) else None, end_fn if last else None))

                idx = 0
                for cc in range(2):
                    mk_item(idx, "ctx", cc, 0, 0, nq)
                    idx += 1
                off = 0
                for j in range(len(chunks)):
                    lo, hi = colr[j]
                    mk_item(idx, "lat", j, off, lo * 64, hi * 64)
                    off += (hi - lo) * 64
                    idx += 1

            for h in range(16):
                make_head(h)
            tab_loaders[0]()
            LA = 3
            deferred = []
            for i in range(min(LA, len(items))):
                items[i][0]()
            for i in range(len(items)):
                if i + LA < len(items):
                    items[i + LA][0]()
                if items[i][2]:
                    items[i][2]()
                items[i][1]()
                if items[i][3]:
                    deferred.append((i + 2, items[i][3]))
                while deferred and deferred[0][0] <= i:
                    deferred.pop(0)[1]()
            while deferred:
                deferred.pop(0)[1]()
            for m in range(NCH):
                ps, R_ps = PS[m % 2]
                for k in range(8):
                    S.op("pe", lambda e, ps=ps, m=m, k=k: e.matmul(ps[:, :nq], lhsT=WO[:, m, k, :], rhs=OT[:, k, 0:nq],
                                                                  start=(k == 0), stop=(k == 7)), [R_WO, R_OT[k]], [R_ps], sig=True)
                post_chunk_evac(m, ps, R_ps, nq, None)
            post_finish(nq, der(1, 0, 2), 0, xsA, R_xsA, t0, xt=xt)

        for (q0, nr, chunks, typ) in AGROUPS:
            do_group(q0, nr, chunks, typ)
        S.barrier()
        ar.off = m0

    if stop_after >= 4:
        attn_phase()
    if stop_after >= 5:
        ffn_phase(1, xsA, R_xsA, outT, R_out, TOP, BOT, TOP, BOT, 0, 0, TOP)

    if dbg:
        srcx = xsA if stop_after in (1, 4) else xsB
        R_srcx = R_xsA if stop_after in (1, 4) else R_xsB
        srcc = csA if stop_after == 1 else csB
        R_srcc = R_csA if stop_after == 1 else R_csB
        R_d1, R_d2 = Res("dbg1"), Res("dbg2")
        S.dma("pool", lambda e: e.dma_start(out=dbgx[:], in_=srcx[:]), [R_srcx], [R_out], R_d1)
        S.dma("pool", lambda e: e.dma_start(out=dbgc[:], in_=srcc[:]), [R_srcc], [R_out], R_d2)
    S.barrier(final=True)
    S.emit(nc, es)
    es.close()
    return nc


def _fm(a2d):
    T = a2d.shape[0]
    return np.ascontiguousarray(a2d.reshape(T, NCH, 128).transpose(2, 1, 0))


def _vec_cols(v):
    return np.ascontiguousarray(v.reshape(-1, 128).T)


def _bias_tables(rpb, quarter):
    r0 = quarter * 32
    tabs = np.full((5, 16, 128, TABMAX), MASK, np.float32)
    qcol = np.arange(64)
    kcol = np.arange(64)
    ws = np.clip(qcol - 8, 0, 48)
    cvalid = (kcol[:, None] >= ws[None, :]) & (kcol[:, None] < ws[None, :] + 16)
    dc = np.clip(kcol[:, None] - qcol[None, :] + 15, 0, 30)
    done = set()
    for (q0, nr, chunks, typ) in AGROUPS:
        if typ in done:
            continue
        done.add(typ)
        off = 0
        for j, c in enumerate(chunks):
            lo, hi = COLR[typ][j]
            for qr in range(lo, hi):
                gq = r0 - 5 + q0 + qr
                for kr2 in range(2):
                    gk = r0 - 5 + 2 * c + kr2
                    ok = (0 <= gq < 128) and (0 <= gk < 128)
                    if ok:
                        wlo, whi = _row_window(gq)
                        ok = wlo <= gk < whi
                    if ok:
                        dr = gk - gq + 7
                        blk = np.where(cvalid[None], rpb[:, dr][:, dc], np.float32(MASK))
                        tabs[typ, :, kr2 * 64:(kr2 + 1) * 64, off + (qr - lo) * 64: off + (qr - lo + 1) * 64] = blk
            off += (hi - lo) * 64
    return tabs


_CACHE = {}


def kernel(x, c, ctx, c_ctx, ada_w, ada_b, mix_pre_g, mix_post_g, ffn_pre_g, ffn_post_g,
           pool_w, pool_scale, na_w_qkv, na_w_o, na_rpb, ffn_w_up, ffn_conv_w, ffn_conv_b, ffn_w_down,
           _stop_after=99, _dbg=False):
    f = lambda a: np.asarray(a, dtype=np.float32)
    x, c, ctx, c_ctx = f(x), f(c), f(ctx), f(c_ctx)
    ada_w, ada_b = f(ada_w), f(ada_b)
    key = (_stop_after, _dbg)
    if key not in _CACHE:
        _CACHE[key] = build_program(_stop_after, _dbg)
    nc = _CACHE[key]

    vecs = np.zeros((128, 2, NV), np.float32)
    for i in range(2):
        vecs[:, i, V_PRE:V_PRE + 8] = _vec_cols(f(mix_pre_g)[i])
        vecs[:, i, V_POST:V_POST + 8] = _vec_cols(f(mix_post_g)[i])
        vecs[:, i, V_FPRE:V_FPRE + 8] = _vec_cols(f(ffn_pre_g)[i])
        vecs[:, i, V_FPOST:V_FPOST + 8] = _vec_cols(f(ffn_post_g)[i])
        vecs[:, i, V_PSC:V_PSC + 8] = _vec_cols(f(pool_scale)[0])
        vecs[:, i, V_ADAB:V_ADAB + 48] = _vec_cols(ada_b[i])
        for tap in range(3):
            vecs[:, i, V_CW + tap * 44:V_CW + (tap + 1) * 44] = _vec_cols(f(ffn_conv_w)[i, tap])
        vecs[:, i, V_CB:V_CB + 44] = _vec_cols(f(ffn_conv_b)[i])
    adaw = np.ascontiguousarray(ada_w.reshape(2, 8, 128, 24, 256).transpose(0, 3, 2, 1, 4))
    wpool = np.ascontiguousarray(f(pool_w)[0].reshape(4, 2, 128, 256).transpose(2, 0, 1, 3))
    wup = np.ascontiguousarray(f(ffn_w_up).reshape(2, 8, 128, 2, NJ, 128).transpose(0, 4, 2, 3, 1, 5))
    wdn = np.ascontiguousarray(f(ffn_w_down).reshape(2, NJ, 128, 8, 128).transpose(0, 3, 2, 1, 4))
    wqkv = f(na_w_qkv)[0].reshape(8, 128, 3, 8, 128)
    wq = np.ascontiguousarray(wqkv[:, :, 0].transpose(1, 2, 0, 3))
    wk = np.ascontiguousarray(wqkv[:, :, 1].transpose(1, 2, 0, 3))
    wv = np.ascontiguousarray(wqkv[:, :, 2].reshape(8, 128, 1024).transpose(1, 0, 2))
    wo = np.ascontiguousarray(f(na_w_o)[0].reshape(8, 128, 8, 128).transpose(1, 2, 0, 3))
    rpb = f(na_rpb)[0]

    def corr_tab(top_on, bot_on):
        t = np.ones((4, 16), np.float32)
        for g in range(4):
            w = 2 << g
            for i in range(8):
                if top_on:
                    t[g, i] = w / (w // 2 + min(i, w // 2))
                if bot_on:
                    ip = 7 - i
                    t[g, 8 + i] = w / (w // 2 + min(ip + 1, w // 2))
        return t

    in_maps = []
    tabs_cache = {}
    for core in range(8):
        b, q = core // 4, core % 4
        r0 = q * 32
        xe = np.zeros((NT, D), np.float32)
        g_lo = (r0 - 5) * 64 - PAD
        lo, hi = max(g_lo, 0), min(g_lo + NT, 8192)
        xe[lo - g_lo:hi - g_lo] = x[b, lo:hi]
        ce = np.zeros((NTC, D), np.float32)
        ce[PAD:PAD + CTX] = ctx[b]
        cnd = np.stack([_vec_cols(c[b]), _vec_cols(c_ctx)], axis=-1)
        fl = np.zeros((128, 4), np.float32)
        fl[:, 0] = 0.0 if q == 0 else 1.0
        fl[:, 1] = 0.0 if q == 3 else 1.0
        cr = np.stack([corr_tab(q == 0, q == 3), corr_tab(True, True)], axis=0)
        cr = np.ascontiguousarray(np.broadcast_to(cr[None], (128, 2, 4, 16)))
        if q not in tabs_cache:
            tabs_cache[q] = _bias_tables(rpb, q)
        in_maps.append(dict(xin=_fm(xe), cin=_fm(ce), cond=np.ascontiguousarray(cnd), vecs=vecs, flg=fl, corr=cr,
                            adaw=adaw, wpool=wpool, wup=wup, wdn=wdn, wq=wq, wk=wk, wv=wv, wo=wo,
                            tabL=np.ascontiguousarray(tabs_cache[q][1:4]),
                            tabS=np.ascontiguousarray(tabs_cache[q][[0, 4]][..., :TABSMALL])))
    res = run_bass_kernel_spmd(nc, in_maps, core_ids=list(range(8)))
    if _dbg:
        return res.results
    out = np.zeros((2, 8192, D), np.float32)
    for core in range(8):
        b, q = core // 4, core % 4
        o = res.results[core]["outT"]
        out[b, q * 2048:(q + 1) * 2048] = o.transpose(2, 1, 0).reshape(2048, D)
    return out
```

```python
import numpy as np
from contextlib import ExitStack
import concourse.bass as bass
import concourse.mybir as mybir
from concourse.bass_utils import run_bass_kernel_spmd

F32 = mybir.dt.float32
BF16 = mybir.dt.bfloat16
AF = mybir.ActivationFunctionType
ALU = mybir.AluOpType

D = 1024
NCH = 8
GRID_W = 64
PAD = 16
ROWS_EXT = 42
NT = PAD + ROWS_EXT * 64 + PAD
TOP = PAD + 5 * 64
BOT = TOP + 2048
CTX = 256
NTC = PAD + CTX + PAD
TOPC = PAD
BOTC = PAD + CTX
DFF = 2816
NJ = 22
EPS = 1e-6
MASK = -30000.0
NV = 8 * 5 + 48 + 3 * 44 + 44
V_PRE, V_POST, V_FPRE, V_FPOST, V_PSC, V_ADAB, V_CW, V_CB = 0, 8, 16, 24, 32, 40, 88, 220

AGROUPS = [
    (4, 1, [0, 1, 2, 3], 0),
    (5, 8, list(range(0, 8)), 1),
    (13, 8, list(range(4, 12)), 2),
    (21, 8, list(range(8, 16)), 2),
    (29, 8, list(range(12, 20)), 3),
    (37, 1, [16, 17, 18, 19, 20], 4),
]


def _row_window(gq):
    s0 = min(max(gq - 4, 0), 120)
    return s0, s0 + 8


def _group_colranges():
    out = {}
    for (q0, nr, chunks, typ) in AGROUPS:
        if typ in out:
            continue
        rngs = []
        for c in chunks:
            rows = set()
            for r0 in (0, 32, 96):
                for qr in range(nr):
                    gq = r0 - 5 + q0 + qr
                    if gq < 0 or gq > 127:
                        continue
                    lo, hi = _row_window(gq)
                    for kr2 in range(2):
                        gk = r0 - 5 + 2 * c + kr2
                        if lo <= gk < hi:
                            rows.add(qr)
            if not rows:
                rows = {0}
            rngs.append((min(rows), max(rows) + 1))
        out[typ] = rngs
    return out


COLR = _group_colranges()
TABW = {t: sum((hi - lo) * 64 for lo, hi in COLR[t]) for t in COLR}
TABMAX = max(TABW.values())
TABSMALL = max(TABW[0], TABW[4])


class Res:
    __slots__ = ("name", "w", "r", "cnt", "sem", "nobar")

    def __init__(self, name):
        self.name = name
        self.w = {}
        self.r = {}
        self.cnt = 0
        self.sem = None
        self.nobar = False


class Sched:
    ENG = ("pe", "act", "dve", "pool", "sp")

    def __init__(self):
        self.streams = {e: [] for e in self.ENG}
        self.cnt = {e: 0 for e in self.ENG}
        self.known = {e: {} for e in self.ENG}
        self.slots = []
        self.sw = {}

    def _need(self, eng, toks, waits, raw):
        for key, val in toks.items():
            if isinstance(key, str):
                if key == eng:
                    if eng == "pe" or not raw:
                        continue
            else:
                val = key.cnt
            if self.known[eng].get(key, 0) >= val:
                continue
            if waits.get(key, 0) < val:
                waits[key] = val

    def _deps(self, eng, reads, writes):
        waits = {}
        for r in reads:
            self._need(eng, r.w, waits, True)
        for w in writes:
            self._need(eng, w.w, waits, False)
            self._need(eng, w.r, waits, False)
        for k, v in waits.items():
            self.known[eng][k] = v
        return list(waits.items())

    def _commit(self, key, val, reads, writes):
        for r in reads:
            if r.r.get(key, 0) < val:
                r.r[key] = val
        for w in writes:
            w.w = {key: val}
            w.r = {}

    def op(self, eng, fn, reads=(), writes=(), sig=True):
        waits = self._deps(eng, reads, writes)
        if sig:
            self.cnt[eng] += 1
            val = self.cnt[eng]
        else:
            val = self.cnt[eng] + 1
        self.streams[eng].append((waits, fn, ("eng", eng) if sig else None))
        self._commit(eng, val, reads, writes)

    def dma(self, q, fn, reads, writes, slot):
        if q == "pool":
            key = id(slot)
            if key not in self.sw:
                r = Res(slot.name + "_sw")
                r.nobar = slot.nobar
                self.sw[key] = r
            slot = self.sw[key]
        waits = self._deps(q, reads, writes)
        if slot.cnt == 0:
            self.slots.append(slot)
        slot.cnt += 16
        self.streams[q].append((waits, fn, ("dma", slot)))
        self._commit(slot, slot.cnt, reads, writes)

    def barrier(self, final=False):
        for e in self.ENG:
            waits = {}
            for o in self.ENG[:4]:
                if o != e and self.known[e].get(o, 0) < self.cnt[o]:
                    waits[o] = self.cnt[o]
            for s in self.slots:
                if s.nobar and not final:
                    continue
                if self.known[e].get(s, 0) < s.cnt:
                    waits[s] = s.cnt
            for k, v in waits.items():
                self.known[e][k] = v
            if waits:
                self.streams[e].append((list(waits.items()), None, None))

    def emit(self, nc, es):
        sems = {}
        for e in self.ENG[:4]:
            sems[e] = es.enter_context(nc.semaphore("sem_" + e))
        for s in self.slots:
            s.sem = es.enter_context(nc.semaphore("sd_" + s.name))
        block = es.enter_context(nc.Block())

        def run(engname):
            def body(eng):
                for waits, fn, inc in self.streams[engname]:
                    for key, val in waits:
                        eng.wait_ge(sems[key] if isinstance(key, str) else key.sem, val)
                    if fn is None:
                        continue
                    inst = fn(eng)
                    if inc is not None:
                        if inc[0] == "eng":
                            inst.then_inc(sems[inc[1]], 1)
                        else:
                            inst.then_inc(inc[1].sem, 16)
            return body

        block.tensor(run("pe"))
        block.scalar(run("act"))
        block.vector(run("dve"))
        block.gpsimd(run("pool"))
        block.sync(run("sp"))


class Arena:
    def __init__(self, nc, base, top):
        self.nc = nc
        self.off = base
        self.top = top
        self.n = 0

    def alloc(self, name, shape, dtype):
        per = 1
        for s in shape[1:]:
            per *= s
        nbytes = per * (2 if dtype == BF16 else 4)
        off = (self.off + 63) // 64 * 64
        assert off + nbytes <= self.top, f"SBUF overflow at {name}: {off + nbytes} > {self.top}"
        self.off = off + nbytes
        self.hw = max(getattr(self, 'hw', 0), self.off)
        self.hwlog = getattr(self, 'hwlog', {})
        self.hwlog[name] = self.off
        self.n += 1
        return self.nc.alloc_sbuf_tensor_at(f"{name}_{self.n}", list(shape), dtype, offset=off)


def build_program(stop_after=99, dbg=False):
    nc = bass.Bass("TRN2", target_bir_lowering=False)
    S = Sched()

    def din(name, shape, dt=F32):
        return nc.dram_tensor(name, list(shape), dt, kind="ExternalInput")

    xin = din("xin", [128, NCH, NT])
    cin = din("cin", [128, NCH, NTC])
    cond = din("cond", [128, NCH, 2])
    vecs = din("vecs", [128, 2, NV])
    flg = din("flg", [128, 4])
    corr = din("corr", [128, 2, 4, 16])
    adaw = din("adaw", [2, 24, 128, 8, 256])
    wpool = din("wpool", [128, 4, 2, 256])
    wup = din("wup", [2, NJ, 128, 2, 8, 128])
    wdn = din("wdn", [2, 8, 128, NJ, 128])
    wq = din("wq", [128, 8, 8, 128])
    wk = din("wk", [128, 8, 8, 128])
    wv = din("wv", [128, 8, 1024])
    wo = din("wo", [128, 8, 8, 128])
    tabL = din("tabL", [3, 16, 128, TABMAX])
    tabS = din("tabS", [2, 16, 128, TABSMALL])
    outT = nc.dram_tensor("outT", [128, NCH, 2048], F32, kind="ExternalOutput")

    xsA = nc.dram_tensor("xsA", [128, NCH, NT], F32)
    xsB = nc.dram_tensor("xsB", [128, NCH, NT], F32)
    csA = nc.dram_tensor("csA", [128, NCH, NTC], F32)
    csB = nc.dram_tensor("csB", [128, NCH, NTC], F32)
    wupb = nc.dram_tensor("wupb", [2, NJ, 128, 2, 8, 128], BF16)
    wdnb = nc.dram_tensor("wdnb", [2, 8, 128, NJ, 128], BF16)
    kts = nc.dram_tensor("kts", [128, NCH, 21 * 128], BF16)
    vts = nc.dram_tensor("vts", [128, 21, 1024], BF16)
    if dbg:
        dbgx = nc.dram_tensor("dbgx", [128, NCH, NT], F32, kind="ExternalOutput")
        dbgc = nc.dram_tensor("dbgc", [128, NCH, NTC], F32, kind="ExternalOutput")

    R_xin, R_cin, R_xsA, R_xsB, R_csA, R_csB = (Res(n) for n in ("xin", "cin", "xsA", "xsB", "csA", "csB"))
    R_wupb, R_wdnb, R_kts, R_vts, R_out, R_const = (Res(n) for n in ("wupb", "wdnb", "kts", "vts", "out", "const"))

    ar = Arena(nc, 16512, 229344)
    RESC = {}
    es = ExitStack()

    def sb(name, shape, dt=F32, nres=1):
        t = ar.alloc(name, shape, dt)
        rs = [RESC.setdefault(f"{name}_{i}", Res(f"{name}_{i}")) for i in range(nres)]
        return t, (rs[0] if nres == 1 else rs)

    PS = []
    for i in range(8):
        PS.append((es.enter_context(nc.psum_tensor(f"ps{i}", [128, 512], F32)), Res(f"ps{i}")))

    VEC, R_VEC = sb("VEC", [128, 2, NV])
    SC, R_SC = sb("SC", [128, NCH, 2])
    SG0, R_SG0 = sb("SG0", [128, NCH, 2])
    MODS, R_MODS = sb("MODS", [128, 2, 48, 2])
    DER, R_DER = sb("DER", [128, 2, 2, 6, 8])
    FLG, R_FLG = sb("FLG", [128, 4])
    CORR, R_CORR = sb("CORR", [128, 2, 4, 16])
    ONES, R_ONES = sb("ONES", [128, 128], BF16)
    WP, R_WP = sb("WP", [128, 4, 2, 256], BF16)
    XT, R_XT = sb("XT", [128, NCH, 512])
    YS, R_YS = sb("YS", [128, NCH, 512], F32, nres=NCH)
    DB, R_DB = sb("DB", [128, NCH, 512], BF16, nres=NCH)
    RS = [sb(f"RS{i}", [128, 512]) for i in range(2)]
    SQ = [sb(f"SQ{i}", [128, 512], BF16) for i in range(3)]
    TMP = [sb(f"TMP{i}", [128, 512]) for i in range(3)]
    base_mark = ar.off

    rot = {}

    def nxt(lst, key):
        i = rot.get(key, 0)
        rot[key] = i + 1
        return lst[i % len(lst)]

    def der(layer, cnd, which):
        return lambda c: DER[:, layer, cnd, which, c:c + 1]

    S.dma("sp", lambda e: e.dma_start(out=VEC[:], in_=vecs[:]), [R_const], [R_VEC], R_VEC)
    S.dma("sp", lambda e: e.dma_start(out=SC[:], in_=cond[:]), [R_const], [R_SC], R_SC)
    S.dma("sp", lambda e: e.dma_start(out=FLG[:], in_=flg[:]), [R_const], [R_FLG], R_FLG)
    S.dma("sp", lambda e: e.dma_start(out=CORR[:], in_=corr[:]), [R_const], [R_CORR], R_CORR)
    S.dma("pool", lambda e: e.dma_start(out=WP[:], in_=wpool[:], max_dma_last_dim=4096), [R_const], [R_WP], R_WP)
    S.op("dve", lambda e: e.memset(ONES[:], 1.0), [], [R_ONES])
    LANES = [Res(f"lane{i}") for i in range(8)]
    for ln in LANES:
        ln.nobar = True
    R_wupc = [[Res(f"wupc{i}_{j}") for j in range(NJ)] for i in range(2)]
    R_wdnc = [[Res(f"wdnc{i}_{m}") for m in range(8)] for i in range(2)]
    kk = 0
    for i in range(2):
        for j in range(NJ):
            ln = LANES[kk % 8]
            S.dma("pool", lambda e, i=i, j=j: e.dma_start(out=wupb[i, j], in_=wup[i, j], max_dma_last_dim=4096), [R_const], [R_wupc[i][j], ln], ln)
            kk += 1
        for m in range(8):
            ln = LANES[kk % 8]
            S.dma("pool", lambda e, i=i, m=m: e.dma_start(out=wdnb[i, m], in_=wdn[i, m], max_dma_last_dim=4096), [R_const], [R_wdnc[i][m], ln], ln)
            kk += 1
    S.op("act", lambda e: e.activation(out=SG0[:], in_=SC[:], func=AF.Sigmoid), [R_SC], [R_SG0])
    S.op("dve", lambda e: e.tensor_tensor(out=SC[:], in0=SC[:], in1=SG0[:], op=ALU.mult), [R_SC, R_SG0], [R_SC])

    EPSB, R_EPSB = sb("EPSB", [128, 1])
    S.op("dve", lambda e: e.memset(EPSB[:], EPS), [], [R_EPSB])
    m1 = ar.off
    WA = [sb(f"WA{i}", [128, 8, 256]) for i in range(4)]
    PMB = {0: 0, 1: 3}
    wa_of = {}

    def mods_dma(layer, g):
        wa, R_wa = nxt(WA, "wa")
        wa_of[(layer, g)] = (wa, R_wa)
        S.dma("act" if layer == 1 else "sp", lambda e: e.dma_start(out=wa[:], in_=adaw[layer, g]), [R_const], [R_wa], R_wa)

    def mods_mm(layer, g):
        pm, R_pm = PS[PMB[layer]]
        wa, R_wa = wa_of[(layer, g)]
        for mm in range(2):
            m = g * 2 + mm
            for k in range(8):
                S.op("pe", lambda e, m=m, mm=mm, k=k: e.matmul(
                    pm[:, 2 * m:2 * m + 2], lhsT=wa[:, k, mm * 128:(mm + 1) * 128], rhs=SC[:, k, :],
                    start=(k == 0), stop=(k == 7)), [R_wa, R_SC], [R_pm], sig=True)

    def mods_finish(layer):
        pm, R_pm = PS[PMB[layer]]
        for cnd in range(2):
            S.op("dve", lambda e, pm=pm, layer=layer, cnd=cnd: e.tensor_tensor(
                out=MODS[:, layer, :, cnd], in0=pm[:, cnd:96:2], in1=VEC[:, layer, V_ADAB:V_ADAB + 48], op=ALU.add),
                [R_pm, R_VEC], [R_MODS])
            def mod(k0, layer=layer, cnd=cnd):
                return MODS[:, layer, k0:k0 + 8, cnd]
            for which, (sc0, gv) in ((0, (8, V_PRE)), (3, (32, V_FPRE))):
                S.op("dve", lambda e, which=which, sc0=sc0, gv=gv, layer=layer, cnd=cnd, mod=mod: e.scalar_tensor_tensor(
                    out=DER[:, layer, cnd, which, :], in0=mod(sc0), scalar=1.0, in1=VEC[:, layer, gv:gv + 8],
                    op0=ALU.add, op1=ALU.mult), [R_MODS, R_VEC], [R_DER])
            for which, sh0 in ((1, 0), (4, 24)):
                S.op("dve", lambda e, which=which, sh0=sh0, layer=layer, cnd=cnd, mod=mod: e.tensor_copy(
                    out=DER[:, layer, cnd, which, :], in_=mod(sh0)), [R_MODS], [R_DER])
            for which, (g0, gv) in ((2, (16, V_POST)), (5, (40, V_FPOST))):
                S.op("dve", lambda e, which=which, g0=g0, gv=gv, layer=layer, cnd=cnd, mod=mod: e.tensor_tensor(
                    out=DER[:, layer, cnd, which, :], in0=mod(g0), in1=VEC[:, layer, gv:gv + 8], op=ALU.mult),
                    [R_MODS, R_VEC], [R_DER])

    for g in range(3):
        mods_dma(0, g)
    for g in range(24):
        if g + 3 < 24:
            mods_dma(0, g + 3)
        mods_mm(0, g)
    mods_finish(0)
    mods_l1_dma = list(range(24))
    mods_l1_mm = []
    S.barrier()

    def rstd_from(ps_ss, R_ss, W):
        rs, R_rs = nxt(RS, "rs")
        S.op("act", lambda e: e.activation(out=rs[:, :W], in_=ps_ss[:, :W], func=AF.Sqrt, bias=EPSB[:, 0:1], scale=1.0 / D),
             [R_ss, R_EPSB], [R_rs])
        S.op("dve", lambda e: e.reciprocal(out=ps_ss[:, :W], in_=rs[:, :W]), [R_rs], [R_ss])
        return ps_ss, R_ss

    def sumsq(src_fn, reads, W, bank):
        ps, R_ps = PS[bank]
        for c in range(NCH):
            sq, R_sq = nxt(SQ, "sq")
            S.op("act", lambda e, c=c, sq=sq: e.activation(out=sq[:, :W], in_=src_fn(c), func=AF.Square), reads(c), [R_sq])
            S.op("pe", lambda e, c=c, sq=sq: e.matmul(ps[:, :W], lhsT=ONES[:], rhs=sq[:, :W], start=(c == 0), stop=(c == 7)),
                 [R_sq, R_ONES], [R_ps], sig=True)
        return ps, R_ps

    def load_x(src, R_src, a, W, xt=None):
        X_, R_X = xt or (XT, R_XT)
        S.dma("sp", lambda e: e.dma_start(out=X_[:, :, 0:W], in_=src[:, :, a:a + W]), [R_src], [R_X], R_X)

    def prenorm(W, A, SH, out_fn, out_res, xt=None):
        X_, R_X = xt or (XT, R_XT)
        ps, R_ps = sumsq(lambda c: X_[:, c, 0:W], lambda c: [R_X], W, 7)
        rs, R_rs = rstd_from(ps, R_ps, W)
        for c in range(NCH):
            tmp, R_tmp = nxt(TMP, "tmp")
            S.op("dve", lambda e, c=c, tmp=tmp: e.scalar_tensor_tensor(
                out=tmp[:, :W], in0=X_[:, c, 0:W], scalar=A(c), in1=rs[:, :W], op0=ALU.mult, op1=ALU.mult),
                [R_X, R_rs, R_DER], [R_tmp])
            S.op("act", lambda e, c=c, tmp=tmp: e.activation(out=out_fn(c), in_=tmp[:, :W], func=AF.Identity, bias=SH(c), scale=1.0),
                 [R_tmp, R_DER], [out_res(c)])

    def post_chunk_evac(m, ps, R_ps, n, scale, ys=None):
        Y_, R_Y = ys or (YS, R_YS)
        if scale is None:
            S.op("act", lambda e: e.activation(out=Y_[:, m, :n], in_=ps[:, :n], func=AF.Copy), [R_ps], [R_Y[m]])
        else:
            S.op("act", lambda e: e.activation(out=Y_[:, m, :n], in_=ps[:, :n], func=AF.Identity, scale=scale(m)),
                 [R_ps, R_VEC], [R_Y[m]])

    def post_finish(n, G, x_off, dst, R_dst, dcol, xt=None, ys=None):
        X_, R_X = xt or (XT, R_XT)
        Y_, R_Y = ys or (YS, R_YS)
        ps, R_ps = sumsq(lambda c: Y_[:, c, :n], lambda c: [R_Y[c]], n, 6)
        rs, R_rs = rstd_from(ps, R_ps, n)
        for c in range(NCH):
            tmp, R_tmp = nxt(TMP, "tmp")
            S.op("dve", lambda e, c=c, tmp=tmp: e.scalar_tensor_tensor(
                out=tmp[:, :n], in0=Y_[:, c, :n], scalar=G(c), in1=rs[:, :n], op0=ALU.mult, op1=ALU.mult),
                [R_Y[c], R_rs, R_DER], [R_tmp])
            S.op("pool", lambda e, c=c, tmp=tmp: e.tensor_tensor(
                out=X_[:, c, x_off:x_off + n], in0=X_[:, c, x_off:x_off + n], in1=tmp[:, :n], op=ALU.add),
                [R_X, R_tmp], [R_X])
        S.dma("pool", lambda e: e.dma_start(out=dst[:, :, dcol:dcol + n], in_=X_[:, :, x_off:x_off + n]), [R_X], [R_dst], R_X)

    def flag_cols(buf_fn, R_buf, a, W, lo, hi, fcol):
        l, h = max(lo, a), min(hi, a + W)
        if l >= h:
            return
        S.op("dve", lambda e: e.tensor_scalar(out=buf_fn(l - a, h - a), in0=buf_fn(l - a, h - a),
                                                scalar1=FLG[:, fcol:fcol + 1], scalar2=None, op0=ALU.mult),
             [R_buf, R_FLG], [R_buf])

    def pool_phase(src, R_src, dst, R_dst, ntok, top, bot, cnd, fcol, cidx, with_mods=False):
        m0 = ar.off
        Us = [sb(f"U{i}", [128, NCH, 512], F32, nres=NCH) for i in range(2)]
        XTs = [(XT, R_XT), sb("XTb", [128, NCH, 512])]
        YSs = [(YS, R_YS), sb("YSb", [128, NCH, 512], F32, nres=NCH)]
        DBs = [(DB, R_DB), sb("DBb", [128, NCH, 512], BF16, nres=NCH)]
        T = [sb(f"T{i}", [128, 512]) for i in range(6)]
        lo_out, hi_out = 8, ntok - 8
        ntile = -(-(hi_out - lo_out) // 496)
        tsz = -(-(hi_out - lo_out) // ntile)

        def mods_hook(k):
            if with_mods:
                for _ in range(k):
                    if mods_l1_mm:
                        mods_mm(1, mods_l1_mm.pop(0))

        def do_tile(a, b, par):
            U, R_U = Us[par]
            xt = XTs[par]
            ys = YSs[par]
            DBp, R_DBp = DBs[par]
            n = b - a
            W = n + 16
            a0 = a - 8
            load_x(src, R_src, a0, W, xt=xt)
            prenorm(W, der(0, cnd, 0), der(0, cnd, 1), lambda c: U[:, c, 0:W], lambda c: R_U[c], xt=xt)
            mods_hook(2)
            for c in range(NCH):
                def ub(l, h, c=c):
                    return U[:, c, l:h]
                flag_cols(ub, R_U[c], a0, W, top - 8, top, fcol)
                flag_cols(ub, R_U[c], a0, W, bot, bot + 8, fcol + 1)
            for c in range(NCH):
                g = c // 2
                w = 2 << g
                t, R_t = nxt(T, "T")
                S.op("pool", lambda e, t=t, c=c: e.tensor_tensor(out=t[:, 1:W], in0=U[:, c, 1:W], in1=U[:, c, 0:W - 1], op=ALU.add),
                     [R_U[c]], [R_t])
                cur, R_cur = t, R_t
                vlo, vhi = 1, W
                sh = 1
                for lvl in range(g):
                    t2, R_t2 = nxt(T, "T")
                    nlo, nhi = vlo + sh, vhi - sh
                    S.op("pool", lambda e, t2=t2, cur=cur, nlo=nlo, nhi=nhi, sh=sh: e.tensor_tensor(
                        out=t2[:, nlo:nhi], in0=cur[:, nlo + sh:nhi + sh], in1=cur[:, nlo - sh:nhi - sh], op=ALU.add),
                        [R_cur], [R_t2])
                    cur, R_cur = t2, R_t2
                    vlo, vhi = nlo, nhi
                    sh *= 2
                assert vlo <= 8 and vhi >= W - 8, (vlo, vhi, W)
                for (l0, h0, off) in ((top, top + 8, 0), (bot - 8, bot, 8)):
                    l, h = max(l0, a), min(h0, b)
                    if l < h:
                        S.op("dve", lambda e, cur=cur, l=l, h=h, l0=l0, off=off, g=g: e.tensor_tensor(
                            out=cur[:, l - a0:h - a0], in0=cur[:, l - a0:h - a0],
                            in1=CORR[:, cidx, g, off + l - l0:off + h - l0], op=ALU.mult), [R_cur, R_CORR], [R_cur])
                S.op("dve", lambda e, cur=cur, c=c, w=w: e.scalar_tensor_tensor(
                    out=DBp[:, c, 0:n], in0=cur[:, 8:8 + n], scalar=1.0 / w, in1=U[:, c, 8:8 + n],
                    op0=ALU.mult, op1=ALU.subtract), [R_cur, R_U[c]], [R_DBp[c]])
            for m in range(NCH):
                g, ml = m // 2, m % 2
                ps, R_ps = PS[m % 2]
                for kc in range(2):
                    S.op("pe", lambda e, ps=ps, g=g, ml=ml, kc=kc: e.matmul(
                        ps[:, :n], lhsT=WP[:, g, kc, ml * 128:(ml + 1) * 128], rhs=DBp[:, 2 * g + kc, 0:n],
                        start=(kc == 0), stop=(kc == 1)), [R_WP, R_DBp[2 * g + kc]], [R_ps], sig=True)
                post_chunk_evac(m, ps, R_ps, n, lambda m: VEC[:, 0, V_PSC + m:V_PSC + m + 1], ys=ys)
            post_finish(n, der(0, cnd, 2), 8, dst, R_dst, a, xt=xt, ys=ys)
            mods_hook(2)

        a = lo_out
        ti = 0
        while a < hi_out:
            b = min(a + tsz, hi_out)
            do_tile(a, b, ti % 2)
            if with_mods:
                while mods_l1_mm:
                    mods_mm(1, mods_l1_mm.pop(0))
                for _ in range(4):
                    if mods_l1_dma:
                        g = mods_l1_dma.pop(0)
                        mods_dma(1, g)
                        mods_l1_mm.append(g)
            a = b
            ti += 1
        if with_mods:
            while mods_l1_mm or mods_l1_dma:
                while mods_l1_mm:
                    mods_mm(1, mods_l1_mm.pop(0))
                for _ in range(4):
                    if mods_l1_dma:
                        g = mods_l1_dma.pop(0)
                        mods_dma(1, g)
                        mods_l1_mm.append(g)
            mods_finish(1)
        S.barrier()
        ar.off = m0

    def ffn_phase(layer, src, R_src, dst, R_dst, lo_out, hi_out, top, bot, cnd, fcol, dst_shift):
        m0 = ar.off
        NST = 2
        UBs = [sb(f"UB{i}", [128, NCH, NST * 512], BF16, nres=NST) for i in range(2)]
        G, R_G = sb("G", [128, NJ, NST * 448], BF16, nres=NST)
        WU = [sb(f"WU{i}", [128, 2, 8, 128], BF16) for i in range(2)]
        WD = [sb(f"WD{i}", [128, NJ, 128], BF16) for i in range(2)]
        FT = [sb(f"FT{i}", [128, 512]) for i in range(12)]
        XTn = sb("XTn", [128, NCH, 512])
        tot = hi_out - lo_out
        ntile = -(-tot // 448)
        tsz = -(-tot // ntile)
        tiles = []
        a = lo_out
        while a < hi_out:
            b = min(a + tsz, hi_out)
            tiles.append((a, b))
            a = b
        sts = [tiles[i:i + NST] for i in range(0, len(tiles), NST)]

        def norm_tile(si, ti):
            a, b = sts[si][ti]
            UB, R_UB = UBs[si % 2]
            n = b - a
            W = n + 2
            load_x(src, R_src, a - 1, W, xt=XTn)
            prenorm(W, der(layer, cnd, 3), der(layer, cnd, 4),
                    lambda c: UB[:, c, ti * 512:ti * 512 + W], lambda c: R_UB[ti], xt=XTn)

            def ubf(l, h):
                return UB[:, :, ti * 512 + l:ti * 512 + h]
            flag_cols(ubf, R_UB[ti], a - 1, W, top - 1, top, fcol)
            flag_cols(ubf, R_UB[ti], a - 1, W, bot, bot + 1, fcol + 1)

        pend = []

        def stage3(ag, R_ag, av, R_av, n, j, ti):
            p, R_p = nxt(FT, "ft")
            S.op("act", lambda e: e.activation(out=p[:, :n], in_=ag[:, :n], func=AF.Gelu_apprx_tanh), [R_ag], [R_p])
            S.op("pool", lambda e: e.tensor_tensor(out=G[:, j, ti * 448:ti * 448 + n], in0=p[:, :n], in1=av[:, :n], op=ALU.mult),
                 [R_p, R_av], [R_G[ti]])

        def stage12(si, j, wu, R_wu, ti):
            a, b = sts[si][ti]
            UB, R_UB = UBs[si % 2]
            n = b - a
            W = n + 2
            hp = []
            for part in range(2):
                ps, R_ps = PS[nxt([0, 1, 2, 3, 4, 5], "ffnps")]
                for k in range(8):
                    S.op("pe", lambda e, ps=ps, part=part, k=k: e.matmul(
                        ps[:, :W], lhsT=wu[:, part, k, :], rhs=UB[:, k, ti * 512:ti * 512 + W],
                        start=(k == 0), stop=(k == 7)), [R_wu, R_UB[ti]], [R_ps], sig=True)
                hp.append((ps, R_ps))
            conv = []
            for part in range(2):
                ps, R_ps = hp[part]
                ch = part * NJ + j
                acc, R_acc = nxt(FT, "ft")
                S.op("act", lambda e, ps=ps, acc=acc, ch=ch: e.activation(
                    out=acc[:, :n], in_=ps[:, 1:n + 1], func=AF.Identity,
                    bias=VEC[:, layer, V_CB + ch:V_CB + ch + 1], scale=VEC[:, layer, V_CW + 44 + ch:V_CW + 44 + ch + 1]),
                    [R_ps, R_VEC], [R_acc])
                for tap, off in ((0, 0), (2, 2)):
                    S.op("dve", lambda e, ps=ps, acc=acc, ch=ch, tap=tap, off=off: e.scalar_tensor_tensor(
                        out=acc[:, :n], in0=ps[:, off:off + n], scalar=VEC[:, layer, V_CW + tap * 44 + ch:V_CW + tap * 44 + ch + 1],
                        in1=acc[:, :n], op0=ALU.mult, op1=ALU.add), [R_ps, R_acc, R_VEC], [R_acc])
                conv.append((acc, R_acc))
            (ag, R_ag), (av, R_av) = conv
            pend.append((ag, R_ag, av, R_av, n, j, ti))
            if len(pend) > 1:
                stage3(*pend.pop(0))

        def down_tile(si, ti):
            a, b = sts[si][ti]
            n = b - a
            for m in range(NCH):
                wd, R_wd = nxt(WD, "wd")
                S.dma("sp", lambda e, wd=wd, m=m: e.dma_start(out=wd[:], in_=wdnb[layer, m]), [R_wdnc[layer][m]], [R_wd], R_wd)
                ps, R_ps = PS[m % 4]
                for j in range(NJ):
                    S.op("pe", lambda e, ps=ps, wd=wd, j=j: e.matmul(
                        ps[:, :n], lhsT=wd[:, j, :], rhs=G[:, j, ti * 448:ti * 448 + n],
                        start=(j == 0), stop=(j == NJ - 1)), [R_wd, R_G[ti]], [R_ps], sig=True)
                post_chunk_evac(m, ps, R_ps, n, None)
            load_x(src, R_src, a, n)
            if si + 1 < len(sts) and ti < len(sts[si + 1]):
                norm_tile(si + 1, ti)
            post_finish(n, der(layer, cnd, 5), 0, dst, R_dst, a - dst_shift)

        for ti in range(len(sts[0])):
            norm_tile(0, ti)
        for si, st in enumerate(sts):
            for j in range(NJ):
                wu, R_wu = nxt(WU, "wu")
                S.dma("sp", lambda e, wu=wu, j=j: e.dma_start(out=wu[:], in_=wupb[layer, j]), [R_wupc[layer][j]], [R_wu], R_wu)
                for ti in range(len(st)):
                    stage12(si, j, wu, R_wu, ti)
            while pend:
                stage3(*pend.pop(0))
            nxt_n = len(sts[si + 1]) if si + 1 < len(sts) else 0
            for ti in range(max(len(st), nxt_n)):
                if ti < len(st):
                    down_tile(si, ti)
                elif ti < nxt_n:
                    norm_tile(si + 1, ti)
        S.barrier()
        ar.off = m0

    if stop_after >= 1:
        pool_phase(xin, R_xin, xsA, R_xsA, NT, TOP, BOT, 0, 0, 0, with_mods=True)
        pool_phase(cin, R_cin, csA, R_csA, NTC, TOPC, BOTC, 1, 2, 1)
    ar.off = m1
    if stop_after >= 2:
        ffn_phase(0, xsA, R_xsA, xsB, R_xsB, PAD, NT - PAD, TOP, BOT, 0, 0, 0)
        ffn_phase(0, csA, R_csA, csB, R_csB, TOPC, BOTC, TOPC, BOTC, 1, 2, 0)

    def kv_phase():
        m0 = ar.off
        WK, R_WK = sb("WK", [128, 8, 8, 128], BF16)
        WV, R_WV = sb("WV", [128, 8, 1024], BF16)
        KOs = [sb(f"KO{i}", [128, NCH, 512], BF16) for i in range(2)]
        VOs = [sb(f"VO{i}", [128, 4, 1024], BF16) for i in range(2)]
        XTs = [(XT, R_XT), sb("XTk", [128, NCH, 512])]
        DBs = [(DB, R_DB), sb("DBk", [128, NCH, 512], BF16, nres=NCH)]
        S.dma("pool", lambda e: e.dma_start(out=WK[:], in_=wk[:], max_dma_last_dim=4096), [R_const], [R_WK], R_WK)
        S.dma("pool", lambda e: e.dma_start(out=WV[:], in_=wv[:], max_dma_last_dim=4096), [R_const], [R_WV], R_WV)

        def kv_tile(src, R_src, a, n, cnd, par):
            xt = XTs[par]
            DBp, R_DBp = DBs[par]
            KO, R_KO = KOs[par]
            VO, R_VO = VOs[par]
            load_x(src, R_src, a, n, xt=xt)
            prenorm(n, der(1, cnd, 0), der(1, cnd, 1), lambda c: DBp[:, c, 0:n], lambda c: R_DBp[c], xt=xt)
            for m in range(NCH):
                ps, R_ps = PS[m % 2]
                for k in range(8):
                    S.op("pe", lambda e, ps=ps, m=m, k=k: e.matmul(ps[:, :n], lhsT=WK[:, m, k, :], rhs=DBp[:, k, 0:n],
                                                                  start=(k == 0), stop=(k == 7)),
                         [R_WK, R_DBp[k]], [R_ps], sig=True)
                S.op("act", lambda e, ps=ps, m=m: e.activation(out=KO[:, m, :n], in_=ps[:, :n], func=AF.Copy), [R_ps], [R_KO])
            for sub in range(n // 128):
                for half in range(2):
                    ps, R_ps = PS[2 + nxt([0, 1, 2, 3], "kvps")]
                    for k in range(8):
                        S.op("pe", lambda e, ps=ps, k=k, sub=sub, half=half: e.matmul(
                            ps[:, :], lhsT=DBp[:, k, sub * 128:(sub + 1) * 128], rhs=WV[:, k, half * 512:(half + 1) * 512],
                            start=(k == 0), stop=(k == 7)), [R_WV, R_DBp[k]], [R_ps], sig=True)
                    if half == 0:
                        S.op("dve", lambda e, ps=ps, sub=sub, half=half: e.tensor_copy(out=VO[:, sub, half * 512:(half + 1) * 512], in_=ps[:, :]),
                             [R_ps], [R_VO])
                    else:
                        S.op("act", lambda e, ps=ps, sub=sub, half=half: e.activation(out=VO[:, sub, half * 512:(half + 1) * 512], in_=ps[:, :], func=AF.Copy),
                             [R_ps], [R_VO])
            if cnd == 0:
                tk = a - PAD
                S.dma("pool", lambda e: e.dma_start(out=kts[:, :, tk:tk + n], in_=KO[:, :, 0:n]), [R_KO], [R_kts], R_KO)
                S.dma("pool", lambda e: e.dma_start(out=vts[:, tk // 128:tk // 128 + n // 128, :], in_=VO[:, 0:n // 128, :]),
                      [R_VO], [R_vts], R_VO)
            else:
                S.op("act", lambda e: e.activation(out=KCT[:, :, 0:n], in_=KO[:, :, 0:n], func=AF.Copy), [R_KO], [R_KCT])
                S.op("dve", lambda e: e.tensor_copy(out=VC[:, 0:n // 128, :], in_=VO[:, 0:n // 128, :]), [R_VO], [R_VC])

        ti = 0
        for (src, R_src, t0, tend, cnd) in ((xsB, R_xsB, PAD, NT - PAD, 0), (csB, R_csB, TOPC, BOTC, 1)):
            a = t0
            while a < tend:
                n = min(512, tend - a)
                kv_tile(src, R_src, a, n, cnd, ti % 2)
                ti += 1
                a += n
        S.barrier()
        ar.off = m0

    if stop_after >= 3:
        KCT, R_KCT = sb("KCT", [128, NCH, 256], BF16)
        VC, R_VC = sb("VC", [128, 2, 1024], BF16)
        kv_phase()

    def attn_phase():
        m0 = ar.off
        WQ, R_WQ = sb("WQ", [128, 8, 8, 128], BF16)
        WO, R_WO = sb("WO", [128, 8, 8, 128], BF16)
        KT, R_KT = sb("KT", [128, NCH, 1024], BF16)
        VT, R_VT = sb("VT", [128, 8, 1024], BF16)
        QT, R_QT = sb("QT", [128, 2, NCH, 512], BF16)
        OT, R_OT = sb("OT", [128, NCH, 512], BF16, nres=NCH)
        TB = [sb(f"TB{i}", [128, TABMAX]) for i in range(2)]
        SBF = [sb(f"SBF{i}", [128, 512]) for i in range(3)]
        EB = [sb(f"EB{i}", [128, 512], BF16) for i in range(4)]
        RD, R_RD = sb("RD", [128, 512])
        S.dma("pool", lambda e: e.dma_start(out=WQ[:], in_=wq[:], max_dma_last_dim=4096), [R_const], [R_WQ], R_WQ)
        S.op("pool", lambda e: e.memset(QT[:], 0.0), [], [R_QT])
        S.dma("pool", lambda e: e.dma_start(out=WO[:], in_=wo[:], max_dma_last_dim=4096), [R_const], [R_WO], R_WO)
        XTq = [(XT, R_XT), sb("XTq", [128, NCH, 512])]
        gcount = [0]

        def do_group(q0, nr, chunks, typ):
            nq = nr * 64
            t0 = PAD + q0 * 64
            c0 = chunks[0]
            nck = len(chunks)
            xt = XTq[gcount[0] % 2]
            gcount[0] += 1
            load_x(xsB, R_xsB, t0, nq, xt=xt)
            S.dma("sp", lambda e, c0=c0, nck=nck: e.dma_start(out=KT[:, :, 0:nck * 128], in_=kts[:, :, c0 * 128:(c0 + nck) * 128]),
                  [R_kts], [R_KT], R_KT)
            S.dma("sp", lambda e, c0=c0, nck=nck: e.dma_start(out=VT[:, 0:nck, :], in_=vts[:, c0:c0 + nck, :]), [R_vts], [R_VT], R_VT)
            prenorm(nq, der(1, 0, 0), der(1, 0, 1), lambda c: DB[:, c, 0:nq], lambda c: R_DB[c], xt=xt)
            for m in range(NCH):
                ps, R_ps = PS[m % 2]
                for k in range(8):
                    S.op("pe", lambda e, ps=ps, m=m, k=k: e.matmul(ps[:, :nq], lhsT=WQ[:, m, k, :], rhs=DB[:, k, 0:nq],
                                                                  start=(k == 0), stop=(k == 7)), [R_WQ, R_DB[k]], [R_ps], sig=True)
                S.op("act", lambda e, ps=ps, m=m: e.activation(out=QT[0:64, 0, m, :nq], in_=ps[0:64, :nq], func=AF.Copy, scale=0.125), [R_ps], [R_QT])
                S.op("act", lambda e, ps=ps, m=m: e.activation(out=QT[64:128, 1, m, :nq], in_=ps[64:128, :nq], func=AF.Copy, scale=0.125), [R_ps], [R_QT])
            colr = COLR[typ]
            items = []
            tab_loaders = {}

            def make_head(h):
                m, pb = h // 2, (h % 2) * 64
                po, R_po = PS[4 + (h % 2)]
                pd, R_pd = PS[6 + (h % 2)]
                hs = {}
                nit = 2 + len(chunks)

                def load_tab():
                    if "tb" in hs:
                        return
                    tb, R_tb = nxt(TB, "tb")
                    hs["tb"] = (tb, R_tb)
                    S.dma("sp", lambda e: e.dma_start(out=tb[:, 0:TABW[typ]], in_=(tabL[typ - 1, h, :, 0:TABW[typ]] if typ in (1, 2, 3)
                                                                                  else tabS[typ // 4, h, :, 0:TABW[typ]])),
                          [R_const], [R_tb], R_tb)

                tab_loaders[h] = load_tab

                def end_fn():
                    S.op("act", lambda e: e.activation(out=RD[pb:pb + 64, :nq], in_=pd[pb:pb + 64, :nq], func=AF.Ln), [R_pd], [R_RD])
                    S.op("act", lambda e: e.activation(out=RD[pb:pb + 64, :nq], in_=RD[pb:pb + 64, :nq], func=AF.Exp, scale=-1.0), [R_RD], [R_RD])
                    S.op("dve", lambda e: e.tensor_tensor(out=OT[pb:pb + 64, m, :nq], in0=po[pb:pb + 64, :nq],
                                                          in1=RD[pb:pb + 64, :nq], op=ALU.mult), [R_po, R_RD], [R_OT[m]])

                def mk_item(idx, kind, j, off, clo, chi):
                    d = {}
                    ncj = chi - clo
                    first, last = (idx == 0), (idx == nit - 1)

                    def s_fn():
                        ps, R_ps = PS[nxt([0, 1, 2, 3], "aps")]
                        d["ps"] = (ps, R_ps)
                        if kind == "ctx":
                            S.op("pe", lambda e: e.matmul(ps[:, :ncj], lhsT=KCT[:, m, j * 128:(j + 1) * 128],
                                                          rhs=QT[:, h % 2, m, clo:chi], start=True, stop=True), [R_KCT, R_QT], [R_ps])
                        else:
                            S.op("pe", lambda e: e.matmul(ps[:, :ncj], lhsT=KT[:, m, j * 128:(j + 1) * 128],
                                                          rhs=QT[:, h % 2, m, clo:chi], start=True, stop=True), [R_KT, R_QT], [R_ps])

                    def pv_fn():
                        ps, R_ps = d["ps"]
                        eb, R_eb = nxt(EB, "eb")
                        if kind == "ctx":
                            S.op("act", lambda e: e.activation(out=eb[:, :ncj], in_=ps[:, :ncj], func=AF.Exp), [R_ps], [R_eb])
                            vsrc, R_vsrc = VC, R_VC
                        else:
                            tb, R_tb = hs["tb"]
                            sbf, R_sbf = nxt(SBF, "sbf")
                            S.op("dve", lambda e: e.tensor_tensor(out=sbf[:, :ncj], in0=ps[:, :ncj], in1=tb[:, off:off + ncj], op=ALU.add),
                                 [R_ps, R_tb], [R_sbf])
                            S.op("act", lambda e: e.activation(out=eb[:, :ncj], in_=sbf[:, :ncj], func=AF.Exp), [R_sbf], [R_eb])
                            vsrc, R_vsrc = VT, R_VT
                        S.op("pe", lambda e: e.matmul(po[:, clo:chi], lhsT=vsrc[:, j, m * 128:(m + 1) * 128], rhs=eb[:, :ncj],
                                                      start=first, stop=last), [R_vsrc, R_eb], [R_po], sig=True)
                        S.op("pe", lambda e: e.matmul(pd[:, clo:chi], lhsT=ONES[:, :], rhs=eb[:, :ncj],
                                                      start=first, stop=last), [R_ONES, R_eb], [R_pd], sig=True)
                    items.append((s_fn, pv_fn, (lambda: tab_loaders[h + 1]()) if (first and h + 1 < 16) else None, end_fn if last else None))

                idx = 0
                for cc in range(2):
                    mk_item(idx, "ctx", cc, 0, 0, nq)
                    idx += 1
                off = 0
                for j in range(len(chunks)):
                    lo, hi = colr[j]
                    mk_item(idx, "lat", j, off, lo * 64, hi * 64)
                    off += (hi - lo) * 64
                    idx += 1

            for h in range(16):
                make_head(h)
            tab_loaders[0]()
            LA = 3
            deferred = []
            for i in range(min(LA, len(items))):
                items[i][0]()
            for i in range(len(items)):
                if i + LA < len(items):
                    items[i + LA][0]()
                if items[i][2]:
                    items[i][2]()
                items[i][1]()
                if items[i][3]:
                    deferred.append((i + 3, items[i][3]))
                while deferred and deferred[0][0] <= i:
                    deferred.pop(0)[1]()
            while deferred:
                deferred.pop(0)[1]()
            for m in range(NCH):
                ps, R_ps = PS[m % 2]
                for k in range(8):
                    S.op("pe", lambda e, ps=ps, m=m, k=k: e.matmul(ps[:, :nq], lhsT=WO[:, m, k, :], rhs=OT[:, k, 0:nq],
                                                                  start=(k == 0), stop=(k == 7)), [R_WO, R_OT[k]], [R_ps], sig=True)
                post_chunk_evac(m, ps, R_ps, nq, None)
            post_finish(nq, der(1, 0, 2), 0, xsA, R_xsA, t0, xt=xt)

        for (q0, nr, chunks, typ) in AGROUPS:
            do_group(q0, nr, chunks, typ)
        S.barrier()
        ar.off = m0

    if stop_after >= 4:
        attn_phase()
    if stop_after >= 5:
        ffn_phase(1, xsA, R_xsA, outT, R_out, TOP, BOT, TOP, BOT, 0, 0, TOP)

    if dbg:
        srcx = xsA if stop_after in (1, 4) else xsB
        R_srcx = R_xsA if stop_after in (1, 4) else R_xsB
        srcc = csA if stop_after == 1 else csB
        R_srcc = R_csA if stop_after == 1 else R_csB
        R_d1, R_d2 = Res("dbg1"), Res("dbg2")
        S.dma("pool", lambda e: e.dma_start(out=dbgx[:], in_=srcx[:]), [R_srcx], [R_out], R_d1)
        S.dma("pool", lambda e: e.dma_start(out=dbgc[:], in_=srcc[:]), [R_srcc], [R_out], R_d2)
    S.barrier(final=True)
    S.emit(nc, es)
    es.close()
    return nc


def _fm(a2d):
    T = a2d.shape[0]
    return np.ascontiguousarray(a2d.reshape(T, NCH, 128).transpose(2, 1, 0))


def _vec_cols(v):
    return np.ascontiguousarray(v.reshape(-1, 128).T)


def _bias_tables(rpb, quarter):
    r0 = quarter * 32
    tabs = np.full((5, 16, 128, TABMAX), MASK, np.float32)
    qcol = np.arange(64)
    kcol = np.arange(64)
    ws = np.clip(qcol - 8, 0, 48)
    cvalid = (kcol[:, None] >= ws[None, :]) & (kcol[:, None] < ws[None, :] + 16)
    dc = np.clip(kcol[:, None] - qcol[None, :] + 15, 0, 30)
    done = set()
    for (q0, nr, chunks, typ) in AGROUPS:
        if typ in done:
            continue
        done.add(typ)
        off = 0
        for j, c in enumerate(chunks):
            lo, hi = COLR[typ][j]
            for qr in range(lo, hi):
                gq = r0 - 5 + q0 + qr
                for kr2 in range(2):
                    gk = r0 - 5 + 2 * c + kr2
                    ok = (0 <= gq < 128) and (0 <= gk < 128)
                    if ok:
                        wlo, whi = _row_window(gq)
                        ok = wlo <= gk < whi
                    if ok:
                        dr = gk - gq + 7
                        blk = np.where(cvalid[None], rpb[:, dr][:, dc], np.float32(MASK))
                        tabs[typ, :, kr2 * 64:(kr2 + 1) * 64, off + (qr - lo) * 64: off + (qr - lo + 1) * 64] = blk
            off += (hi - lo) * 64
    return tabs


_CACHE = {}


def kernel(x, c, ctx, c_ctx, ada_w, ada_b, mix_pre_g, mix_post_g, ffn_pre_g, ffn_post_g,
           pool_w, pool_scale, na_w_qkv, na_w_o, na_rpb, ffn_w_up, ffn_conv_w, ffn_conv_b, ffn_w_down,
           _stop_after=99, _dbg=False):
    f = lambda a: np.asarray(a, dtype=np.float32)
    x, c, ctx, c_ctx = f(x), f(c), f(ctx), f(c_ctx)
    ada_w, ada_b = f(ada_w), f(ada_b)
    key = (_stop_after, _dbg)
    if key not in _CACHE:
        _CACHE[key] = build_program(_stop_after, _dbg)
    nc = _CACHE[key]

    vecs = np.zeros((128, 2, NV), np.float32)
    for i in range(2):
        vecs[:, i, V_PRE:V_PRE + 8] = _vec_cols(f(mix_pre_g)[i])
        vecs[:, i, V_POST:V_POST + 8] = _vec_cols(f(mix_post_g)[i])
        vecs[:, i, V_FPRE:V_FPRE + 8] = _vec_cols(f(ffn_pre_g)[i])
        vecs[:, i, V_FPOST:V_FPOST + 8] = _vec_cols(f(ffn_post_g)[i])
        vecs[:, i, V_PSC:V_PSC + 8] = _vec_cols(f(pool_scale)[0])
        vecs[:, i, V_ADAB:V_ADAB + 48] = _vec_cols(ada_b[i])
        for tap in range(3):
            vecs[:, i, V_CW + tap * 44:V_CW + (tap + 1) * 44] = _vec_cols(f(ffn_conv_w)[i, tap])
        vecs[:, i, V_CB:V_CB + 44] = _vec_cols(f(ffn_conv_b)[i])
    adaw = np.ascontiguousarray(ada_w.reshape(2, 8, 128, 24, 256).transpose(0, 3, 2, 1, 4))
    wpool = np.ascontiguousarray(f(pool_w)[0].reshape(4, 2, 128, 256).transpose(2, 0, 1, 3))
    wup = np.ascontiguousarray(f(ffn_w_up).reshape(2, 8, 128, 2, NJ, 128).transpose(0, 4, 2, 3, 1, 5))
    wdn = np.ascontiguousarray(f(ffn_w_down).reshape(2, NJ, 128, 8, 128).transpose(0, 3, 2, 1, 4))
    wqkv = f(na_w_qkv)[0].reshape(8, 128, 3, 8, 128)
    wq = np.ascontiguousarray(wqkv[:, :, 0].transpose(1, 2, 0, 3))
    wk = np.ascontiguousarray(wqkv[:, :, 1].transpose(1, 2, 0, 3))
    wv = np.ascontiguousarray(wqkv[:, :, 2].reshape(8, 128, 1024).transpose(1, 0, 2))
    wo = np.ascontiguousarray(f(na_w_o)[0].reshape(8, 128, 8, 128).transpose(1, 2, 0, 3))
    rpb = f(na_rpb)[0]

    def corr_tab(top_on, bot_on):
        t = np.ones((4, 16), np.float32)
        for g in range(4):
            w = 2 << g
            for i in range(8):
                if top_on:
                    t[g, i] = w / (w // 2 + min(i, w // 2))
                if bot_on:
                    ip = 7 - i
                    t[g, 8 + i] = w / (w // 2 + min(ip + 1, w // 2))
        return t

    in_maps = []
    tabs_cache = {}
    for core in range(8):
        b, q = core // 4, core % 4
        r0 = q * 32
        xe = np.zeros((NT, D), np.float32)
        g_lo = (r0 - 5) * 64 - PAD
        lo, hi = max(g_lo, 0), min(g_lo + NT, 8192)
        xe[lo - g_lo:hi - g_lo] = x[b, lo:hi]
        ce = np.zeros((NTC, D), np.float32)
        ce[PAD:PAD + CTX] = ctx[b]
        cnd = np.stack([_vec_cols(c[b]), _vec_cols(c_ctx)], axis=-1)
        fl = np.zeros((128, 4), np.float32)
        fl[:, 0] = 0.0 if q == 0 else 1.0
        fl[:, 1] = 0.0 if q == 3 else 1.0
        cr = np.stack([corr_tab(q == 0, q == 3), corr_tab(True, True)], axis=0)
        cr = np.ascontiguousarray(np.broadcast_to(cr[None], (128, 2, 4, 16)))
        if q not in tabs_cache:
            tabs_cache[q] = _bias_tables(rpb, q)
        in_maps.append(dict(xin=_fm(xe), cin=_fm(ce), cond=np.ascontiguousarray(cnd), vecs=vecs, flg=fl, corr=cr,
                            adaw=adaw, wpool=wpool, wup=wup, wdn=wdn, wq=wq, wk=wk, wv=wv, wo=wo,
                            tabL=np.ascontiguousarray(tabs_cache[q][1:4]),
                            tabS=np.ascontiguousarray(tabs_cache[q][[0, 4]][..., :TABSMALL])))
    res = run_bass_kernel_spmd(nc, in_maps, core_ids=list(range(8)))
    if _dbg:
        return res.results
    out = np.zeros((2, 8192, D), np.float32)
    for core in range(8):
        b, q = core // 4, core % 4
        o = res.results[core]["outT"]
        out[b, q * 2048:(q + 1) * 2048] = o.transpose(2, 1, 0).reshape(2048, D)
    return out
```
